# Optimizing a Trainium2 kernel written in Bass

```python
import jax, jax.numpy as jnp
from jax import lax
import numpy as np

D_MODEL = 2048
BATCH = 4
SEQ = 8192
DEPTH = 1

D_MIX = D_MODEL
D_RET = D_MIX // 2
D_SGU = D_MIX - D_RET
RET_HEADS = 8
RET_HEAD_DIM = D_RET // RET_HEADS
SGU_GROUPS = 8
SGU_GROUP_DIM = D_SGU // SGU_GROUPS
CHUNK = 128
D_FF = 5632
ROPE_BASE = 10000.0
EPS = 1e-6
D_PROJ = 4 * D_RET + 2 * D_SGU

kernel_name = "hybrid_retention_sgu_macaron_layer"


def rmsnorm(x, g):
    xf = x.astype(jnp.float32)
    y = xf * lax.rsqrt(jnp.mean(xf * xf, axis=-1, keepdims=True) + EPS)
    return (y * g.astype(jnp.float32)).astype(x.dtype)


def layernorm_nobias(x, g):
    xf = x.astype(jnp.float32)
    mu = jnp.mean(xf, axis=-1, keepdims=True)
    xc = xf - mu
    y = xc * lax.rsqrt(jnp.mean(xc * xc, axis=-1, keepdims=True) + EPS)
    return (y * g.astype(jnp.float32)).astype(x.dtype)


def swiglu(x, w_gate, w_up, w_down):
    return (jax.nn.silu(x @ w_gate) * (x @ w_up)) @ w_down


def rotary(x, positions):
    half = x.shape[-1] // 2
    inv_freq = ROPE_BASE ** (-jnp.arange(half, dtype=jnp.float32) / half)
    ang = positions.astype(jnp.float32)[..., None] * inv_freq
    cos = jnp.cos(ang)[:, :, None, :]
    sin = jnp.sin(ang)[:, :, None, :]
    xf = x.astype(jnp.float32)
    x1, x2 = xf[..., :half], xf[..., half:]
    return jnp.concatenate([x1 * cos - x2 * sin, x1 * sin + x2 * cos], axis=-1)


def retention_chunkwise(q, k, v):
    b, s, h, d = q.shape
    n = s // CHUNK
    log_gamma = jnp.log1p(-jnp.exp2(-5.0 - jnp.arange(h, dtype=jnp.float32)))
    idx = jnp.arange(CHUNK, dtype=jnp.float32)
    rel = idx[:, None] - idx[None, :]
    causal = rel >= 0
    decay_in = jnp.where(causal[None], jnp.exp(log_gamma[:, None, None] * jnp.where(causal, rel, 0.0)[None]), 0.0)
    xi = jnp.exp(log_gamma[None, :] * (idx + 1.0)[:, None])
    zeta = jnp.exp(log_gamma[None, :] * (CHUNK - 1.0 - idx)[:, None])
    chunk_decay = jnp.exp(log_gamma * CHUNK)

    qc = q.reshape(b, n, CHUNK, h, d)
    kc = k.reshape(b, n, CHUNK, h, d)
    vc = v.reshape(b, n, CHUNK, h, d)

    scores = jnp.einsum('bnchd,bnmhd->bnhcm', qc, kc) * decay_in[None, None]
    inner = jnp.einsum('bnhcm,bnmhe->bnche', scores, vc)

    kv = jnp.einsum('bnmhd,bnmhe->bnhde', kc * zeta[None, None, :, :, None], vc)

    def step(state, kv_n):
        return state * chunk_decay[None, :, None, None] + kv_n, state

    _, prev = lax.scan(step, jnp.zeros((b, h, d, d), jnp.float32), jnp.moveaxis(kv, 1, 0))
    prev = jnp.moveaxis(prev, 0, 1)
    cross = jnp.einsum('bnchd,bnhde->bnche', qc * xi[None, None, :, :, None], prev)
    return (inner + cross).reshape(b, s, h, d)


def spatial_gating(u, v, norm_g, w_s, b_s):
    b, s, _ = u.shape
    n = s // CHUNK
    v = layernorm_nobias(v, norm_g)
    vc = v.reshape(b, n, CHUNK, SGU_GROUPS, SGU_GROUP_DIM)
    mask = jnp.tril(jnp.ones((CHUNK, CHUNK), dtype=bool))
    w = jnp.where(mask[None], w_s, jnp.zeros((), w_s.dtype))
    mixed = jnp.einsum('gts,bnsgc->bntgc', w, vc) + b_s.T[None, None, :, :, None]
    return u * mixed.reshape(b, s, D_SGU)


def setup_inputs(seed: int = 0) -> dict:
    key = jax.random.key(seed)
    ks = jax.random.split(key, 20)
    f32 = jnp.float32

    def normal(k, shape, scale):
        return jax.random.normal(k, shape, f32) * scale

    def gain(k, dim):
        return 1.0 + 0.05 * jax.random.normal(k, (DEPTH, dim), f32)

    x = jax.random.normal(ks[0], (BATCH, SEQ, D_MODEL), f32)
    positions = jnp.broadcast_to(jnp.arange(SEQ, dtype=jnp.int32)[None, :], (BATCH, SEQ))
    return {
        "x": x,
        "positions": positions,
        "ffn1_pre_g": gain(ks[1], D_MODEL),
        "ffn1_w_gate": normal(ks[2], (DEPTH, D_MODEL, D_FF), D_MODEL ** -0.5),
        "ffn1_w_up": normal(ks[3], (DEPTH, D_MODEL, D_FF), D_MODEL ** -0.5),
        "ffn1_w_down": normal(ks[4], (DEPTH, D_FF, D_MODEL), D_FF ** -0.5),
        "ffn1_post_g": gain(ks[5], D_MODEL),
        "mix_pre_g": gain(ks[6], D_MODEL),
        "w_in": normal(ks[7], (DEPTH, D_MODEL, D_PROJ), D_MODEL ** -0.5),
        "ret_norm_g": gain(ks[8], D_RET),
        "sgu_norm_g": gain(ks[9], D_SGU),
        "sgu_w_s": normal(ks[10], (DEPTH, SGU_GROUPS, CHUNK, CHUNK), 0.05),
        "sgu_b_s": 1.0 + 0.1 * jax.random.normal(ks[11], (DEPTH, SGU_GROUPS, CHUNK), f32),
        "w_out": normal(ks[12], (DEPTH, D_MIX, D_MODEL), D_MIX ** -0.5),
        "mix_post_g": gain(ks[13], D_MODEL),
        "ffn2_pre_g": gain(ks[14], D_MODEL),
        "ffn2_w_gate": normal(ks[15], (DEPTH, D_MODEL, D_FF), D_MODEL ** -0.5),
        "ffn2_w_up": normal(ks[16], (DEPTH, D_MODEL, D_FF), D_MODEL ** -0.5),
        "ffn2_w_down": normal(ks[17], (DEPTH, D_FF, D_MODEL), D_FF ** -0.5),
        "ffn2_post_g": gain(ks[18], D_MODEL),
    }


def reference(x, positions, ffn1_pre_g, ffn1_w_gate, ffn1_w_up, ffn1_w_down, ffn1_post_g,
              mix_pre_g, w_in, ret_norm_g, sgu_norm_g, sgu_w_s, sgu_b_s, w_out, mix_post_g,
              ffn2_pre_g, ffn2_w_gate, ffn2_w_up, ffn2_w_down, ffn2_post_g):
    b, s, _ = x.shape
    split_at = [D_RET, 2 * D_RET, 3 * D_RET, 4 * D_RET, 4 * D_RET + D_SGU]
    for l in range(DEPTH):
        h = swiglu(rmsnorm(x, ffn1_pre_g[l]), ffn1_w_gate[l], ffn1_w_up[l], ffn1_w_down[l])
        x = x + 0.5 * rmsnorm(h, ffn1_post_g[l])

        h = rmsnorm(x, mix_pre_g[l])
        z = h @ w_in[l]
        q, k, v, g, u, vs = jnp.split(z, split_at, axis=-1)

        q = q.reshape(b, s, RET_HEADS, RET_HEAD_DIM)
        k = k.reshape(b, s, RET_HEADS, RET_HEAD_DIM)
        v = v.reshape(b, s, RET_HEADS, RET_HEAD_DIM).astype(jnp.float32)
        q = rotary(q, positions) * (RET_HEAD_DIM ** -0.5)
        k = rotary(k, positions)
        ret = retention_chunkwise(q, k, v)
        ret = layernorm_nobias(ret, ret_norm_g[l].reshape(RET_HEADS, RET_HEAD_DIM))
        ret = ret.reshape(b, s, D_RET).astype(x.dtype) * jax.nn.silu(g)

        sgu = spatial_gating(jax.nn.gelu(u), jax.nn.gelu(vs), sgu_norm_g[l], sgu_w_s[l], sgu_b_s[l])

        mix = jnp.concatenate([ret, sgu], axis=-1) @ w_out[l]
        x = x + rmsnorm(mix, mix_post_g[l])

        h = swiglu(rmsnorm(x, ffn2_pre_g[l]), ffn2_w_gate[l], ffn2_w_up[l], ffn2_w_down[l])
        x = x + 0.5 * rmsnorm(h, ffn2_post_g[l])
    return x
```

```python
import numpy as np
import ml_dtypes
from contextlib import ExitStack
import concourse.bass as bass
import concourse.mybir as mybir
from concourse.bass_utils import run_bass_kernel_spmd

F32 = mybir.dt.float32
BF16 = mybir.dt.bfloat16
I32 = mybir.dt.int32
AF = mybir.ActivationFunctionType
ALU = mybir.AluOpType
AX = mybir.AxisListType

D = 2048
DFF = 5632
T = 512
NH = 8
NKC = 16
NFC = 44
EPS = 1e-6
NSLOT = 4
TWO_PI = float(2 * np.pi)
SAME_ENGINE_SYNC = True
DBG = None
PIPE_NORMS = False
SCRATCH_KIND = "ExternalOutput"

ENGS = ("pe", "act", "dve", "pool", "sp")


class Op:
    __slots__ = ("id", "eng", "fn", "deps", "dma_sem", "signal", "sig", "waits", "real")


class Sched:
    def __init__(self):
        self.ops = []
        self.lw = {}
        self.rd = {}

    def add(self, eng, fn, reads=(), writes=(), dma_sem=None):
        op = Op()
        op.id = len(self.ops)
        op.eng = eng
        op.fn = fn
        op.dma_sem = dma_sem
        op.signal = False
        op.sig = None
        deps = set()
        for r in reads:
            w = self.lw.get(r)
            if w is not None:
                deps.add(w)
        if eng != "pe":
            for r in reads:
                if isinstance(r, tuple) and r[0] == "ps":
                    key = ("psr", r[1])
                    l = self.lw.get(key)
                    if l is not None and self.ops[l].eng != eng:
                        deps.add(l)
                    self.lw[key] = op.id
        for w in writes:
            l = self.lw.get(w)
            if l is not None:
                deps.add(l)
            for x in self.rd.get(w, ()):
                deps.add(x)
        for w in writes:
            self.lw[w] = op.id
            self.rd[w] = []
        for r in reads:
            self.rd.setdefault(r, []).append(op.id)
        deps.discard(op.id)
        op.deps = deps
        self.ops.append(op)
        return op

    def alias(self, new_names, old_names):
        acc = set()
        for o in old_names:
            l = self.lw.get(o)
            if l is not None:
                acc.add(l)
            acc.update(self.rd.get(o, ()))
        for n in new_names:
            l = self.lw.get(n)
            if l is not None:
                acc.add(l)
            acc.update(self.rd.get(n, ()))
        acc = sorted(acc)
        for n in new_names:
            self.lw[n] = None
            self.rd[n] = list(acc)

    def finalize(self):
        ops = self.ops
        for op in ops:
            real = []
            best = {}
            for d in op.deps:
                p = ops[d]
                if p.dma_sem is None and p.eng == op.eng:
                    if op.eng in ("pe", "sp") or not SAME_ENGINE_SYNC:
                        continue
                if p.dma_sem is None:
                    if p.eng not in best or best[p.eng].id < p.id:
                        best[p.eng] = p
                else:
                    real.append(p)
            for p in best.values():
                real.append(p)
            for p in real:
                p.signal = True
            op.real = real
        cnt = {e: 0 for e in ENGS}
        dcnt = {}
        for op in ops:
            if op.dma_sem is not None:
                dcnt[op.dma_sem] = dcnt.get(op.dma_sem, 0) + 16
                op.sig = (op.dma_sem, dcnt[op.dma_sem])
            elif op.signal:
                cnt[op.eng] += 1
                op.sig = ("e_" + op.eng, cnt[op.eng])
        seen = {e: {} for e in ENGS}
        for op in ops:
            need = {}
            for p in op.real:
                s, v = p.sig
                if v > need.get(s, 0):
                    need[s] = v
            op.waits = []
            for s, v in need.items():
                if seen[op.eng].get(s, 0) >= v:
                    continue
                seen[op.eng][s] = v
                op.waits.append((s, v))
        self.dma_totals = dcnt


def _consts():
    f = np.float32
    h = np.arange(NH, dtype=f)
    log_gamma = np.log1p(-np.exp2(f(-5.0) - h)).astype(f)
    idx = np.arange(128, dtype=f)
    xi = np.exp(log_gamma[None, :] * (idx + f(1.0))[:, None]).astype(f)
    ginv = np.exp(-log_gamma[None, :] * (idx + f(1.0))[:, None]).astype(f)
    cd = np.exp(log_gamma * f(128.0)).astype(f)
    scale = f(128.0 ** -0.5)
    c_xi = np.ascontiguousarray((xi * scale).T).reshape(1, NH * 128).astype(f)
    c_ginv = np.ascontiguousarray(ginv.T).reshape(1, NH * 128).astype(f)
    half = 64
    inv_freq = (f(10000.0) ** (-np.arange(half, dtype=f) / f(half))).astype(f)
    c_invf = np.concatenate([inv_freq, inv_freq]).reshape(128, 1).astype(f)
    m = np.arange(128)
    c_mask = (m[None, :] >= m[:, None]).astype(f)
    return dict(c_xi=c_xi, c_ginv=c_ginv, cd=[float(x) for x in cd], c_invf=c_invf, c_mask=c_mask,
                c_idb=np.eye(128).astype(ml_dtypes.bfloat16), c_idf=np.eye(128, dtype=f))


def build(NP, NM):
    nc = bass.Bass("TRN2", target_bir_lowering=False)
    CST = _consts()
    cdv = CST["cd"]
    NTP = max(NP, 1) * T
    NTM = NM * T

    def din(name, shape, dt=F32):
        return nc.dram_tensor(name, list(shape), dt, kind="ExternalInput").ap()

    XP = din("xp", [NTP, D])
    XM = din("xm", [NTM, D])
    POSP = din("posp", [1, NTP], I32)
    POSM = din("posm", [1, NTM], I32)
    WG = [din("wg1", [D, DFF]), din("wg2", [D, DFF])]
    WU = [din("wu1", [D, DFF]), din("wu2", [D, DFF])]
    WD = [din("wd1", [DFF, D]), din("wd2", [DFF, D])]
    WIN = din("win", [D, 6144])
    WOUT = din("wout", [D, D])
    GV = {k: din("g_" + k, [1, D]) for k in ("f1pre", "f1post", "mpre", "mpost", "f2pre", "f2post")}
    GTP = {k: din("gt_" + k, [128, NKC]) for k in ("f1pre", "mpre", "f2pre")}
    G_RET = din("g_ret", [128, NH])
    G_SGU = din("g_sgu", [1, 1024])
    SGU_W = din("sgu_w", [NH, 128, 128])
    SGU_B = din("sgu_b", [1, 1024])
    C_IDB = din("c_idb", [128, 128], BF16)
    C_IDF = din("c_idf", [128, 128])
    C_MASK = din("c_mask", [128, 128])
    C_XI = din("c_xi", [1, 1024])
    C_GINV = din("c_ginv", [1, 1024])
    C_INVF = din("c_invf", [128, 1])
    SFLAG = din("sflag", [128, 1])
    Y = nc.dram_tensor("y", [NTM, D], F32, kind="ExternalOutput").ap()

    def dint(name, shape):
        return nc.dram_tensor(name, list(shape), BF16, kind=SCRATCH_KIND).ap()

    SG = [dint("sg1", [22, 128, 16, 256]), dint("sg2", [22, 128, 16, 256])]
    SU = [dint("su1", [22, 128, 16, 256]), dint("su2", [22, 128, 16, 256])]
    SD = [dint("sd1", [4, 128, NFC, 512]), dint("sd2", [4, 128, NFC, 512])]
    SIN = dint("sin_", [24, 128, 16, 256])
    SOUT = dint("sout", [8, 128, 16, 256])

    ES = ExitStack()

    def sb(name, shape, dt=F32):
        return ES.enter_context(nc.sbuf_tensor(name, list(shape), dt))

    Xs = sb("Xs", [128, 4, D])
    XNT = sb("XNT", [128, NKC, T], BF16)
    WS = [sb("WS%d" % i, [128, 4096], BF16) for i in range(NSLOT)]
    GS = sb("GS", [128, D])
    XNTOK = [sb("XNTOK0", [128, D], BF16), sb("XNTOK1", [128, D], BF16)]
    EPS4 = sb("EPS4", [128, 1])
    GPRE = {k: sb("GP_" + k, [128, NKC]) for k in ("f1pre", "mpre", "f2pre")}
    IDB = sb("IDB", [128, 128], BF16)
    IDF = sb("IDF", [128, 128])
    MASK = sb("MASK", [128, 128])
    XIT = sb("XIT", [128, NH, 128])
    GINVT = sb("GINVT", [128, NH, 128])
    S = sb("S", [128, NH, 128])
    SBF = sb("SBF", [128, NH, 128], BF16)
    WTS = sb("WTS", [128, NH, 128], BF16)
    BST = sb("BST", [128, NH, 128])
    SGAIN = sb("SGAIN", [128, 1024])
    COS = sb("COS", [128, T])
    SSG = sb("SSG", [128, T])
    INVF = sb("INVF", [128, 1])
    SFL = sb("SFL", [128, 1])
    EPST = sb("EPST", [128, 1])
    RETG = sb("RETG", [128, NH])
    ST = sb("ST", [128, 128])
    REG = sb("REG", [128, 81920 // 4])

    def carve(off_bytes, shape, dt):
        n = int(np.prod(shape[1:]))
        if dt == BF16:
            v = REG[:, off_bytes // 4: off_bytes // 4 + n // 2].bitcast(BF16)
        elif dt == I32:
            v = REG[:, off_bytes // 4: off_bytes // 4 + n].bitcast(I32)
        else:
            v = REG[:, off_bytes // 4: off_bytes // 4 + n]
        if len(shape) == 3:
            v = v.rearrange("p (a b) -> p a b", a=shape[1])
        return v

    ACTT = carve(0, [128, NFC, T], BF16)
    H = carve(45056, [128, 4, D], F32)
    SILU = [carve(77824, [128, T], F32), carve(79872, [128, T], F32)]
    MIXT = carve(0, [128, NKC, T], BF16)
    UT = carve(16384, [128, NH, T], BF16)
    VSG = carve(24576, [128, 4, 1024], F32)
    VLN = carve(40960, [128, 4, 1024], BF16)
    SGT = [carve(49152, [128, T], F32), carve(51200, [128, T], F32)]
    GT = carve(16384, [128, NH, T], BF16)
    QT = carve(24576, [128, NH, T], BF16)
    KT = carve(32768, [128, NH, T], BF16)
    KZ = carve(40960, [128, 4, 1024], BF16)
    V = carve(49152, [128, 4, 1024], BF16)
    RA = [carve(57344, [128, T], F32), carve(61440, [128, T], F32)]
    RB = [carve(59392, [128, T], F32), carve(63488, [128, T], F32)]
    RETSB = carve(65536, [128, NH, 128], F32)
    RETSQ = carve(69632, [128, NH, 128], F32)
    RETN = carve(73728, [128, NH, 128], F32)
    PT = carve(77824, [128, 2, T], BF16)
    TMPG = RETSQ
    POSI = carve(57344, [128, T], I32)
    ANG = carve(59392, [128, T], F32)
    TMPA = carve(61440, [128, T], F32)
    TMPB = carve(63488, [128, T], F32)

    PS = [ES.enter_context(nc.psum_tensor("PS%d" % i, [128, 512], F32)) for i in range(8)]

    sc = Sched()
    A = sc.add

    psrr = [0]

    def ps_next():
        b = psrr[0] % 8
        psrr[0] += 1
        return b

    def dma(eng, out, in_, sem, reads=(), writes=()):
        return A(eng, lambda e, o=out, i=in_: e.dma_start(out=o, in_=i), reads=reads, writes=writes, dma_sem=sem)

    def cast(stage, dst, src, last):
        A("pool", lambda e, o=dst, i=src: e.dma_start(out=o, in_=i), reads=(),
          writes=([("wstage", stage)] if last else ()), dma_sem="cast_" + stage)

    def st_gu(f, b):
        return "g%d_%d" % (f + 1, b // 4) if f == 0 else "g2"

    def st_d(f, cb):
        return "d1_%d" % cb if f == 0 else "d2"

    WIN_CAST_ORDER = [4, 5, 6, 7, 8, 9, 10, 11] + [b for b in range(24) if not (4 <= b <= 11)]

    def st_in(b):
        return "win_0" if 4 <= b <= 11 else "win_1"

    def cast_gu(f):
        for b in range(22):
            for W, Sx, lastw in ((WG[f], SG[f], False), (WU[f], SU[f], True)):
                src = W[:, b * 256:(b + 1) * 256].rearrange("(kc p) c -> p kc c", p=128)
                last = lastw and (b == 21 or (f == 0 and b % 4 == 3))
                cast(st_gu(f, b), Sx[b], src, last)

    def cast_down(f):
        for cb in range(4):
            for fb in range(6):
                f0 = fb * 8
                n = min(8, NFC - f0)
                src = WD[f][f0 * 128:(f0 + n) * 128, cb * 512:(cb + 1) * 512].rearrange("(fc p) c -> p fc c", p=128)
                cast(st_d(f, cb), SD[f][cb][:, f0:f0 + n, :], src, fb == 5 and (f == 0 or cb == 3))

    cast_gu(0)
    cast_down(0)
    for i_, b in enumerate(WIN_CAST_ORDER):
        src = WIN[:, b * 256:(b + 1) * 256].rearrange("(kc p) c -> p kc c", p=128)
        cast(st_in(b), SIN[b], src, i_ == 7 or i_ == 23)
    for b in range(8):
        src = WOUT[:, b * 256:(b + 1) * 256].rearrange("(kc p) c -> p kc c", p=128)
        cast("wout", SOUT[b], src, b == 7)
    cast_gu(1)
    cast_down(1)

    cl = []
    def cload(out, in_, name):
        dma("sp", out, in_, "const", writes=[name])
        cl.append(name)
    cload(IDB[:], C_IDB[:], "IDB")
    cload(IDF[:], C_IDF[:], "IDF")
    cload(MASK[:], C_MASK[:], "MASK")
    cload(XIT[:].rearrange("p a b -> p (a b)"), C_XI.partition_broadcast(128), "XIT")
    cload(GINVT[:].rearrange("p a b -> p (a b)"), C_GINV.partition_broadcast(128), "GINVT")
    cload(BST[:].rearrange("p a b -> p (a b)"), SGU_B.partition_broadcast(128), "BST")
    cload(SGAIN[:], G_SGU.partition_broadcast(128), "SGAIN")
    cload(INVF[:], C_INVF[:], "INVF")
    cload(SFL[:], SFLAG[:], "SFL")
    cload(RETG[:], G_RET[:], "RETG")
    for k_ in ("f1pre", "mpre", "f2pre"):
        cload(GPRE[k_][:], GTP[k_][:], "GPRE")
    last_const = sc.ops[-1].id
    for n in cl:
        sc.lw[n] = last_const

    A("dve", lambda e: e.memset(EPST[:], EPS), writes=["EPST"])
    A("dve", lambda e: e.memset(EPS4[:], 4.0 * EPS), writes=["EPS4"])
    A("dve", lambda e: e.memset(S[:].rearrange("p a b -> p (a b)"), 0.0), writes=["S"])
    A("dve", lambda e: e.memset(SBF[:].rearrange("p a b -> p (a b)"), 0.0), writes=["SBF"])

    for g in range(NH):
        wtmp = TMPA if g % 2 == 0 else TMPB
        nm = ("TMPW", g % 2)
        dma("sp", wtmp[:, 0:128], SGU_W[g], "misc%d" % (g % 2), writes=[nm])
        b = ps_next()
        A("pe", lambda e, b=b, w=wtmp: e.transpose(out=PS[b][:, 0:128], in_=w[:, 0:128], identity=IDF[:]),
          reads=[nm, "IDF"], writes=[("ps", b)])
        A("dve", lambda e, b=b, g=g: e.tensor_tensor(out=WTS[:, g, :], in0=PS[b][:, 0:128], in1=MASK[:], op=ALU.mult),
          reads=[("ps", b), "MASK"], writes=[("WTS", g)])

    wq = []

    def seq_ffn(f):
        for b in range(22):
            wq.append(("g", f, b))
            wq.append(("u", f, b))
        for cb in range(4):
            for fb in range(6):
                wq.append(("d", f, cb, fb))

    for g in range(NP):
        seq_ffn(0)
        for b in (4, 5, 6, 7, 8, 9, 10, 11):
            wq.append(("in", b))
    MIX_ORDER = [20, 21, 22, 23, 16, 17, 18, 19, 12, 13, 14, 15, 8, 9, 10, 11, 4, 5, 6, 7, 0, 1, 2, 3]
    for g in range(NM):
        seq_ffn(0)
        for b in MIX_ORDER:
            wq.append(("in", b))
        for b in range(8):
            wq.append(("out", b))
        seq_ffn(1)
    wstate = {"issued": 0, "taken": 0}

    def w_issue(i):
        key = wq[i]
        slot = i % NSLOT
        if key[0] in ("g", "u"):
            f, b = key[1], key[2]
            src = (SG if key[0] == "g" else SU)[f][b].rearrange("p a b -> p (a b)")
            dst = WS[slot][:, 0:4096]
            stage = st_gu(f, b)
        elif key[0] == "d":
            f, cb, fb = key[1], key[2], key[3]
            f0 = fb * 8
            n = min(8, NFC - f0)
            src = SD[f][cb][:, f0:f0 + n, :].rearrange("p a b -> p (a b)")
            dst = WS[slot][:, 0:n * 512]
            stage = st_d(f, cb)
        elif key[0] == "in":
            src = SIN[key[1]].rearrange("p a b -> p (a b)")
            dst = WS[slot][:, 0:4096]
            stage = st_in(key[1])
        else:
            src = SOUT[key[1]].rearrange("p a b -> p (a b)")
            dst = WS[slot][:, 0:4096]
            stage = "wout"
        dma("sp", dst, src, "ws%d" % slot, reads=[("wstage", stage)], writes=[("ws", slot)])

    def w_take(key, hold=0):
        while wstate["issued"] < min(len(wq), wstate["taken"] - hold + NSLOT):
            w_issue(wstate["issued"])
            wstate["issued"] += 1
        i = wstate["taken"]
        assert wq[i] == key, (wq[i], key)
        wstate["taken"] += 1
        return i % NSLOT

    def rstd_small(dst, src, n, scale, nm):
        A("act", lambda e: e.activation(out=dst, in_=src, func=AF.Sqrt, bias=EPST[:, 0:1], scale=scale),
          reads=[nm, "EPST"], writes=[nm])
        A("dve", lambda e: e.reciprocal(out=dst, in_=dst), reads=[nm], writes=[nm])

    def load_gain(key):
        dma("act", GS[:], GV[key].partition_broadcast(128), "gs", writes=["GS"])

    def S2(tt):
        xb = XNTOK[tt % 2]
        xn = ("XNTOK", tt % 2)
        A("act", lambda e: e.activation(out=xb[:], in_=Xs[:, tt, :], func=AF.Square, accum_out=ST[:, tt:tt + 1]),
          reads=[("X", tt)], writes=[xn, ("s2", tt)])
        A("act", lambda e: e.activation(out=ST[:, 8 + tt:9 + tt], in_=ST[:, tt:tt + 1], func=AF.Sqrt, bias=EPST[:, 0:1], scale=1.0 / D),
          reads=[("s2", tt), "EPST"], writes=[("r2", tt)])
        A("dve", lambda e: e.reciprocal(out=ST[:, 8 + tt:9 + tt], in_=ST[:, 8 + tt:9 + tt]), reads=[("r2", tt)], writes=[("r2", tt)])
        A("act", lambda e: e.activation(out=xb[:], in_=Xs[:, tt, :], func=AF.Copy, scale=ST[:, 8 + tt:9 + tt]),
          reads=[("X", tt), ("r2", tt)], writes=[xn])

    def S3(tt, gkey):
        xb = XNTOK[tt % 2]
        xn = ("XNTOK", tt % 2)
        gp = GPRE[gkey]
        for half in range(2):
            b = ps_next()
            pb = PS[b][:].bitcast(BF16)
            for k8 in range(8):
                kc = half * 8 + k8
                A("pe", lambda e, pb=pb, k8=k8, kc=kc: e.transpose(out=pb[:, k8 * 128:(k8 + 1) * 128],
                                                                  in_=xb[:, kc * 128:(kc + 1) * 128], identity=IDB[:]),
                  reads=[xn, "IDB"], writes=[("ps", b)])
            A("dve", lambda e, pb=pb, half=half: e.tensor_tensor(
                out=XNT[:, half * 8:(half + 1) * 8, tt * 128:(tt + 1) * 128],
                in0=pb.rearrange("p (a b) -> p a b", a=8),
                in1=gp[:, half * 8:(half + 1) * 8].unsqueeze(2).to_broadcast([128, 8, 128]), op=ALU.mult),
              reads=[("ps", b), "GPRE"], writes=[("XNT", tt)])

    XNT_ALL = [("XNT", tt) for tt in range(4)]

    def S1(tt, nparts, mul, to_h):
        A("dve", lambda e: e.reduce_sum(out=ST[:, 48 + tt:49 + tt], in_=ST[:, 16 + tt * 8:16 + tt * 8 + nparts], axis=AX.X),
          reads=[("ssp", tt)], writes=[("r1", tt)])
        bias = EPST if mul == 1.0 else EPS4
        assert mul in (1.0, 0.5)
        A("act", lambda e: e.activation(out=ST[:, 52 + tt:53 + tt], in_=ST[:, 48 + tt:49 + tt], func=AF.Sqrt, bias=bias[:, 0:1],
                                        scale=1.0 / (D * mul * mul)), reads=[("r1", tt), "EPST", "EPS4"], writes=[("r1b", tt)])
        A("dve", lambda e: e.reciprocal(out=ST[:, 52 + tt:53 + tt], in_=ST[:, 52 + tt:53 + tt]), reads=[("r1b", tt)], writes=[("r1b", tt)])
        if to_h:
            A("dve", lambda e: e.scalar_tensor_tensor(out=H[:, tt, :], in0=H[:, tt, :], scalar=ST[:, 52 + tt:53 + tt],
                                                      in1=Xs[:, tt, :], op0=ALU.mult, op1=ALU.add),
              reads=[("H", tt), ("r1b", tt), ("X", tt)], writes=[("H", tt)])
        else:
            A("dve", lambda e: e.scalar_tensor_tensor(out=Xs[:, tt, :], in0=H[:, tt, :], scalar=ST[:, 52 + tt:53 + tt],
                                                      in1=Xs[:, tt, :], op0=ALU.mult, op1=ALU.add),
              reads=[("H", tt), ("r1b", tt), ("X", tt)], writes=[("X", tt)])

    def pipelined(stages):
        n = len(stages)
        if not PIPE_NORMS:
            for tt in range(4):
                for st_ in stages:
                    st_(tt)
            return
        for step in range(4 + n - 1):
            for si in range(n):
                tt = step - si
                if 0 <= tt < 4:
                    stages[si](tt)

    def prenorm_all(gkey):
        pipelined([S2, lambda tt: S3(tt, gkey)])

    def boundary_all(nparts, mul, next_gkey, after_s1=None, to_h=False):
        def s1(tt):
            S1(tt, nparts, mul, to_h)
            if after_s1 is not None:
                after_s1(tt)
        st = [s1]
        if next_gkey is not None:
            st += [S2, lambda tt: S3(tt, next_gkey)]
        pipelined(st)

    FFN_NAMES = [("ACTT", j) for j in range(NFC)] + [("H", tt) for tt in range(4)] + [("SILU", i) for i in range(2)]
    MIX_NAMES = ([("MIXT", c) for c in range(4)] + ["UT", "VSG", "VLN", "SGT0", "SGT1", "GT", "QT", "KZ", "V",
                 "RA0", "RA1", "RB0", "RB1", "RETSB", "RETSQ", "RETN", "PT", "TMPG", "POSI", "ANG", "TMPA", "TMPB",
                 ("TMPW", 0), ("TMPW", 1)] + [("MIXS", g) for g in range(8)] + [("H", tt) for tt in range(4)] + [("KT", h_) for h_ in range(NH)]
                 + [("PT", 0), ("PT", 1)] + [("VSG", t_) for t_ in range(4)])

    def ffn(f, post, next_gkey, after_s1=None, to_h=False):
        sc.alias(FFN_NAMES, MIX_NAMES)
        load_gain(post)
        for b in range(22):
            sg = w_take(("g", f, b))
            su = w_take(("u", f, b), hold=1)
            gv = WS[sg][:, 0:4096].rearrange("p (a b) -> p a b", a=16)
            uv = WS[su][:, 0:4096].rearrange("p (a b) -> p a b", a=16)
            for jj in range(2):
                j = 2 * b + jj
                pg = ps_next()
                pu = ps_next()
                for kc in range(NKC):
                    A("pe", lambda e, pg=pg, gv=gv, kc=kc, jj=jj: e.matmul(PS[pg][:], lhsT=gv[:, kc, jj * 128:(jj + 1) * 128],
                                                                         rhs=XNT[:, kc, :], start=(kc == 0), stop=(kc == NKC - 1)),
                      reads=[("ws", sg)] + XNT_ALL, writes=[("ps", pg)])
                for kc in range(NKC):
                    A("pe", lambda e, pu=pu, uv=uv, kc=kc, jj=jj: e.matmul(PS[pu][:], lhsT=uv[:, kc, jj * 128:(jj + 1) * 128],
                                                                         rhs=XNT[:, kc, :], start=(kc == 0), stop=(kc == NKC - 1)),
                      reads=[("ws", su)] + XNT_ALL, writes=[("ps", pu)])
                si = j % 2
                A("act", lambda e, pg=pg, si=si: e.activation(out=SILU[si][:], in_=PS[pg][:], func=AF.Silu),
                  reads=[("ps", pg)], writes=[("SILU", si)])
                A("dve", lambda e, pu=pu, si=si, j=j: e.tensor_tensor(out=ACTT[:, j, :], in0=PS[pu][:], in1=SILU[si][:], op=ALU.mult),
                  reads=[("ps", pu), ("SILU", si)], writes=[("ACTT", j)])
        for cb in range(4):
            banks = [ps_next() for _ in range(4)]
            for fb in range(6):
                sl = w_take(("d", f, cb, fb))
                f0 = fb * 8
                n = min(8, NFC - f0)
                dv = WS[sl][:, 0:n * 512].rearrange("p (a b) -> p a b", a=n)
                for tt in range(4):
                    for fl in range(n):
                        fc = f0 + fl
                        A("pe", lambda e, bk=banks[tt], dv=dv, fl=fl, fc=fc, tt=tt: e.matmul(
                            PS[bk][:], lhsT=ACTT[:, fc, tt * 128:(tt + 1) * 128], rhs=dv[:, fl, :],
                            start=(fc == 0), stop=(fc == NFC - 1)),
                          reads=[("ws", sl), ("ACTT", fc)], writes=[("ps", banks[tt])])
            for tt in range(4):
                bk = banks[tt]
                A("dve", lambda e, bk=bk, tt=tt, cb=cb: e.tensor_tensor(out=H[:, tt, cb * 512:(cb + 1) * 512], in0=PS[bk][:],
                                                                        in1=GS[:, cb * 512:(cb + 1) * 512], op=ALU.mult),
                  reads=[("ps", bk), "GS"], writes=[("H", tt)])
                A("act", lambda e, bk=bk, tt=tt, cb=cb: e.activation(out=SILU[tt % 2][:], in_=PS[bk][:], func=AF.Square,
                                                                     accum_out=ST[:, 16 + tt * 8 + cb:17 + tt * 8 + cb]),
                  reads=[("ps", bk)], writes=[("SILU", tt % 2), ("ssp", tt)])
        boundary_all(4, 0.5, next_gkey, after_s1, to_h)

    def pos_tables(POS, g):
        dma("act", POSI[:], POS[:, g * T:(g + 1) * T].partition_broadcast(128), "pos", writes=["POSI"])
        A("dve", lambda e: e.tensor_copy(out=ANG[:], in_=POSI[:]), reads=["POSI"], writes=["ANG"])
        A("dve", lambda e: e.tensor_scalar(out=ANG[:], in0=ANG[:], scalar1=INVF[:, 0:1], scalar2=None, op0=ALU.mult),
          reads=["ANG", "INVF"], writes=["ANG"])
        for which in range(2):
            dst = SSG if which == 0 else COS
            shift = 0.0 if which == 0 else float(np.pi / 2)
            A("dve", lambda e, shift=shift: e.tensor_scalar(out=TMPA[:], in0=ANG[:], scalar1=shift, scalar2=None, op0=ALU.add),
              reads=["ANG"], writes=["TMPA"])
            A("dve", lambda e: e.tensor_scalar(out=TMPB[:], in0=TMPA[:], scalar1=float(1.0 / TWO_PI), scalar2=0.5,
                                               op0=ALU.mult, op1=ALU.add), reads=["TMPA"], writes=["TMPB"])
            A("dve", lambda e: e.tensor_copy(out=POSI[:], in_=TMPB[:]), reads=["TMPB"], writes=["POSI"])
            A("dve", lambda e: e.tensor_copy(out=TMPB[:], in_=POSI[:]), reads=["POSI"], writes=["TMPB"])
            A("dve", lambda e: e.scalar_tensor_tensor(out=TMPA[:], in0=TMPB[:], scalar=-TWO_PI, in1=TMPA[:],
                                                      op0=ALU.mult, op1=ALU.add), reads=["TMPB", "TMPA"], writes=["TMPA"])
            A("dve", lambda e: e.tensor_scalar(out=TMPB[:], in0=TMPA[:], scalar1=-float(np.pi), scalar2=TWO_PI,
                                               op0=ALU.is_lt, op1=ALU.mult), reads=["TMPA"], writes=["TMPB"])
            A("dve", lambda e: e.tensor_tensor(out=TMPA[:], in0=TMPA[:], in1=TMPB[:], op=ALU.add),
              reads=["TMPA", "TMPB"], writes=["TMPA"])
            A("act", lambda e, dst=dst: e.activation(out=dst[:], in_=TMPA[:], func=AF.Sin), reads=["TMPA"],
              writes=["SSG" if which == 0 else "COS"])
        A("dve", lambda e: e.tensor_scalar(out=SSG[0:64, :], in0=SSG[0:64, :], scalar1=-1.0, scalar2=None, op0=ALU.mult),
          reads=["SSG"], writes=["SSG"])

    def proj_fm(blk, handler):
        sl = w_take(("in", blk))
        wv = WS[sl][:, 0:4096].rearrange("p (a b) -> p a b", a=16)
        for cc in range(2):
            b = ps_next()
            for kc in range(NKC):
                A("pe", lambda e, b=b, wv=wv, kc=kc, cc=cc: e.matmul(PS[b][:], lhsT=wv[:, kc, cc * 128:(cc + 1) * 128],
                                                                  rhs=XNT[:, kc, :], start=(kc == 0), stop=(kc == NKC - 1)),
                  reads=[("ws", sl)] + XNT_ALL, writes=[("ps", b)])
            handler(b, cc)

    def proj_tm(blk, handler):
        sl = w_take(("in", blk))
        wv = WS[sl][:, 0:4096].rearrange("p (a b) -> p a b", a=16)
        for tt in range(4):
            b = ps_next()
            for kc in range(NKC):
                A("pe", lambda e, b=b, wv=wv, kc=kc, tt=tt: e.matmul(PS[b][:, 0:256], lhsT=XNT[:, kc, tt * 128:(tt + 1) * 128],
                                                                  rhs=wv[:, kc, :], start=(kc == 0), stop=(kc == NKC - 1)),
                  reads=[("ws", sl), ("XNT", tt)], writes=[("ps", b)])
            handler(b, tt)

    def rotary(b, h, dstT, dname, decT):
        i = h % 2
        A("dve", lambda e: e.tensor_tensor(out=RA[i][:], in0=PS[b][:], in1=COS[:], op=ALU.mult),
          reads=[("ps", b), "COS"], writes=["RA%d" % i])
        A("dve", lambda e: e.tensor_tensor(out=RB[i][0:64, :], in0=PS[b][64:128, :], in1=SSG[0:64, :], op=ALU.mult),
          reads=[("ps", b), "SSG"], writes=["RB%d" % i])
        A("dve", lambda e: e.tensor_tensor(out=RB[i][64:128, :], in0=PS[b][0:64, :], in1=SSG[64:128, :], op=ALU.mult),
          reads=[("ps", b), "SSG"], writes=["RB%d" % i])
        A("dve", lambda e: e.tensor_tensor(out=RA[i][:], in0=RA[i][:], in1=RB[i][:], op=ALU.add),
          reads=["RA%d" % i, "RB%d" % i], writes=["RA%d" % i])
        A("dve", lambda e: e.tensor_tensor(out=dstT[:, h, :].rearrange("p (a b) -> p a b", a=4),
                                           in0=RA[i][:].rearrange("p (a b) -> p a b", a=4),
                                           in1=decT[:, h, :].unsqueeze(1).to_broadcast([128, 4, 128]), op=ALU.mult),
          reads=["RA%d" % i, "XIT", "GINVT"], writes=[dname])

    kpend = []

    def k_flush():
        while kpend:
            h = kpend.pop(0)
            b2 = ps_next()
            pb = PS[b2][:].bitcast(BF16)
            for c in range(4):
                A("pe", lambda e, pb=pb, c=c, h=h: e.transpose(out=pb[:, c * 128:(c + 1) * 128], in_=KT[:, h, c * 128:(c + 1) * 128],
                                                            identity=IDB[:]), reads=[("KT", h), "IDB"], writes=[("ps", b2)])
            A("act", lambda e, pb=pb, h=h: e.activation(out=KZ[:, :, h * 128:(h + 1) * 128],
                                                       in_=pb[:, 0:512].rearrange("p (a b) -> p a b", a=4),
                                                       func=AF.Copy, scale=cdv[h]),
              reads=[("ps", b2)], writes=["KZ"])

    def k_block(blk):
        sl = w_take(("in", blk))
        wv = WS[sl][:, 0:4096].rearrange("p (a b) -> p a b", a=16)
        for cc in range(2):
            h = (blk - 4) * 2 + cc
            b = ps_next()
            for kc in range(NKC):
                A("pe", lambda e, b=b, wv=wv, kc=kc, cc=cc: e.matmul(PS[b][:], lhsT=wv[:, kc, cc * 128:(cc + 1) * 128],
                                                                  rhs=XNT[:, kc, :], start=(kc == 0), stop=(kc == NKC - 1)),
                  reads=[("ws", sl)] + XNT_ALL, writes=[("ps", b)])
            k_flush()
            rotary(b, h, KT, ("KT", h), GINVT)
            kpend.append(h)

    def v_block(blk):
        def hv(b, tt):
            c0 = (blk - 8) * 256
            A("act", lambda e, b=b, tt=tt, c0=c0: e.activation(out=V[:, tt, c0:c0 + 256], in_=PS[b][:, 0:256], func=AF.Copy),
              reads=[("ps", b)], writes=["V"])
        proj_tm(blk, hv)

    def kv_update(c):
        for hb in range(2):
            b = ps_next()
            for hh in range(4):
                h = hb * 4 + hh
                A("pe", lambda e, b=b, hh=hh, h=h, c=c: e.matmul(PS[b][:, hh * 128:(hh + 1) * 128], lhsT=KZ[:, c, h * 128:(h + 1) * 128],
                                                              rhs=V[:, c, h * 128:(h + 1) * 128], start=True, stop=True),
                  reads=["KZ", "V"], writes=[("ps", b)])
            for hh in range(4):
                h = hb * 4 + hh
                A("dve", lambda e, b=b, hh=hh, h=h: e.scalar_tensor_tensor(out=S[:, h, :], in0=S[:, h, :], scalar=cdv[h],
                                                                        in1=PS[b][:, hh * 128:(hh + 1) * 128], op0=ALU.mult, op1=ALU.add),
                  reads=["S", ("ps", b)], writes=["S"])
        A("act", lambda e: e.activation(out=SBF[:].rearrange("p a b -> p (a b)"), in_=S[:].rearrange("p a b -> p (a b)"), func=AF.Copy),
          reads=["S"], writes=["SBF"])

    def load_x_tt(XD, g, tt, eng="sp"):
        dma("sp", Xs[:, tt, :], XD[g * T + tt * 128: g * T + (tt + 1) * 128, :], "xi%d" % tt, writes=[("X", tt)])

    def start_group(XD, g, eng="pool"):
        for tt in range(4):
            load_x_tt(XD, g, tt, eng)
        prenorm_all("f1pre")

    def prefix_group(g, nxt):
        ffn(0, "f1post", "mpre")
        sc.alias(MIX_NAMES, FFN_NAMES)
        for tt in range(4):
            load_x_tt(nxt[0], nxt[1], tt)
        pos_tables(POSP, g)
        for blk in (4, 5, 6, 7):
            k_block(blk)
        k_flush()
        for blk in (8, 9, 10, 11):
            v_block(blk)
        prenorm_all("f1pre")
        for c in range(4):
            kv_update(c)

    def store_y_tt(g, tt):
        dma("pool", Y[g * T + tt * 128: g * T + (tt + 1) * 128, :], Xs[:, tt, :], "st%d" % tt, reads=[("X", tt)])

    def store_y(g):
        for tt in range(4):
            store_y_tt(g, tt)

    def main_group(g, nxt):
        ffn(0, "f1post", None if DBG == "ffn1" else "mpre")
        if DBG == "ffn1":
            return store_y(g)
        sc.alias(MIX_NAMES, FFN_NAMES)
        load_gain("mpost")
        pos_tables(POSM, g)
        for blk in (20, 21, 22, 23):
            def hvs(b, tt, blk=blk):
                c0 = (blk - 20) * 256
                A("act", lambda e, b=b, tt=tt, c0=c0: e.activation(out=VSG[:, tt, c0:c0 + 256], in_=PS[b][:, 0:256],
                                                                   func=AF.Gelu_apprx_tanh), reads=[("ps", b)], writes=[("VSG", tt)])
            proj_tm(blk, hvs)
        for tt in range(4):
            A("dve", lambda e, tt=tt: e.reduce_sum(out=ST[:, 56:57], in_=VSG[:, tt, :], axis=AX.X),
              reads=[("VSG", tt)], writes=["lnv"])
            A("act", lambda e, tt=tt: e.activation(out=VLN[:, tt, :], in_=VSG[:, tt, :], func=AF.Square, accum_out=ST[:, 57:58]),
              reads=[("VSG", tt)], writes=["VLN", "lnv"])
            A("dve", lambda e: e.tensor_scalar(out=ST[:, 58:59], in0=ST[:, 56:57], scalar1=1.0 / 1024, scalar2=None, op0=ALU.mult),
              reads=["lnv"], writes=["lnv"])
            A("dve", lambda e: e.tensor_tensor(out=ST[:, 59:60], in0=ST[:, 58:59], in1=ST[:, 58:59], op=ALU.mult),
              reads=["lnv"], writes=["lnv"])
            A("dve", lambda e: e.scalar_tensor_tensor(out=ST[:, 59:60], in0=ST[:, 57:58], scalar=1.0 / 1024, in1=ST[:, 59:60],
                                                      op0=ALU.mult, op1=ALU.subtract), reads=["lnv"], writes=["lnv"])
            rstd_small(ST[:, 60:61], ST[:, 59:60], 1, 1.0, "lnv")
            A("dve", lambda e: e.scalar_tensor_tensor(out=ST[:, 61:62], in0=ST[:, 58:59], scalar=-1.0, in1=ST[:, 60:61],
                                                      op0=ALU.mult, op1=ALU.mult), reads=["lnv"], writes=["lnv"])
            A("dve", lambda e, tt=tt: e.tensor_scalar(out=VSG[:, tt, :], in0=VSG[:, tt, :], scalar1=ST[:, 60:61], scalar2=ST[:, 61:62],
                                                      op0=ALU.mult, op1=ALU.add), reads=[("VSG", tt), "lnv"], writes=[("VSG", tt)])
            A("dve", lambda e, tt=tt: e.tensor_tensor(out=VLN[:, tt, :], in0=VSG[:, tt, :], in1=SGAIN[:], op=ALU.mult),
              reads=[("VSG", tt), "SGAIN"], writes=["VLN"])
        for blk in (16, 17, 18, 19):
            def hu(b, cc, blk=blk):
                gi = (blk - 16) * 2 + cc
                A("act", lambda e, b=b, gi=gi: e.activation(out=UT[:, gi, :], in_=PS[b][:], func=AF.Gelu_apprx_tanh),
                  reads=[("ps", b)], writes=["UT"])
            proj_fm(blk, hu)
        for gi in range(NH):
            b = ps_next()
            for c in range(4):
                A("pe", lambda e, b=b, c=c, gi=gi: e.matmul(PS[b][:, c * 128:(c + 1) * 128], lhsT=VLN[:, c, gi * 128:(gi + 1) * 128],
                                                          rhs=WTS[:, gi, :], start=True, stop=True),
                  reads=["VLN", ("WTS", gi)], writes=[("ps", b)])
            i = gi % 2
            A("dve", lambda e, b=b, gi=gi, i=i: e.tensor_tensor(out=SGT[i][:].rearrange("p (a b) -> p a b", a=4),
                                                              in0=PS[b][:].rearrange("p (a b) -> p a b", a=4),
                                                              in1=BST[:, gi, :].unsqueeze(1).to_broadcast([128, 4, 128]), op=ALU.add),
              reads=[("ps", b), "BST"], writes=["SGT%d" % i])
            A("dve", lambda e, gi=gi, i=i: e.tensor_tensor(out=MIXT[:, 8 + gi, :], in0=SGT[i][:], in1=UT[:, gi, :], op=ALU.mult),
              reads=["SGT%d" % i, "UT"], writes=[("MIXS", gi)])
        sc.alias(["GT", "QT", "KZ", "V"] + [("KT", h_) for h_ in range(NH)], ["UT", "VSG", "VLN", "SGT0", "SGT1"] + [("VSG", t) for t in range(4)])
        for blk in (12, 13, 14, 15):
            def hg(b, cc, blk=blk):
                h = (blk - 12) * 2 + cc
                A("act", lambda e, b=b, h=h: e.activation(out=GT[:, h, :], in_=PS[b][:], func=AF.Silu),
                  reads=[("ps", b)], writes=["GT"])
            proj_fm(blk, hg)
        for blk in (8, 9, 10, 11):
            v_block(blk)
        for blk in (4, 5, 6, 7):
            k_block(blk)
        k_flush()
        for blk in (0, 1, 2, 3):
            def hq(b, cc, blk=blk):
                h = blk * 2 + cc
                rotary(b, h, QT, "QT", XIT)
            proj_fm(blk, hq)
        for c in range(4):
            cs = slice(c * 128, (c + 1) * 128)
            rb = []
            for hb in range(2):
                b = ps_next()
                for hh in range(4):
                    h = hb * 4 + hh
                    A("pe", lambda e, b=b, hh=hh, h=h, cs=cs: e.matmul(PS[b][:, hh * 128:(hh + 1) * 128], lhsT=KT[:, h, cs], rhs=QT[:, h, cs],
                                                                    start=True, stop=True),
                      reads=[("KT", h), "QT"], writes=[("ps", b)])
                A("dve", lambda e, b=b, hb=hb: e.tensor_tensor(out=PT[:, hb, :].rearrange("p (a b) -> p a b", a=4),
                                                             in0=PS[b][:].rearrange("p (a b) -> p a b", a=4),
                                                             in1=MASK[:].unsqueeze(1).to_broadcast([128, 4, 128]), op=ALU.mult),
                  reads=[("ps", b), "MASK"], writes=[("PT", hb)])
            for hb in range(2):
                b = ps_next()
                rb.append(b)
                for hh in range(4):
                    h = hb * 4 + hh
                    A("pe", lambda e, b=b, hh=hh, h=h, hb=hb, c=c: e.matmul(PS[b][:, hh * 128:(hh + 1) * 128], lhsT=PT[:, hb, hh * 128:(hh + 1) * 128],
                                                                         rhs=V[:, c, h * 128:(h + 1) * 128], start=True, stop=False),
                      reads=[("PT", hb), "V"], writes=[("ps", b)])
                    A("pe", lambda e, b=b, hh=hh, h=h, cs=cs: e.matmul(PS[b][:, hh * 128:(hh + 1) * 128], lhsT=QT[:, h, cs],
                                                                    rhs=SBF[:, h, :], start=False, stop=True),
                      reads=["QT", "SBF"], writes=[("ps", b)])
            kv_update(c)
            for hb in range(2):
                b = rb[hb]
                A("act", lambda e, b=b, hb=hb: e.activation(out=RETSB[:, hb * 4:(hb + 1) * 4, :].rearrange("p a b -> p (a b)"),
                                                          in_=PS[b][:], func=AF.Copy), reads=[("ps", b)], writes=["RETSB"])
                A("act", lambda e, b=b, hb=hb: e.activation(out=RETSQ[:, hb * 4:(hb + 1) * 4, :].rearrange("p a b -> p (a b)"),
                                                          in_=PS[b][:], func=AF.Square), reads=[("ps", b)], writes=["RETSQ"])
            A("dve", lambda e: e.reduce_sum(out=ST[:, 64:72], in_=RETSB[:], axis=AX.X), reads=["RETSB"], writes=["lnr"])
            A("dve", lambda e: e.reduce_sum(out=ST[:, 80:88], in_=RETSQ[:], axis=AX.X), reads=["RETSQ"], writes=["lnr"])
            A("dve", lambda e: e.tensor_scalar(out=ST[:, 64:72], in0=ST[:, 64:72], scalar1=1.0 / 128, scalar2=None, op0=ALU.mult),
              reads=["lnr"], writes=["lnr"])
            A("dve", lambda e: e.tensor_tensor(out=ST[:, 72:80], in0=ST[:, 64:72], in1=ST[:, 64:72], op=ALU.mult),
              reads=["lnr"], writes=["lnr"])
            A("dve", lambda e: e.scalar_tensor_tensor(out=ST[:, 80:88], in0=ST[:, 80:88], scalar=1.0 / 128, in1=ST[:, 72:80],
                                                      op0=ALU.mult, op1=ALU.subtract), reads=["lnr"], writes=["lnr"])
            rstd_small(ST[:, 80:88], ST[:, 80:88], 8, 1.0, "lnr")
            A("dve", lambda e: e.tensor_tensor(out=RETSB[:], in0=RETSB[:], in1=ST[:, 64:72].unsqueeze(2).to_broadcast([128, 8, 128]),
                                               op=ALU.subtract), reads=["RETSB", "lnr"], writes=["RETSB"])
            A("dve", lambda e: e.tensor_tensor(out=RETN[:], in0=RETSB[:], in1=ST[:, 80:88].unsqueeze(2).to_broadcast([128, 8, 128]),
                                               op=ALU.mult), reads=["RETSB", "lnr"], writes=["RETN"])
            for hb in range(2):
                b = ps_next()
                for hh in range(4):
                    h = hb * 4 + hh
                    A("pe", lambda e, b=b, hh=hh, h=h: e.transpose(out=PS[b][:, hh * 128:(hh + 1) * 128], in_=RETN[:, h, :], identity=IDF[:]),
                      reads=["RETN", "IDF"], writes=[("ps", b)])
                A("dve", lambda e, b=b, hb=hb: e.tensor_tensor(out=TMPG[:, hb * 4:(hb + 1) * 4, :],
                                                             in0=PS[b][:].rearrange("p (a b) -> p a b", a=4),
                                                             in1=RETG[:, hb * 4:(hb + 1) * 4].unsqueeze(2).to_broadcast([128, 4, 128]),
                                                             op=ALU.mult), reads=[("ps", b), "RETG"], writes=["RETSQ"])
                A("dve", lambda e, hb=hb, cs=cs: e.tensor_tensor(out=MIXT[:, hb * 4:(hb + 1) * 4, cs], in0=TMPG[:, hb * 4:(hb + 1) * 4, :],
                                                               in1=GT[:, hb * 4:(hb + 1) * 4, cs], op=ALU.mult),
                  reads=["RETSQ", "GT"], writes=[("MIXT", c)])
        sc.alias([("H", tt) for tt in range(4)], ["KZ", "V", "RA0", "RA1", "RB0", "RB1", "RETSB", "RETSQ", "RETN", "QT"] + [("KT", h_) for h_ in range(NH)] + [
                                                  "POSI", "ANG", "TMPA", "TMPB"])
        MIX_ALL = [("MIXT", c) for c in range(4)] + [("MIXS", g_) for g_ in range(8)]
        for blk in range(8):
            sl = w_take(("out", blk))
            wv = WS[sl][:, 0:4096].rearrange("p (a b) -> p a b", a=16)
            for tt in range(4):
                b = ps_next()
                for fc in range(NKC):
                    A("pe", lambda e, b=b, wv=wv, fc=fc, tt=tt: e.matmul(PS[b][:, 0:256], lhsT=MIXT[:, fc, tt * 128:(tt + 1) * 128],
                                                                      rhs=wv[:, fc, :], start=(fc == 0), stop=(fc == NKC - 1)),
                      reads=[("ws", sl)] + MIX_ALL, writes=[("ps", b)])
                A("dve", lambda e, b=b, tt=tt, blk=blk: e.tensor_tensor(out=H[:, tt, blk * 256:(blk + 1) * 256], in0=PS[b][:, 0:256],
                                                                        in1=GS[:, blk * 256:(blk + 1) * 256], op=ALU.mult),
                  reads=[("ps", b), "GS"], writes=[("H", tt)])
                A("act", lambda e, b=b, tt=tt, blk=blk: e.activation(out=PT[:].rearrange("p a b -> p (a b)")[:, 0:256], in_=PS[b][:, 0:256],
                                                                     func=AF.Square, accum_out=ST[:, 16 + tt * 8 + blk:17 + tt * 8 + blk]),
                  reads=[("ps", b)], writes=[("PT", 0), ("ssp", tt)])
        boundary_all(8, 1.0, None if DBG == "mix" else "f2pre")
        if DBG == "mix":
            return store_y(g)

        def after(tt):
            dma("pool", Y[g * T + tt * 128: g * T + (tt + 1) * 128, :], H[:, tt, :], "st%d" % tt, reads=[("H", tt)])
            if nxt is not None:
                load_x_tt(nxt[0], nxt[1], tt)
        ffn(1, "f2post", "f1pre" if nxt is not None else None, after_s1=after, to_h=True)

    seq = [("p", g) for g in range(NP)] + [("m", g) for g in range(NM)]
    first = seq[0]
    start_group(XP if first[0] == "p" else XM, first[1], eng="sp")
    for i, (kind, g) in enumerate(seq):
        nx = seq[i + 1] if i + 1 < len(seq) else None
        nxt = None if nx is None else ((XP if nx[0] == "p" else XM), nx[1])
        if kind == "p":
            prefix_group(g, nxt)
            if nx is not None and nx[0] == "m":
                A("dve", lambda e: e.tensor_scalar(out=S[:].rearrange("p a b -> p (a b)"), in0=S[:].rearrange("p a b -> p (a b)"),
                                                   scalar1=SFL[:, 0:1], scalar2=None, op0=ALU.mult), reads=["S", "SFL"], writes=["S"])
                A("act", lambda e: e.activation(out=SBF[:].rearrange("p a b -> p (a b)"), in_=S[:].rearrange("p a b -> p (a b)"),
                                                func=AF.Copy), reads=["S"], writes=["SBF"])
        else:
            main_group(g, nxt)
    assert DBG or wstate["taken"] == len(wq)

    print('sbuf bytes remaining', nc.sbuf_bytes_remaining)
    sc.finalize()
    sem_names = sorted({s for op in sc.ops if op.sig for s in [op.sig[0]]} | {"e_" + e for e in ENGS})
    sems = {n: ES.enter_context(nc.semaphore(n)) for n in sem_names}

    def emit(engname, e):
        for op in sc.ops:
            if op.eng != engname:
                continue
            for s, v in op.waits:
                e.wait_ge(sems[s], v)
            ins = op.fn(e)
            if op.sig is not None:
                ins.then_inc(sems[op.sig[0]], 16 if op.dma_sem is not None else 1)
        if engname == "pool":
            for s, v in sc.dma_totals.items():
                if s.startswith("st"):
                    e.wait_ge(sems[s], v)

    with nc.Block() as block:
        @block.tensor
        def _(e):
            emit("pe", e)

        @block.scalar
        def _(e):
            emit("act", e)

        @block.vector
        def _(e):
            emit("dve", e)

        @block.gpsimd
        def _(e):
            emit("pool", e)

        @block.sync
        def _(e):
            emit("sp", e)
    ES.close()
    return nc, CST


def make_in_maps(inputs, NP, NM, B, CST):
    x = np.asarray(inputs["x"], dtype=np.float32)
    pos = np.asarray(inputs["positions"], dtype=np.int32)
    half = NM * T
    f32 = lambda a: np.ascontiguousarray(np.asarray(a, dtype=np.float32))
    shared = {
        "wg1": f32(inputs["ffn1_w_gate"][0]), "wu1": f32(inputs["ffn1_w_up"][0]), "wd1": f32(inputs["ffn1_w_down"][0]),
        "wg2": f32(inputs["ffn2_w_gate"][0]), "wu2": f32(inputs["ffn2_w_up"][0]), "wd2": f32(inputs["ffn2_w_down"][0]),
        "win": f32(inputs["w_in"][0]), "wout": f32(inputs["w_out"][0]),
        "g_f1pre": f32(inputs["ffn1_pre_g"]).reshape(1, D), "g_f1post": f32(inputs["ffn1_post_g"]).reshape(1, D),
        "g_mpre": f32(inputs["mix_pre_g"]).reshape(1, D), "g_mpost": f32(inputs["mix_post_g"]).reshape(1, D),
        "g_f2pre": f32(inputs["ffn2_pre_g"]).reshape(1, D), "g_f2post": f32(inputs["ffn2_post_g"]).reshape(1, D),
        "gt_f1pre": f32(np.asarray(inputs["ffn1_pre_g"], dtype=np.float32).reshape(NKC, 128).T),
        "gt_mpre": f32(np.asarray(inputs["mix_pre_g"], dtype=np.float32).reshape(NKC, 128).T),
        "gt_f2pre": f32(np.asarray(inputs["ffn2_pre_g"], dtype=np.float32).reshape(NKC, 128).T),
        "g_ret": f32(np.asarray(inputs["ret_norm_g"], dtype=np.float32).reshape(NH, 128).T),
        "g_sgu": f32(inputs["sgu_norm_g"]).reshape(1, 1024),
        "sgu_w": f32(inputs["sgu_w_s"][0]), "sgu_b": f32(inputs["sgu_b_s"][0]).reshape(1, 1024),
        "c_idb": CST["c_idb"], "c_idf": CST["c_idf"], "c_mask": CST["c_mask"], "c_xi": CST["c_xi"],
        "c_ginv": CST["c_ginv"], "c_invf": CST["c_invf"],
    }
    maps = []
    for c in range(2 * B):
        b, hf = c // 2, c % 2
        m = dict(shared)
        m["xm"] = np.ascontiguousarray(x[b, hf * half:(hf + 1) * half])
        m["xp"] = np.ascontiguousarray(x[b, 0:half])
        m["posm"] = np.ascontiguousarray(pos[b, hf * half:(hf + 1) * half]).reshape(1, half)
        m["posp"] = np.ascontiguousarray(pos[b, 0:half]).reshape(1, half)
        m["sflag"] = np.full((128, 1), float(hf), np.float32)
        maps.append(m)
    return maps


def run(inputs, NG):
    B = np.asarray(inputs["x"]).shape[0]
    assert 2 * B == 8
    nc, CST = build(NG, NG)
    maps = make_in_maps(inputs, NG, NG, B, CST)
    res = run_bass_kernel_spmd(nc, maps, core_ids=list(range(8)))
    half = NG * T
    out = np.empty((B, 2 * half, D), np.float32)
    for c in range(8):
        out[c // 2, (c % 2) * half:(c % 2 + 1) * half] = res.results[c]["y"]
    return out


def kernel(**inputs):
    return run(inputs, 8)
```

```python
import numpy as np
import ml_dtypes
from contextlib import ExitStack
import concourse.bass as bass
import concourse.mybir as mybir
from concourse.bass_utils import run_bass_kernel_spmd

F32 = mybir.dt.float32
BF16 = mybir.dt.bfloat16
I32 = mybir.dt.int32
AF = mybir.ActivationFunctionType
ALU = mybir.AluOpType
AX = mybir.AxisListType

D = 2048
DFF = 5632
T = 512
NH = 8
NKC = 16
NFC = 44
EPS = 1e-6
NSLOT = 4
TWO_PI = float(2 * np.pi)
SAME_ENGINE_SYNC = True
DBG = None
PIPE_NORMS = True
SCRATCH_KIND = "ExternalOutput"

ENGS = ("pe", "act", "dve", "pool", "sp")


class Op:
    __slots__ = ("id", "eng", "fn", "deps", "dma_sem", "signal", "sig", "waits", "real")


class Sched:
    def __init__(self):
        self.ops = []
        self.lw = {}
        self.rd = {}

    def add(self, eng, fn, reads=(), writes=(), dma_sem=None):
        op = Op()
        op.id = len(self.ops)
        op.eng = eng
        op.fn = fn
        op.dma_sem = dma_sem
        op.signal = False
        op.sig = None
        deps = set()
        for r in reads:
            w = self.lw.get(r)
            if w is not None:
                deps.add(w)
        if eng != "pe":
            for r in reads:
                if isinstance(r, tuple) and r[0] == "ps":
                    key = ("psr", r[1])
                    l = self.lw.get(key)
                    if l is not None and self.ops[l].eng != eng:
                        deps.add(l)
                    self.lw[key] = op.id
        for w in writes:
            l = self.lw.get(w)
            if l is not None:
                deps.add(l)
            for x in self.rd.get(w, ()):
                deps.add(x)
        for w in writes:
            self.lw[w] = op.id
            self.rd[w] = []
        for r in reads:
            self.rd.setdefault(r, []).append(op.id)
        deps.discard(op.id)
        op.deps = deps
        self.ops.append(op)
        return op

    def alias(self, new_names, old_names):
        acc = set()
        for o in old_names:
            l = self.lw.get(o)
            if l is not None:
                acc.add(l)
            acc.update(self.rd.get(o, ()))
        for n in new_names:
            l = self.lw.get(n)
            if l is not None:
                acc.add(l)
            acc.update(self.rd.get(n, ()))
        acc = sorted(acc)
        for n in new_names:
            self.lw[n] = None
            self.rd[n] = list(acc)

    def finalize(self):
        ops = self.ops
        for op in ops:
            real = []
            best = {}
            for d in op.deps:
                p = ops[d]
                if p.dma_sem is None and p.eng == op.eng:
                    if op.eng in ("pe", "sp") or not SAME_ENGINE_SYNC:
                        continue
                if p.dma_sem is None:
                    if p.eng not in best or best[p.eng].id < p.id:
                        best[p.eng] = p
                else:
                    real.append(p)
            for p in best.values():
                real.append(p)
            for p in real:
                p.signal = True
            op.real = real
        cnt = {e: 0 for e in ENGS}
        dcnt = {}
        for op in ops:
            if op.dma_sem is not None:
                dcnt[op.dma_sem] = dcnt.get(op.dma_sem, 0) + 16
                op.sig = (op.dma_sem, dcnt[op.dma_sem])
            elif op.signal:
                cnt[op.eng] += 1
                op.sig = ("e_" + op.eng, cnt[op.eng])
        seen = {e: {} for e in ENGS}
        for op in ops:
            need = {}
            for p in op.real:
                s, v = p.sig
                if v > need.get(s, 0):
                    need[s] = v
            op.waits = []
            for s, v in need.items():
                if seen[op.eng].get(s, 0) >= v:
                    continue
                seen[op.eng][s] = v
                op.waits.append((s, v))
        self.dma_totals = dcnt


def _consts():
    f = np.float32
    h = np.arange(NH, dtype=f)
    log_gamma = np.log1p(-np.exp2(f(-5.0) - h)).astype(f)
    idx = np.arange(128, dtype=f)
    xi = np.exp(log_gamma[None, :] * (idx + f(1.0))[:, None]).astype(f)
    ginv = np.exp(-log_gamma[None, :] * (idx + f(1.0))[:, None]).astype(f)
    cd = np.exp(log_gamma * f(128.0)).astype(f)
    scale = f(128.0 ** -0.5)
    c_xi = np.ascontiguousarray((xi * scale).T).reshape(1, NH * 128).astype(f)
    c_ginv = np.ascontiguousarray(ginv.T).reshape(1, NH * 128).astype(f)
    half = 64
    inv_freq = (f(10000.0) ** (-np.arange(half, dtype=f) / f(half))).astype(f)
    c_invf = np.concatenate([inv_freq, inv_freq]).reshape(128, 1).astype(f)
    m = np.arange(128)
    c_mask = (m[None, :] >= m[:, None]).astype(f)
    return dict(c_xi=c_xi, c_ginv=c_ginv, cd=[float(x) for x in cd], c_invf=c_invf, c_mask=c_mask,
                c_idb=np.eye(128).astype(ml_dtypes.bfloat16), c_idf=np.eye(128, dtype=f))


def build(NP, NM):
    nc = bass.Bass("TRN2", target_bir_lowering=False)
    CST = _consts()
    cdv = CST["cd"]
    NTP = max(NP, 1) * T
    NTM = NM * T

    def din(name, shape, dt=F32):
        return nc.dram_tensor(name, list(shape), dt, kind="ExternalInput").ap()

    XP = din("xp", [NTP, D])
    XM = din("xm", [NTM, D])
    POSP = din("posp", [1, NTP], I32)
    POSM = din("posm", [1, NTM], I32)
    WG = [din("wg1", [D, DFF]), din("wg2", [D, DFF])]
    WU = [din("wu1", [D, DFF]), din("wu2", [D, DFF])]
    WD = [din("wd1", [DFF, D]), din("wd2", [DFF, D])]
    WIN = din("win", [D, 6144])
    WOUT = din("wout", [D, D])
    GV = {k: din("g_" + k, [1, D]) for k in ("f1pre", "f1post", "mpre", "mpost", "f2pre", "f2post")}
    GTP = {k: din("gt_" + k, [128, NKC]) for k in ("f1pre", "mpre", "f2pre")}
    G_RET = din("g_ret", [128, NH])
    G_SGU = din("g_sgu", [1, 1024])
    SGU_W = din("sgu_w", [NH, 128, 128])
    SGU_B = din("sgu_b", [1, 1024])
    C_IDB = din("c_idb", [128, 128], BF16)
    C_IDF = din("c_idf", [128, 128])
    C_MASK = din("c_mask", [128, 128])
    C_XI = din("c_xi", [1, 1024])
    C_GINV = din("c_ginv", [1, 1024])
    C_INVF = din("c_invf", [128, 1])
    SFLAG = din("sflag", [128, 1])
    Y = nc.dram_tensor("y", [NTM, D], F32, kind="ExternalOutput").ap()

    def dint(name, shape):
        return nc.dram_tensor(name, list(shape), BF16, kind=SCRATCH_KIND).ap()

    SG = [dint("sg1", [22, 128, 16, 256]), dint("sg2", [22, 128, 16, 256])]
    SU = [dint("su1", [22, 128, 16, 256]), dint("su2", [22, 128, 16, 256])]
    SD = [dint("sd1", [4, 128, NFC, 512]), dint("sd2", [4, 128, NFC, 512])]
    SIN = dint("sin_", [24, 128, 16, 256])
    SOUT = dint("sout", [8, 128, 16, 256])

    ES = ExitStack()

    def sb(name, shape, dt=F32):
        return ES.enter_context(nc.sbuf_tensor(name, list(shape), dt))

    Xs = sb("Xs", [128, 4, D])
    XNT = sb("XNT", [128, NKC, T], BF16)
    WS = [sb("WS%d" % i, [128, 4096], BF16) for i in range(NSLOT)]
    GS = sb("GS", [128, D])
    XNTOK = [sb("XNTOK0", [128, D], BF16), sb("XNTOK1", [128, D], BF16)]
    EPS4 = sb("EPS4", [128, 1])
    GPRE = {k: sb("GP_" + k, [128, NKC]) for k in ("f1pre", "mpre", "f2pre")}
    IDB = sb("IDB", [128, 128], BF16)
    IDF = sb("IDF", [128, 128])
    MASK = sb("MASK", [128, 128])
    XIT = sb("XIT", [128, NH, 128])
    GINVT = sb("GINVT", [128, NH, 128])
    S = sb("S", [128, NH, 128])
    SBF = sb("SBF", [128, NH, 128], BF16)
    WTS = sb("WTS", [128, NH, 128], BF16)
    BST = sb("BST", [128, NH, 128])
    SGAIN = sb("SGAIN", [128, 1024])
    COS = sb("COS", [128, T])
    SSG = sb("SSG", [128, T])
    INVF = sb("INVF", [128, 1])
    SFL = sb("SFL", [128, 1])
    EPST = sb("EPST", [128, 1])
    RETG = sb("RETG", [128, NH])
    ST = sb("ST", [128, 128])
    REG = sb("REG", [128, 81920 // 4])

    def carve(off_bytes, shape, dt):
        n = int(np.prod(shape[1:]))
        if dt == BF16:
            v = REG[:, off_bytes // 4: off_bytes // 4 + n // 2].bitcast(BF16)
        elif dt == I32:
            v = REG[:, off_bytes // 4: off_bytes // 4 + n].bitcast(I32)
        else:
            v = REG[:, off_bytes // 4: off_bytes // 4 + n]
        if len(shape) == 3:
            v = v.rearrange("p (a b) -> p a b", a=shape[1])
        return v

    ACTT = carve(0, [128, NFC, T], BF16)
    H = carve(45056, [128, 4, D], F32)
    SILU = [carve(77824, [128, T], F32), carve(79872, [128, T], F32)]
    MIXT = carve(0, [128, NKC, T], BF16)
    UT = carve(16384, [128, NH, T], BF16)
    VSG = carve(24576, [128, 4, 1024], F32)
    VLN = carve(40960, [128, 4, 1024], BF16)
    SGT = [carve(49152, [128, T], F32), carve(51200, [128, T], F32)]
    GT = carve(16384, [128, NH, T], BF16)
    QT = carve(24576, [128, NH, T], BF16)
    KT = carve(32768, [128, NH, T], BF16)
    KZ = carve(40960, [128, 4, 1024], BF16)
    V = carve(49152, [128, 4, 1024], BF16)
    RA = [carve(57344, [128, T], F32), carve(61440, [128, T], F32)]
    RB = [carve(59392, [128, T], F32), carve(63488, [128, T], F32)]
    RETSB = carve(65536, [128, NH, 128], F32)
    RETSQ = carve(69632, [128, NH, 128], F32)
    RETN = carve(73728, [128, NH, 128], F32)
    PT = carve(77824, [128, 2, T], BF16)
    TMPG = RETSQ
    POSI = carve(57344, [128, T], I32)
    ANG = carve(59392, [128, T], F32)
    TMPA = carve(61440, [128, T], F32)
    TMPB = carve(63488, [128, T], F32)

    PS = [ES.enter_context(nc.psum_tensor("PS%d" % i, [128, 512], F32)) for i in range(8)]

    sc = Sched()
    A = sc.add

    psrr = [0]

    def ps_next():
        b = psrr[0] % 8
        psrr[0] += 1
        return b

    def dma(eng, out, in_, sem, reads=(), writes=()):
        return A(eng, lambda e, o=out, i=in_: e.dma_start(out=o, in_=i), reads=reads, writes=writes, dma_sem=sem)

    def cast(stage, dst, src, last):
        A("pool", lambda e, o=dst, i=src: e.dma_start(out=o, in_=i), reads=(),
          writes=([("wstage", stage)] if last else ()), dma_sem="cast_" + stage)

    def st_gu(f, b):
        return "g%d_%d" % (f + 1, b // 4) if f == 0 else "g2"

    def st_d(f, cb):
        return "d1_%d" % cb if f == 0 else "d2"

    WIN_CAST_ORDER = [4, 5, 6, 7, 8, 9, 10, 11] + [b for b in range(24) if not (4 <= b <= 11)]

    def st_in(b):
        return "win_0" if 4 <= b <= 11 else "win_1"

    def cast_gu(f):
        for b in range(22):
            for W, Sx, lastw in ((WG[f], SG[f], False), (WU[f], SU[f], True)):
                src = W[:, b * 256:(b + 1) * 256].rearrange("(kc p) c -> p kc c", p=128)
                last = lastw and (b == 21 or (f == 0 and b % 4 == 3))
                cast(st_gu(f, b), Sx[b], src, last)

    def cast_down(f):
        for cb in range(4):
            for fb in range(6):
                f0 = fb * 8
                n = min(8, NFC - f0)
                src = WD[f][f0 * 128:(f0 + n) * 128, cb * 512:(cb + 1) * 512].rearrange("(fc p) c -> p fc c", p=128)
                cast(st_d(f, cb), SD[f][cb][:, f0:f0 + n, :], src, fb == 5 and (f == 0 or cb == 3))

    cast_gu(0)
    cast_down(0)
    for i_, b in enumerate(WIN_CAST_ORDER):
        src = WIN[:, b * 256:(b + 1) * 256].rearrange("(kc p) c -> p kc c", p=128)
        cast(st_in(b), SIN[b], src, i_ == 7 or i_ == 23)
    for b in range(8):
        src = WOUT[:, b * 256:(b + 1) * 256].rearrange("(kc p) c -> p kc c", p=128)
        cast("wout", SOUT[b], src, b == 7)
    cast_gu(1)
    cast_down(1)

    cl = []
    def cload(out, in_, name):
        dma("sp", out, in_, "const", writes=[name])
        cl.append(name)
    cload(IDB[:], C_IDB[:], "IDB")
    cload(IDF[:], C_IDF[:], "IDF")
    cload(MASK[:], C_MASK[:], "MASK")
    cload(XIT[:].rearrange("p a b -> p (a b)"), C_XI.partition_broadcast(128), "XIT")
    cload(GINVT[:].rearrange("p a b -> p (a b)"), C_GINV.partition_broadcast(128), "GINVT")
    cload(BST[:].rearrange("p a b -> p (a b)"), SGU_B.partition_broadcast(128), "BST")
    cload(SGAIN[:], G_SGU.partition_broadcast(128), "SGAIN")
    cload(INVF[:], C_INVF[:], "INVF")
    cload(SFL[:], SFLAG[:], "SFL")
    cload(RETG[:], G_RET[:], "RETG")
    for k_ in ("f1pre", "mpre", "f2pre"):
        cload(GPRE[k_][:], GTP[k_][:], "GPRE")
    last_const = sc.ops[-1].id
    for n in cl:
        sc.lw[n] = last_const

    A("dve", lambda e: e.memset(EPST[:], EPS), writes=["EPST"])
    A("dve", lambda e: e.memset(EPS4[:], 4.0 * EPS), writes=["EPS4"])
    A("dve", lambda e: e.memset(S[:].rearrange("p a b -> p (a b)"), 0.0), writes=["S"])
    A("dve", lambda e: e.memset(SBF[:].rearrange("p a b -> p (a b)"), 0.0), writes=["SBF"])

    for g in range(NH):
        wtmp = TMPA if g % 2 == 0 else TMPB
        nm = ("TMPW", g % 2)
        dma("sp", wtmp[:, 0:128], SGU_W[g], "misc%d" % (g % 2), writes=[nm])
        b = ps_next()
        A("pe", lambda e, b=b, w=wtmp: e.transpose(out=PS[b][:, 0:128], in_=w[:, 0:128], identity=IDF[:]),
          reads=[nm, "IDF"], writes=[("ps", b)])
        A("dve", lambda e, b=b, g=g: e.tensor_tensor(out=WTS[:, g, :], in0=PS[b][:, 0:128], in1=MASK[:], op=ALU.mult),
          reads=[("ps", b), "MASK"], writes=[("WTS", g)])

    wq = []

    def seq_ffn(f):
        for b in range(22):
            wq.append(("g", f, b))
            wq.append(("u", f, b))
        for cb in range(4):
            for fb in range(6):
                wq.append(("d", f, cb, fb))

    for g in range(NP):
        seq_ffn(0)
        for b in (4, 5, 6, 7, 8, 9, 10, 11):
            wq.append(("in", b))
    MIX_ORDER = [20, 21, 22, 23, 16, 17, 18, 19, 12, 13, 14, 15, 8, 9, 10, 11, 4, 5, 6, 7, 0, 1, 2, 3]
    for g in range(NM):
        seq_ffn(0)
        for b in MIX_ORDER:
            wq.append(("in", b))
        for b in range(8):
            wq.append(("out", b))
        seq_ffn(1)
    wstate = {"issued": 0, "taken": 0}

    def w_issue(i):
        key = wq[i]
        slot = i % NSLOT
        if key[0] in ("g", "u"):
            f, b = key[1], key[2]
            src = (SG if key[0] == "g" else SU)[f][b].rearrange("p a b -> p (a b)")
            dst = WS[slot][:, 0:4096]
            stage = st_gu(f, b)
        elif key[0] == "d":
            f, cb, fb = key[1], key[2], key[3]
            f0 = fb * 8
            n = min(8, NFC - f0)
            src = SD[f][cb][:, f0:f0 + n, :].rearrange("p a b -> p (a b)")
            dst = WS[slot][:, 0:n * 512]
            stage = st_d(f, cb)
        elif key[0] == "in":
            src = SIN[key[1]].rearrange("p a b -> p (a b)")
            dst = WS[slot][:, 0:4096]
            stage = st_in(key[1])
        else:
            src = SOUT[key[1]].rearrange("p a b -> p (a b)")
            dst = WS[slot][:, 0:4096]
            stage = "wout"
        dma("sp", dst, src, "ws%d" % slot, reads=[("wstage", stage)], writes=[("ws", slot)])

    def w_take(key, hold=0):
        while wstate["issued"] < min(len(wq), wstate["taken"] - hold + NSLOT):
            w_issue(wstate["issued"])
            wstate["issued"] += 1
        i = wstate["taken"]
        assert wq[i] == key, (wq[i], key)
        wstate["taken"] += 1
        return i % NSLOT

    def rstd_small(dst, src, n, scale, nm):
        A("act", lambda e: e.activation(out=dst, in_=src, func=AF.Sqrt, bias=EPST[:, 0:1], scale=scale),
          reads=[nm, "EPST"], writes=[nm])
        A("dve", lambda e: e.reciprocal(out=dst, in_=dst), reads=[nm], writes=[nm])

    def load_gain(key):
        dma("act", GS[:], GV[key].partition_broadcast(128), "gs", writes=["GS"])

    def S2(tt):
        xb = XNTOK[tt % 2]
        xn = ("XNTOK", tt % 2)
        A("act", lambda e: e.activation(out=xb[:], in_=Xs[:, tt, :], func=AF.Square, accum_out=ST[:, tt:tt + 1]),
          reads=[("X", tt)], writes=[xn, ("s2", tt)])
        A("act", lambda e: e.activation(out=ST[:, 8 + tt:9 + tt], in_=ST[:, tt:tt + 1], func=AF.Sqrt, bias=EPST[:, 0:1], scale=1.0 / D),
          reads=[("s2", tt), "EPST"], writes=[("r2", tt)])
        A("dve", lambda e: e.reciprocal(out=ST[:, 8 + tt:9 + tt], in_=ST[:, 8 + tt:9 + tt]), reads=[("r2", tt)], writes=[("r2", tt)])
        A("act", lambda e: e.activation(out=xb[:], in_=Xs[:, tt, :], func=AF.Copy, scale=ST[:, 8 + tt:9 + tt]),
          reads=[("X", tt), ("r2", tt)], writes=[xn])

    def S3(tt, gkey):
        xb = XNTOK[tt % 2]
        xn = ("XNTOK", tt % 2)
        gp = GPRE[gkey]
        for half in range(2):
            b = ps_next()
            pb = PS[b][:].bitcast(BF16)
            for k8 in range(8):
                kc = half * 8 + k8
                A("pe", lambda e, pb=pb, k8=k8, kc=kc: e.transpose(out=pb[:, k8 * 128:(k8 + 1) * 128],
                                                                  in_=xb[:, kc * 128:(kc + 1) * 128], identity=IDB[:]),
                  reads=[xn, "IDB"], writes=[("ps", b)])
            A("dve", lambda e, pb=pb, half=half: e.tensor_tensor(
                out=XNT[:, half * 8:(half + 1) * 8, tt * 128:(tt + 1) * 128],
                in0=pb.rearrange("p (a b) -> p a b", a=8),
                in1=gp[:, half * 8:(half + 1) * 8].unsqueeze(2).to_broadcast([128, 8, 128]), op=ALU.mult),
              reads=[("ps", b), "GPRE"], writes=[("XNT", tt)])

    XNT_ALL = [("XNT", tt) for tt in range(4)]

    def S1(tt, nparts, mul, to_h):
        A("dve", lambda e: e.reduce_sum(out=ST[:, 48 + tt:49 + tt], in_=ST[:, 16 + tt * 8:16 + tt * 8 + nparts], axis=AX.X),
          reads=[("ssp", tt)], writes=[("r1", tt)])
        bias = EPST if mul == 1.0 else EPS4
        assert mul in (1.0, 0.5)
        A("act", lambda e: e.activation(out=ST[:, 52 + tt:53 + tt], in_=ST[:, 48 + tt:49 + tt], func=AF.Sqrt, bias=bias[:, 0:1],
                                        scale=1.0 / (D * mul * mul)), reads=[("r1", tt), "EPST", "EPS4"], writes=[("r1b", tt)])
        A("dve", lambda e: e.reciprocal(out=ST[:, 52 + tt:53 + tt], in_=ST[:, 52 + tt:53 + tt]), reads=[("r1b", tt)], writes=[("r1b", tt)])
        if to_h:
            A("dve", lambda e: e.scalar_tensor_tensor(out=H[:, tt, :], in0=H[:, tt, :], scalar=ST[:, 52 + tt:53 + tt],
                                                      in1=Xs[:, tt, :], op0=ALU.mult, op1=ALU.add),
              reads=[("H", tt), ("r1b", tt), ("X", tt)], writes=[("H", tt)])
        else:
            A("dve", lambda e: e.scalar_tensor_tensor(out=Xs[:, tt, :], in0=H[:, tt, :], scalar=ST[:, 52 + tt:53 + tt],
                                                      in1=Xs[:, tt, :], op0=ALU.mult, op1=ALU.add),
              reads=[("H", tt), ("r1b", tt), ("X", tt)], writes=[("X", tt)])

    def pipelined(stages):
        n = len(stages)
        if not PIPE_NORMS:
            for tt in range(4):
                for st_ in stages:
                    st_(tt)
            return
        for step in range(4 + n - 1):
            for si in range(n):
                tt = step - si
                if 0 <= tt < 4:
                    stages[si](tt)

    def prenorm_all(gkey):
        pipelined([S2, lambda tt: S3(tt, gkey)])

    def boundary_all(nparts, mul, next_gkey, after_s1=None, to_h=False):
        def s1(tt):
            S1(tt, nparts, mul, to_h)
            if after_s1 is not None:
                after_s1(tt)
        st = [s1]
        if next_gkey is not None:
            st += [S2, lambda tt: S3(tt, next_gkey)]
        pipelined(st)

    FFN_NAMES = [("ACTT", j) for j in range(NFC)] + [("H", tt) for tt in range(4)] + [("SILU", i) for i in range(2)]
    MIX_NAMES = ([("MIXT", c) for c in range(4)] + ["UT", "VSG", "VLN", "SGT0", "SGT1", "GT", "QT", "KZ", "V",
                 "RA0", "RA1", "RB0", "RB1", "RETSB", "RETSQ", "RETN", "PT", "TMPG", "POSI", "ANG", "TMPA", "TMPB",
                 ("TMPW", 0), ("TMPW", 1)] + [("MIXS", g) for g in range(8)] + [("H", tt) for tt in range(4)] + [("KT", h_) for h_ in range(NH)]
                 + [("PT", 0), ("PT", 1)] + [("VSG", t_) for t_ in range(4)])

    def ffn(f, post, next_gkey, after_s1=None, to_h=False):
        sc.alias(FFN_NAMES, MIX_NAMES)
        load_gain(post)
        for b in range(22):
            sg = w_take(("g", f, b))
            su = w_take(("u", f, b), hold=1)
            gv = WS[sg][:, 0:4096].rearrange("p (a b) -> p a b", a=16)
            uv = WS[su][:, 0:4096].rearrange("p (a b) -> p a b", a=16)
            for jj in range(2):
                j = 2 * b + jj
                pg = ps_next()
                pu = ps_next()
                for kc in range(NKC):
                    A("pe", lambda e, pg=pg, gv=gv, kc=kc, jj=jj: e.matmul(PS[pg][:], lhsT=gv[:, kc, jj * 128:(jj + 1) * 128],
                                                                         rhs=XNT[:, kc, :], start=(kc == 0), stop=(kc == NKC - 1)),
                      reads=[("ws", sg)] + XNT_ALL, writes=[("ps", pg)])
                for kc in range(NKC):
                    A("pe", lambda e, pu=pu, uv=uv, kc=kc, jj=jj: e.matmul(PS[pu][:], lhsT=uv[:, kc, jj * 128:(jj + 1) * 128],
                                                                         rhs=XNT[:, kc, :], start=(kc == 0), stop=(kc == NKC - 1)),
                      reads=[("ws", su)] + XNT_ALL, writes=[("ps", pu)])
                si = j % 2
                A("act", lambda e, pg=pg, si=si: e.activation(out=SILU[si][:], in_=PS[pg][:], func=AF.Silu),
                  reads=[("ps", pg)], writes=[("SILU", si)])
                A("dve", lambda e, pu=pu, si=si, j=j: e.tensor_tensor(out=ACTT[:, j, :], in0=PS[pu][:], in1=SILU[si][:], op=ALU.mult),
                  reads=[("ps", pu), ("SILU", si)], writes=[("ACTT", j)])
        for cb in range(4):
            banks = [ps_next() for _ in range(4)]
            for fb in range(6):
                sl = w_take(("d", f, cb, fb))
                f0 = fb * 8
                n = min(8, NFC - f0)
                dv = WS[sl][:, 0:n * 512].rearrange("p (a b) -> p a b", a=n)
                for tt in range(4):
                    for fl in range(n):
                        fc = f0 + fl
                        A("pe", lambda e, bk=banks[tt], dv=dv, fl=fl, fc=fc, tt=tt: e.matmul(
                            PS[bk][:], lhsT=ACTT[:, fc, tt * 128:(tt + 1) * 128], rhs=dv[:, fl, :],
                            start=(fc == 0), stop=(fc == NFC - 1)),
                          reads=[("ws", sl), ("ACTT", fc)], writes=[("ps", banks[tt])])
            for tt in range(4):
                bk = banks[tt]
                A("dve", lambda e, bk=bk, tt=tt, cb=cb: e.tensor_tensor(out=H[:, tt, cb * 512:(cb + 1) * 512], in0=PS[bk][:],
                                                                        in1=GS[:, cb * 512:(cb + 1) * 512], op=ALU.mult),
                  reads=[("ps", bk), "GS"], writes=[("H", tt)])
                A("act", lambda e, bk=bk, tt=tt, cb=cb: e.activation(out=SILU[tt % 2][:], in_=PS[bk][:], func=AF.Square,
                                                                     accum_out=ST[:, 16 + tt * 8 + cb:17 + tt * 8 + cb]),
                  reads=[("ps", bk)], writes=[("SILU", tt % 2), ("ssp", tt)])
        boundary_all(4, 0.5, next_gkey, after_s1, to_h)

    def pos_tables(POS, g):
        dma("act", POSI[:], POS[:, g * T:(g + 1) * T].partition_broadcast(128), "pos", writes=["POSI"])
        A("dve", lambda e: e.tensor_copy(out=ANG[:], in_=POSI[:]), reads=["POSI"], writes=["ANG"])
        A("dve", lambda e: e.tensor_scalar(out=ANG[:], in0=ANG[:], scalar1=INVF[:, 0:1], scalar2=None, op0=ALU.mult),
          reads=["ANG", "INVF"], writes=["ANG"])
        for which in range(2):
            dst = SSG if which == 0 else COS
            shift = 0.0 if which == 0 else float(np.pi / 2)
            A("dve", lambda e, shift=shift: e.tensor_scalar(out=TMPA[:], in0=ANG[:], scalar1=shift, scalar2=None, op0=ALU.add),
              reads=["ANG"], writes=["TMPA"])
            A("dve", lambda e: e.tensor_scalar(out=TMPB[:], in0=TMPA[:], scalar1=float(1.0 / TWO_PI), scalar2=0.5,
                                               op0=ALU.mult, op1=ALU.add), reads=["TMPA"], writes=["TMPB"])
            A("dve", lambda e: e.tensor_copy(out=POSI[:], in_=TMPB[:]), reads=["TMPB"], writes=["POSI"])
            A("dve", lambda e: e.tensor_copy(out=TMPB[:], in_=POSI[:]), reads=["POSI"], writes=["TMPB"])
            A("dve", lambda e: e.scalar_tensor_tensor(out=TMPA[:], in0=TMPB[:], scalar=-TWO_PI, in1=TMPA[:],
                                                      op0=ALU.mult, op1=ALU.add), reads=["TMPB", "TMPA"], writes=["TMPA"])
            A("dve", lambda e: e.tensor_scalar(out=TMPB[:], in0=TMPA[:], scalar1=-float(np.pi), scalar2=TWO_PI,
                                               op0=ALU.is_lt, op1=ALU.mult), reads=["TMPA"], writes=["TMPB"])
            A("dve", lambda e: e.tensor_tensor(out=TMPA[:], in0=TMPA[:], in1=TMPB[:], op=ALU.add),
              reads=["TMPA", "TMPB"], writes=["TMPA"])
            A("act", lambda e, dst=dst: e.activation(out=dst[:], in_=TMPA[:], func=AF.Sin), reads=["TMPA"],
              writes=["SSG" if which == 0 else "COS"])
        A("dve", lambda e: e.tensor_scalar(out=SSG[0:64, :], in0=SSG[0:64, :], scalar1=-1.0, scalar2=None, op0=ALU.mult),
          reads=["SSG"], writes=["SSG"])

    def proj_fm(blk, handler):
        sl = w_take(("in", blk))
        wv = WS[sl][:, 0:4096].rearrange("p (a b) -> p a b", a=16)
        for cc in range(2):
            b = ps_next()
            for kc in range(NKC):
                A("pe", lambda e, b=b, wv=wv, kc=kc, cc=cc: e.matmul(PS[b][:], lhsT=wv[:, kc, cc * 128:(cc + 1) * 128],
                                                                  rhs=XNT[:, kc, :], start=(kc == 0), stop=(kc == NKC - 1)),
                  reads=[("ws", sl)] + XNT_ALL, writes=[("ps", b)])
            handler(b, cc)

    def proj_tm(blk, handler):
        sl = w_take(("in", blk))
        wv = WS[sl][:, 0:4096].rearrange("p (a b) -> p a b", a=16)
        for tt in range(4):
            b = ps_next()
            for kc in range(NKC):
                A("pe", lambda e, b=b, wv=wv, kc=kc, tt=tt: e.matmul(PS[b][:, 0:256], lhsT=XNT[:, kc, tt * 128:(tt + 1) * 128],
                                                                  rhs=wv[:, kc, :], start=(kc == 0), stop=(kc == NKC - 1)),
                  reads=[("ws", sl), ("XNT", tt)], writes=[("ps", b)])
            handler(b, tt)

    def rotary(b, h, dstT, dname, decT):
        i = h % 2
        A("dve", lambda e: e.tensor_tensor(out=RA[i][:], in0=PS[b][:], in1=COS[:], op=ALU.mult),
          reads=[("ps", b), "COS"], writes=["RA%d" % i])
        A("dve", lambda e: e.tensor_tensor(out=RB[i][0:64, :], in0=PS[b][64:128, :], in1=SSG[0:64, :], op=ALU.mult),
          reads=[("ps", b), "SSG"], writes=["RB%d" % i])
        A("dve", lambda e: e.tensor_tensor(out=RB[i][64:128, :], in0=PS[b][0:64, :], in1=SSG[64:128, :], op=ALU.mult),
          reads=[("ps", b), "SSG"], writes=["RB%d" % i])
        A("dve", lambda e: e.tensor_tensor(out=RA[i][:], in0=RA[i][:], in1=RB[i][:], op=ALU.add),
          reads=["RA%d" % i, "RB%d" % i], writes=["RA%d" % i])
        A("dve", lambda e: e.tensor_tensor(out=dstT[:, h, :].rearrange("p (a b) -> p a b", a=4),
                                           in0=RA[i][:].rearrange("p (a b) -> p a b", a=4),
                                           in1=decT[:, h, :].unsqueeze(1).to_broadcast([128, 4, 128]), op=ALU.mult),
          reads=["RA%d" % i, "XIT", "GINVT"], writes=[dname])

    kpend = []

    def k_flush():
        while kpend:
            h = kpend.pop(0)
            b2 = ps_next()
            pb = PS[b2][:].bitcast(BF16)
            for c in range(4):
                A("pe", lambda e, pb=pb, c=c, h=h: e.transpose(out=pb[:, c * 128:(c + 1) * 128], in_=KT[:, h, c * 128:(c + 1) * 128],
                                                            identity=IDB[:]), reads=[("KT", h), "IDB"], writes=[("ps", b2)])
            A("act", lambda e, pb=pb, h=h: e.activation(out=KZ[:, :, h * 128:(h + 1) * 128],
                                                       in_=pb[:, 0:512].rearrange("p (a b) -> p a b", a=4),
                                                       func=AF.Copy, scale=cdv[h]),
              reads=[("ps", b2)], writes=["KZ"])

    def k_block(blk):
        sl = w_take(("in", blk))
        wv = WS[sl][:, 0:4096].rearrange("p (a b) -> p a b", a=16)
        for cc in range(2):
            h = (blk - 4) * 2 + cc
            b = ps_next()
            for kc in range(NKC):
                A("pe", lambda e, b=b, wv=wv, kc=kc, cc=cc: e.matmul(PS[b][:], lhsT=wv[:, kc, cc * 128:(cc + 1) * 128],
                                                                  rhs=XNT[:, kc, :], start=(kc == 0), stop=(kc == NKC - 1)),
                  reads=[("ws", sl)] + XNT_ALL, writes=[("ps", b)])
            k_flush()
            rotary(b, h, KT, ("KT", h), GINVT)
            kpend.append(h)

    def v_block(blk):
        def hv(b, tt):
            c0 = (blk - 8) * 256
            A("act", lambda e, b=b, tt=tt, c0=c0: e.activation(out=V[:, tt, c0:c0 + 256], in_=PS[b][:, 0:256], func=AF.Copy),
              reads=[("ps", b)], writes=["V"])
        proj_tm(blk, hv)

    def kv_update(c):
        for hb in range(2):
            b = ps_next()
            for hh in range(4):
                h = hb * 4 + hh
                A("pe", lambda e, b=b, hh=hh, h=h, c=c: e.matmul(PS[b][:, hh * 128:(hh + 1) * 128], lhsT=KZ[:, c, h * 128:(h + 1) * 128],
                                                              rhs=V[:, c, h * 128:(h + 1) * 128], start=True, stop=True),
                  reads=["KZ", "V"], writes=[("ps", b)])
            for hh in range(4):
                h = hb * 4 + hh
                A("dve", lambda e, b=b, hh=hh, h=h: e.scalar_tensor_tensor(out=S[:, h, :], in0=S[:, h, :], scalar=cdv[h],
                                                                        in1=PS[b][:, hh * 128:(hh + 1) * 128], op0=ALU.mult, op1=ALU.add),
                  reads=["S", ("ps", b)], writes=["S"])
        A("act", lambda e: e.activation(out=SBF[:].rearrange("p a b -> p (a b)"), in_=S[:].rearrange("p a b -> p (a b)"), func=AF.Copy),
          reads=["S"], writes=["SBF"])

    def load_x_tt(XD, g, tt, eng="sp"):
        dma("sp", Xs[:, tt, :], XD[g * T + tt * 128: g * T + (tt + 1) * 128, :], "xi%d" % tt, writes=[("X", tt)])

    def start_group(XD, g, eng="pool"):
        for tt in range(4):
            load_x_tt(XD, g, tt, eng)
        prenorm_all("f1pre")

    def prefix_group(g, nxt):
        ffn(0, "f1post", "mpre")
        sc.alias(MIX_NAMES, FFN_NAMES)
        for tt in range(4):
            load_x_tt(nxt[0], nxt[1], tt)
        pos_tables(POSP, g)
        for blk in (4, 5, 6, 7):
            k_block(blk)
        k_flush()
        for blk in (8, 9, 10, 11):
            v_block(blk)
        prenorm_all("f1pre")
        for c in range(4):
            kv_update(c)

    def store_y_tt(g, tt):
        dma("pool", Y[g * T + tt * 128: g * T + (tt + 1) * 128, :], Xs[:, tt, :], "st%d" % tt, reads=[("X", tt)])

    def store_y(g):
        for tt in range(4):
            store_y_tt(g, tt)

    def main_group(g, nxt):
        ffn(0, "f1post", None if DBG == "ffn1" else "mpre")
        if DBG == "ffn1":
            return store_y(g)
        sc.alias(MIX_NAMES, FFN_NAMES)
        load_gain("mpost")
        pos_tables(POSM, g)
        for blk in (20, 21, 22, 23):
            def hvs(b, tt, blk=blk):
                c0 = (blk - 20) * 256
                A("act", lambda e, b=b, tt=tt, c0=c0: e.activation(out=VSG[:, tt, c0:c0 + 256], in_=PS[b][:, 0:256],
                                                                   func=AF.Gelu_apprx_tanh), reads=[("ps", b)], writes=[("VSG", tt)])
            proj_tm(blk, hvs)
        for tt in range(4):
            A("dve", lambda e, tt=tt: e.reduce_sum(out=ST[:, 56:57], in_=VSG[:, tt, :], axis=AX.X),
              reads=[("VSG", tt)], writes=["lnv"])
            A("act", lambda e, tt=tt: e.activation(out=VLN[:, tt, :], in_=VSG[:, tt, :], func=AF.Square, accum_out=ST[:, 57:58]),
              reads=[("VSG", tt)], writes=["VLN", "lnv"])
            A("dve", lambda e: e.tensor_scalar(out=ST[:, 58:59], in0=ST[:, 56:57], scalar1=1.0 / 1024, scalar2=None, op0=ALU.mult),
              reads=["lnv"], writes=["lnv"])
            A("dve", lambda e: e.tensor_tensor(out=ST[:, 59:60], in0=ST[:, 58:59], in1=ST[:, 58:59], op=ALU.mult),
              reads=["lnv"], writes=["lnv"])
            A("dve", lambda e: e.scalar_tensor_tensor(out=ST[:, 59:60], in0=ST[:, 57:58], scalar=1.0 / 1024, in1=ST[:, 59:60],
                                                      op0=ALU.mult, op1=ALU.subtract), reads=["lnv"], writes=["lnv"])
            rstd_small(ST[:, 60:61], ST[:, 59:60], 1, 1.0, "lnv")
            A("dve", lambda e: e.scalar_tensor_tensor(out=ST[:, 61:62], in0=ST[:, 58:59], scalar=-1.0, in1=ST[:, 60:61],
                                                      op0=ALU.mult, op1=ALU.mult), reads=["lnv"], writes=["lnv"])
            A("dve", lambda e, tt=tt: e.tensor_scalar(out=VSG[:, tt, :], in0=VSG[:, tt, :], scalar1=ST[:, 60:61], scalar2=ST[:, 61:62],
                                                      op0=ALU.mult, op1=ALU.add), reads=[("VSG", tt), "lnv"], writes=[("VSG", tt)])
            A("dve", lambda e, tt=tt: e.tensor_tensor(out=VLN[:, tt, :], in0=VSG[:, tt, :], in1=SGAIN[:], op=ALU.mult),
              reads=[("VSG", tt), "SGAIN"], writes=["VLN"])
        for blk in (16, 17, 18, 19):
            def hu(b, cc, blk=blk):
                gi = (blk - 16) * 2 + cc
                A("act", lambda e, b=b, gi=gi: e.activation(out=UT[:, gi, :], in_=PS[b][:], func=AF.Gelu_apprx_tanh),
                  reads=[("ps", b)], writes=["UT"])
            proj_fm(blk, hu)
        for gi in range(NH):
            b = ps_next()
            for c in range(4):
                A("pe", lambda e, b=b, c=c, gi=gi: e.matmul(PS[b][:, c * 128:(c + 1) * 128], lhsT=VLN[:, c, gi * 128:(gi + 1) * 128],
                                                          rhs=WTS[:, gi, :], start=True, stop=True),
                  reads=["VLN", ("WTS", gi)], writes=[("ps", b)])
            i = gi % 2
            A("dve", lambda e, b=b, gi=gi, i=i: e.tensor_tensor(out=SGT[i][:].rearrange("p (a b) -> p a b", a=4),
                                                              in0=PS[b][:].rearrange("p (a b) -> p a b", a=4),
                                                              in1=BST[:, gi, :].unsqueeze(1).to_broadcast([128, 4, 128]), op=ALU.add),
              reads=[("ps", b), "BST"], writes=["SGT%d" % i])
            A("dve", lambda e, gi=gi, i=i: e.tensor_tensor(out=MIXT[:, 8 + gi, :], in0=SGT[i][:], in1=UT[:, gi, :], op=ALU.mult),
              reads=["SGT%d" % i, "UT"], writes=[("MIXS", gi)])
        sc.alias(["GT", "QT", "KZ", "V"] + [("KT", h_) for h_ in range(NH)], ["UT", "VSG", "VLN", "SGT0", "SGT1"] + [("VSG", t) for t in range(4)])
        for blk in (12, 13, 14, 15):
            def hg(b, cc, blk=blk):
                h = (blk - 12) * 2 + cc
                A("act", lambda e, b=b, h=h: e.activation(out=GT[:, h, :], in_=PS[b][:], func=AF.Silu),
                  reads=[("ps", b)], writes=["GT"])
            proj_fm(blk, hg)
        for blk in (8, 9, 10, 11):
            v_block(blk)
        for blk in (4, 5, 6, 7):
            k_block(blk)
        k_flush()
        for blk in (0, 1, 2, 3):
            def hq(b, cc, blk=blk):
                h = blk * 2 + cc
                rotary(b, h, QT, "QT", XIT)
            proj_fm(blk, hq)
        for c in range(4):
            cs = slice(c * 128, (c + 1) * 128)
            rb = []
            for hb in range(2):
                b = ps_next()
                for hh in range(4):
                    h = hb * 4 + hh
                    A("pe", lambda e, b=b, hh=hh, h=h, cs=cs: e.matmul(PS[b][:, hh * 128:(hh + 1) * 128], lhsT=KT[:, h, cs], rhs=QT[:, h, cs],
                                                                    start=True, stop=True),
                      reads=[("KT", h), "QT"], writes=[("ps", b)])
                A("dve", lambda e, b=b, hb=hb: e.tensor_tensor(out=PT[:, hb, :].rearrange("p (a b) -> p a b", a=4),
                                                             in0=PS[b][:].rearrange("p (a b) -> p a b", a=4),
                                                             in1=MASK[:].unsqueeze(1).to_broadcast([128, 4, 128]), op=ALU.mult),
                  reads=[("ps", b), "MASK"], writes=[("PT", hb)])
            for hb in range(2):
                b = ps_next()
                rb.append(b)
                for hh in range(4):
                    h = hb * 4 + hh
                    A("pe", lambda e, b=b, hh=hh, h=h, hb=hb, c=c: e.matmul(PS[b][:, hh * 128:(hh + 1) * 128], lhsT=PT[:, hb, hh * 128:(hh + 1) * 128],
                                                                         rhs=V[:, c, h * 128:(h + 1) * 128], start=True, stop=False),
                      reads=[("PT", hb), "V"], writes=[("ps", b)])
                    A("pe", lambda e, b=b, hh=hh, h=h, cs=cs: e.matmul(PS[b][:, hh * 128:(hh + 1) * 128], lhsT=QT[:, h, cs],
                                                                    rhs=SBF[:, h, :], start=False, stop=True),
                      reads=["QT", "SBF"], writes=[("ps", b)])
            kv_update(c)
            for hb in range(2):
                b = rb[hb]
                A("act", lambda e, b=b, hb=hb: e.activation(out=RETSB[:, hb * 4:(hb + 1) * 4, :].rearrange("p a b -> p (a b)"),
                                                          in_=PS[b][:], func=AF.Copy), reads=[("ps", b)], writes=["RETSB"])
                A("act", lambda e, b=b, hb=hb: e.activation(out=RETSQ[:, hb * 4:(hb + 1) * 4, :].rearrange("p a b -> p (a b)"),
                                                          in_=PS[b][:], func=AF.Square), reads=[("ps", b)], writes=["RETSQ"])
            A("dve", lambda e: e.reduce_sum(out=ST[:, 64:72], in_=RETSB[:], axis=AX.X), reads=["RETSB"], writes=["lnr"])
            A("dve", lambda e: e.reduce_sum(out=ST[:, 80:88], in_=RETSQ[:], axis=AX.X), reads=["RETSQ"], writes=["lnr"])
            A("dve", lambda e: e.tensor_scalar(out=ST[:, 64:72], in0=ST[:, 64:72], scalar1=1.0 / 128, scalar2=None, op0=ALU.mult),
              reads=["lnr"], writes=["lnr"])
            A("dve", lambda e: e.tensor_tensor(out=ST[:, 72:80], in0=ST[:, 64:72], in1=ST[:, 64:72], op=ALU.mult),
              reads=["lnr"], writes=["lnr"])
            A("dve", lambda e: e.scalar_tensor_tensor(out=ST[:, 80:88], in0=ST[:, 80:88], scalar=1.0 / 128, in1=ST[:, 72:80],
                                                      op0=ALU.mult, op1=ALU.subtract), reads=["lnr"], writes=["lnr"])
            rstd_small(ST[:, 80:88], ST[:, 80:88], 8, 1.0, "lnr")
            A("dve", lambda e: e.tensor_tensor(out=RETSB[:], in0=RETSB[:], in1=ST[:, 64:72].unsqueeze(2).to_broadcast([128, 8, 128]),
                                               op=ALU.subtract), reads=["RETSB", "lnr"], writes=["RETSB"])
            A("dve", lambda e: e.tensor_tensor(out=RETN[:], in0=RETSB[:], in1=ST[:, 80:88].unsqueeze(2).to_broadcast([128, 8, 128]),
                                               op=ALU.mult), reads=["RETSB", "lnr"], writes=["RETN"])
            for hb in range(2):
                b = ps_next()
                for hh in range(4):
                    h = hb * 4 + hh
                    A("pe", lambda e, b=b, hh=hh, h=h: e.transpose(out=PS[b][:, hh * 128:(hh + 1) * 128], in_=RETN[:, h, :], identity=IDF[:]),
                      reads=["RETN", "IDF"], writes=[("ps", b)])
                A("dve", lambda e, b=b, hb=hb: e.tensor_tensor(out=TMPG[:, hb * 4:(hb + 1) * 4, :],
                                                             in0=PS[b][:].rearrange("p (a b) -> p a b", a=4),
                                                             in1=RETG[:, hb * 4:(hb + 1) * 4].unsqueeze(2).to_broadcast([128, 4, 128]),
                                                             op=ALU.mult), reads=[("ps", b), "RETG"], writes=["RETSQ"])
                A("dve", lambda e, hb=hb, cs=cs: e.tensor_tensor(out=MIXT[:, hb * 4:(hb + 1) * 4, cs], in0=TMPG[:, hb * 4:(hb + 1) * 4, :],
                                                               in1=GT[:, hb * 4:(hb + 1) * 4, cs], op=ALU.mult),
                  reads=["RETSQ", "GT"], writes=[("MIXT", c)])
        sc.alias([("H", tt) for tt in range(4)], ["KZ", "V", "RA0", "RA1", "RB0", "RB1", "RETSB", "RETSQ", "RETN", "QT"] + [("KT", h_) for h_ in range(NH)] + [
                                                  "POSI", "ANG", "TMPA", "TMPB"])
        MIX_ALL = [("MIXT", c) for c in range(4)] + [("MIXS", g_) for g_ in range(8)]
        for blk in range(8):
            sl = w_take(("out", blk))
            wv = WS[sl][:, 0:4096].rearrange("p (a b) -> p a b", a=16)
            for tt in range(4):
                b = ps_next()
                for fc in range(NKC):
                    A("pe", lambda e, b=b, wv=wv, fc=fc, tt=tt: e.matmul(PS[b][:, 0:256], lhsT=MIXT[:, fc, tt * 128:(tt + 1) * 128],
                                                                      rhs=wv[:, fc, :], start=(fc == 0), stop=(fc == NKC - 1)),
                      reads=[("ws", sl)] + MIX_ALL, writes=[("ps", b)])
                A("dve", lambda e, b=b, tt=tt, blk=blk: e.tensor_tensor(out=H[:, tt, blk * 256:(blk + 1) * 256], in0=PS[b][:, 0:256],
                                                                        in1=GS[:, blk * 256:(blk + 1) * 256], op=ALU.mult),
                  reads=[("ps", b), "GS"], writes=[("H", tt)])
                A("act", lambda e, b=b, tt=tt, blk=blk: e.activation(out=PT[:].rearrange("p a b -> p (a b)")[:, 0:256], in_=PS[b][:, 0:256],
                                                                     func=AF.Square, accum_out=ST[:, 16 + tt * 8 + blk:17 + tt * 8 + blk]),
                  reads=[("ps", b)], writes=[("PT", 0), ("ssp", tt)])
        boundary_all(8, 1.0, None if DBG == "mix" else "f2pre")
        if DBG == "mix":
            return store_y(g)

        def after(tt):
            dma("pool", Y[g * T + tt * 128: g * T + (tt + 1) * 128, :], H[:, tt, :], "st%d" % tt, reads=[("H", tt)])
            if nxt is not None:
                load_x_tt(nxt[0], nxt[1], tt)
        ffn(1, "f2post", "f1pre" if nxt is not None else None, after_s1=after, to_h=True)

    seq = [("p", g) for g in range(NP)] + [("m", g) for g in range(NM)]
    first = seq[0]
    start_group(XP if first[0] == "p" else XM, first[1], eng="sp")
    for i, (kind, g) in enumerate(seq):
        nx = seq[i + 1] if i + 1 < len(seq) else None
        nxt = None if nx is None else ((XP if nx[0] == "p" else XM), nx[1])
        if kind == "p":
            prefix_group(g, nxt)
            if nx is not None and nx[0] == "m":
                A("dve", lambda e: e.tensor_scalar(out=S[:].rearrange("p a b -> p (a b)"), in0=S[:].rearrange("p a b -> p (a b)"),
                                                   scalar1=SFL[:, 0:1], scalar2=None, op0=ALU.mult), reads=["S", "SFL"], writes=["S"])
                A("act", lambda e: e.activation(out=SBF[:].rearrange("p a b -> p (a b)"), in_=S[:].rearrange("p a b -> p (a b)"),
                                                func=AF.Copy), reads=["S"], writes=["SBF"])
        else:
            main_group(g, nxt)
    assert DBG or wstate["taken"] == len(wq)

    print('sbuf bytes remaining', nc.sbuf_bytes_remaining)
    sc.finalize()
    sem_names = sorted({s for op in sc.ops if op.sig for s in [op.sig[0]]} | {"e_" + e for e in ENGS})
    sems = {n: ES.enter_context(nc.semaphore(n)) for n in sem_names}

    def emit(engname, e):
        for op in sc.ops:
            if op.eng != engname:
                continue
            for s, v in op.waits:
                e.wait_ge(sems[s], v)
            ins = op.fn(e)
            if op.sig is not None:
                ins.then_inc(sems[op.sig[0]], 16 if op.dma_sem is not None else 1)
        if engname == "pool":
            for s, v in sc.dma_totals.items():
                if s.startswith("st"):
                    e.wait_ge(sems[s], v)

    with nc.Block() as block:
        @block.tensor
        def _(e):
            emit("pe", e)

        @block.scalar
        def _(e):
            emit("act", e)

        @block.vector
        def _(e):
            emit("dve", e)

        @block.gpsimd
        def _(e):
            emit("pool", e)

        @block.sync
        def _(e):
            emit("sp", e)
    ES.close()
    return nc, CST


def make_in_maps(inputs, NP, NM, B, CST):
    x = np.asarray(inputs["x"], dtype=np.float32)
    pos = np.asarray(inputs["positions"], dtype=np.int32)
    half = NM * T
    f32 = lambda a: np.ascontiguousarray(np.asarray(a, dtype=np.float32))
    shared = {
        "wg1": f32(inputs["ffn1_w_gate"][0]), "wu1": f32(inputs["ffn1_w_up"][0]), "wd1": f32(inputs["ffn1_w_down"][0]),
        "wg2": f32(inputs["ffn2_w_gate"][0]), "wu2": f32(inputs["ffn2_w_up"][0]), "wd2": f32(inputs["ffn2_w_down"][0]),
        "win": f32(inputs["w_in"][0]), "wout": f32(inputs["w_out"][0]),
        "g_f1pre": f32(inputs["ffn1_pre_g"]).reshape(1, D), "g_f1post": f32(inputs["ffn1_post_g"]).reshape(1, D),
        "g_mpre": f32(inputs["mix_pre_g"]).reshape(1, D), "g_mpost": f32(inputs["mix_post_g"]).reshape(1, D),
        "g_f2pre": f32(inputs["ffn2_pre_g"]).reshape(1, D), "g_f2post": f32(inputs["ffn2_post_g"]).reshape(1, D),
        "gt_f1pre": f32(np.asarray(inputs["ffn1_pre_g"], dtype=np.float32).reshape(NKC, 128).T),
        "gt_mpre": f32(np.asarray(inputs["mix_pre_g"], dtype=np.float32).reshape(NKC, 128).T),
        "gt_f2pre": f32(np.asarray(inputs["ffn2_pre_g"], dtype=np.float32).reshape(NKC, 128).T),
        "g_ret": f32(np.asarray(inputs["ret_norm_g"], dtype=np.float32).reshape(NH, 128).T),
        "g_sgu": f32(inputs["sgu_norm_g"]).reshape(1, 1024),
        "sgu_w": f32(inputs["sgu_w_s"][0]), "sgu_b": f32(inputs["sgu_b_s"][0]).reshape(1, 1024),
        "c_idb": CST["c_idb"], "c_idf": CST["c_idf"], "c_mask": CST["c_mask"], "c_xi": CST["c_xi"],
        "c_ginv": CST["c_ginv"], "c_invf": CST["c_invf"],
    }
    maps = []
    zeros_half = np.zeros((half, D), np.float32)
    for c in range(2 * B):
        b, hf = c // 2, c % 2
        m = dict(shared)
        m["xm"] = np.ascontiguousarray(x[b, hf * half:(hf + 1) * half])
        m["xp"] = np.ascontiguousarray(x[b, 0:half]) if hf == 1 else zeros_half
        m["posm"] = np.ascontiguousarray(pos[b, hf * half:(hf + 1) * half]).reshape(1, half)
        m["posp"] = np.ascontiguousarray(pos[b, 0:half]).reshape(1, half)
        m["sflag"] = np.full((128, 1), float(hf), np.float32)
        maps.append(m)
    return maps


def run(inputs, NG):
    B = np.asarray(inputs["x"]).shape[0]
    assert 2 * B == 8
    nc, CST = build(NG, NG)
    maps = make_in_maps(inputs, NG, NG, B, CST)
    res = run_bass_kernel_spmd(nc, maps, core_ids=list(range(8)))
    half = NG * T
    out = np.empty((B, 2 * half, D), np.float32)
    for c in range(8):
        out[c // 2, (c % 2) * half:(c % 2 + 1) * half] = res.results[c]["y"]
    return out


def kernel(**inputs):
    return run(inputs, 8)
```

```python
import numpy as np
import ml_dtypes
from contextlib import ExitStack
import concourse.bass as bass
import concourse.mybir as mybir
from concourse.bass_utils import run_bass_kernel_spmd

F32 = mybir.dt.float32
BF16 = mybir.dt.bfloat16
I32 = mybir.dt.int32
AF = mybir.ActivationFunctionType
ALU = mybir.AluOpType
AX = mybir.AxisListType

D = 2048
DFF = 5632
T = 512
NH = 8
NKC = 16
NFC = 44
EPS = 1e-6
NSLOT = 4
TWO_PI = float(2 * np.pi)
SAME_ENGINE_SYNC = True
DBG = None
PIPE_NORMS = True
SCRATCH_KIND = "ExternalOutput"

ENGS = ("pe", "act", "dve", "pool", "sp")


class Op:
    __slots__ = ("id", "eng", "fn", "deps", "dma_sem", "signal", "sig", "waits", "real")


class Sched:
    def __init__(self):
        self.ops = []
        self.lw = {}
        self.rd = {}

    def add(self, eng, fn, reads=(), writes=(), dma_sem=None):
        op = Op()
        op.id = len(self.ops)
        op.eng = eng
        op.fn = fn
        op.dma_sem = dma_sem
        op.signal = False
        op.sig = None
        deps = set()
        for r in reads:
            w = self.lw.get(r)
            if w is not None:
                deps.add(w)
        if eng != "pe":
            for r in reads:
                if isinstance(r, tuple) and r[0] == "ps":
                    key = ("psr", r[1])
                    l = self.lw.get(key)
                    if l is not None and self.ops[l].eng != eng:
                        deps.add(l)
                    self.lw[key] = op.id
        for w in writes:
            l = self.lw.get(w)
            if l is not None:
                deps.add(l)
            for x in self.rd.get(w, ()):
                deps.add(x)
        for w in writes:
            self.lw[w] = op.id
            self.rd[w] = []
        for r in reads:
            self.rd.setdefault(r, []).append(op.id)
        deps.discard(op.id)
        op.deps = deps
        self.ops.append(op)
        return op

    def alias(self, new_names, old_names):
        acc = set()
        for o in old_names:
            l = self.lw.get(o)
            if l is not None:
                acc.add(l)
            acc.update(self.rd.get(o, ()))
        for n in new_names:
            l = self.lw.get(n)
            if l is not None:
                acc.add(l)
            acc.update(self.rd.get(n, ()))
        acc = sorted(acc)
        for n in new_names:
            self.lw[n] = None
            self.rd[n] = list(acc)

    def finalize(self):
        ops = self.ops
        for op in ops:
            real = []
            best = {}
            for d in op.deps:
                p = ops[d]
                if p.dma_sem is None and p.eng == op.eng:
                    if op.eng in ("pe", "sp") or not SAME_ENGINE_SYNC:
                        continue
                if p.dma_sem is None:
                    if p.eng not in best or best[p.eng].id < p.id:
                        best[p.eng] = p
                else:
                    real.append(p)
            for p in best.values():
                real.append(p)
            for p in real:
                p.signal = True
            op.real = real
        cnt = {e: 0 for e in ENGS}
        dcnt = {}
        for op in ops:
            if op.dma_sem is not None:
                dcnt[op.dma_sem] = dcnt.get(op.dma_sem, 0) + 16
                op.sig = (op.dma_sem, dcnt[op.dma_sem])
            elif op.signal:
                cnt[op.eng] += 1
                op.sig = ("e_" + op.eng, cnt[op.eng])
        seen = {e: {} for e in ENGS}
        for op in ops:
            need = {}
            for p in op.real:
                s, v = p.sig
                if v > need.get(s, 0):
                    need[s] = v
            op.waits = []
            for s, v in need.items():
                if seen[op.eng].get(s, 0) >= v:
                    continue
                seen[op.eng][s] = v
                op.waits.append((s, v))
        self.dma_totals = dcnt


def _consts():
    f = np.float32
    h = np.arange(NH, dtype=f)
    log_gamma = np.log1p(-np.exp2(f(-5.0) - h)).astype(f)
    idx = np.arange(128, dtype=f)
    xi = np.exp(log_gamma[None, :] * (idx + f(1.0))[:, None]).astype(f)
    ginv = np.exp(-log_gamma[None, :] * (idx + f(1.0))[:, None]).astype(f)
    cd = np.exp(log_gamma * f(128.0)).astype(f)
    scale = f(128.0 ** -0.5)
    c_xi = np.ascontiguousarray((xi * scale).T).reshape(1, NH * 128).astype(f)
    c_ginv = np.ascontiguousarray(ginv.T).reshape(1, NH * 128).astype(f)
    half = 64
    inv_freq = (f(10000.0) ** (-np.arange(half, dtype=f) / f(half))).astype(f)
    c_invf = np.concatenate([inv_freq, inv_freq]).reshape(128, 1).astype(f)
    m = np.arange(128)
    c_mask = (m[None, :] >= m[:, None]).astype(f)
    return dict(c_xi=c_xi, c_ginv=c_ginv, cd=[float(x) for x in cd], c_invf=c_invf, c_mask=c_mask,
                c_idb=np.eye(128).astype(ml_dtypes.bfloat16), c_idf=np.eye(128, dtype=f))


def build(NP, NM):
    nc = bass.Bass("TRN2", target_bir_lowering=False)
    CST = _consts()
    cdv = CST["cd"]
    NTP = max(NP, 1) * T
    NTM = NM * T

    def din(name, shape, dt=F32):
        return nc.dram_tensor(name, list(shape), dt, kind="ExternalInput").ap()

    XP = din("xp", [NTP, D])
    XM = din("xm", [NTM, D])
    POSP = din("posp", [1, NTP], I32)
    POSM = din("posm", [1, NTM], I32)
    WG = [din("wg1", [D, DFF]), din("wg2", [D, DFF])]
    WU = [din("wu1", [D, DFF]), din("wu2", [D, DFF])]
    WD = [din("wd1", [DFF, D]), din("wd2", [DFF, D])]
    WIN = din("win", [D, 6144])
    WOUT = din("wout", [D, D])
    GV = {k: din("g_" + k, [1, D]) for k in ("f1pre", "f1post", "mpre", "mpost", "f2pre", "f2post")}
    GTP = {k: din("gt_" + k, [128, NKC]) for k in ("f1pre", "mpre", "f2pre")}
    G_RET = din("g_ret", [128, NH])
    G_SGU = din("g_sgu", [1, 1024])
    SGU_W = din("sgu_w", [NH, 128, 128])
    SGU_B = din("sgu_b", [1, 1024])
    C_IDB = din("c_idb", [128, 128], BF16)
    C_IDF = din("c_idf", [128, 128])
    C_MASK = din("c_mask", [128, 128])
    C_XI = din("c_xi", [1, 1024])
    C_GINV = din("c_ginv", [1, 1024])
    C_INVF = din("c_invf", [128, 1])
    SFLAG = din("sflag", [128, 1])
    Y = nc.dram_tensor("y", [NTM, D], F32, kind="ExternalOutput").ap()

    def dint(name, shape):
        return nc.dram_tensor(name, list(shape), BF16, kind=SCRATCH_KIND).ap()

    SG = [dint("sg1", [22, 128, 16, 256]), dint("sg2", [22, 128, 16, 256])]
    SU = [dint("su1", [22, 128, 16, 256]), dint("su2", [22, 128, 16, 256])]
    SD = [dint("sd1", [4, 128, NFC, 512]), dint("sd2", [4, 128, NFC, 512])]
    SIN = dint("sin_", [24, 128, 16, 256])
    SOUT = dint("sout", [8, 128, 16, 256])

    ES = ExitStack()

    def sb(name, shape, dt=F32):
        return ES.enter_context(nc.sbuf_tensor(name, list(shape), dt))

    Xs = sb("Xs", [128, 4, D])
    XNT = sb("XNT", [128, NKC, T], BF16)
    WS = [sb("WS%d" % i, [128, 4096], BF16) for i in range(NSLOT)]
    GS = sb("GS", [128, D])
    XNTOK = [sb("XNTOK0", [128, D], BF16), sb("XNTOK1", [128, D], BF16)]
    EPS4 = sb("EPS4", [128, 1])
    GPRE = {k: sb("GP_" + k, [128, NKC]) for k in ("f1pre", "mpre", "f2pre")}
    IDB = sb("IDB", [128, 128], BF16)
    IDF = sb("IDF", [128, 128])
    MASK = sb("MASK", [128, 128])
    XIT = sb("XIT", [128, NH, 128])
    GINVT = sb("GINVT", [128, NH, 128])
    S = sb("S", [128, NH, 128])
    SBF = sb("SBF", [128, NH, 128], BF16)
    WTS = sb("WTS", [128, NH, 128], BF16)
    BST = sb("BST", [128, NH, 128])
    SGAIN = sb("SGAIN", [128, 1024])
    COS = sb("COS", [128, T])
    SSG = sb("SSG", [128, T])
    INVF = sb("INVF", [128, 1])
    SFL = sb("SFL", [128, 1])
    EPST = sb("EPST", [128, 1])
    RETG = sb("RETG", [128, NH])
    ST = sb("ST", [128, 160])
    REG = sb("REG", [128, 81920 // 4])

    def carve(off_bytes, shape, dt):
        n = int(np.prod(shape[1:]))
        if dt == BF16:
            v = REG[:, off_bytes // 4: off_bytes // 4 + n // 2].bitcast(BF16)
        elif dt == I32:
            v = REG[:, off_bytes // 4: off_bytes // 4 + n].bitcast(I32)
        else:
            v = REG[:, off_bytes // 4: off_bytes // 4 + n]
        if len(shape) == 3:
            v = v.rearrange("p (a b) -> p a b", a=shape[1])
        return v

    ACTT = carve(0, [128, NFC, T], BF16)
    H = carve(45056, [128, 4, D], F32)
    SILU = [carve(77824, [128, T], F32), carve(79872, [128, T], F32)]
    MIXT = carve(0, [128, NKC, T], BF16)
    UT = carve(16384, [128, NH, T], BF16)
    VSG = carve(24576, [128, 4, 1024], F32)
    VLN = carve(40960, [128, 4, 1024], BF16)
    SGT = [carve(49152, [128, T], F32), carve(51200, [128, T], F32)]
    GT = carve(16384, [128, NH, T], BF16)
    QT = carve(24576, [128, NH, T], BF16)
    KT = carve(32768, [128, NH, T], BF16)
    KZ = carve(40960, [128, 4, 1024], BF16)
    V = carve(49152, [128, 4, 1024], BF16)
    RA = [carve(57344, [128, T], F32), carve(61440, [128, T], F32)]
    RB = [carve(59392, [128, T], F32), carve(63488, [128, T], F32)]
    RETSB = carve(65536, [128, NH, 128], F32)
    RETSQ = carve(73728, [128, NH, 128], F32)
    RS4 = carve(57344, [128, 4 * NH, 128], F32)
    RETN = carve(73728, [128, NH, 128], F32)
    PT = carve(77824, [128, 2, T], BF16)
    TMPG = RETSQ
    POSI = carve(57344, [128, T], I32)
    ANG = carve(59392, [128, T], F32)
    TMPA = carve(61440, [128, T], F32)
    TMPB = carve(63488, [128, T], F32)

    PS = [ES.enter_context(nc.psum_tensor("PS%d" % i, [128, 512], F32)) for i in range(8)]

    sc = Sched()
    A = sc.add

    psrr = [0]

    def ps_next():
        b = psrr[0] % 8
        psrr[0] += 1
        return b

    def dma(eng, out, in_, sem, reads=(), writes=()):
        return A(eng, lambda e, o=out, i=in_: e.dma_start(out=o, in_=i), reads=reads, writes=writes, dma_sem=sem)

    def cast(stage, dst, src, last):
        A("pool", lambda e, o=dst, i=src: e.dma_start(out=o, in_=i), reads=(),
          writes=([("wstage", stage)] if last else ()), dma_sem="cast_" + stage)

    def st_gu(f, b):
        return "g%d_%d" % (f + 1, b // 4) if f == 0 else "g2"

    def st_d(f, cb):
        return "d1_%d" % cb if f == 0 else "d2"

    WIN_CAST_ORDER = [4, 5, 6, 7, 8, 9, 10, 11] + [b for b in range(24) if not (4 <= b <= 11)]

    def st_in(b):
        return "win_0" if 4 <= b <= 11 else "win_1"

    def cast_gu(f):
        for b in range(22):
            for W, Sx, lastw in ((WG[f], SG[f], False), (WU[f], SU[f], True)):
                src = W[:, b * 256:(b + 1) * 256].rearrange("(kc p) c -> p kc c", p=128)
                last = lastw and (b == 21 or (f == 0 and b % 4 == 3))
                cast(st_gu(f, b), Sx[b], src, last)

    def cast_down(f):
        for cb in range(4):
            for fb in range(6):
                f0 = fb * 8
                n = min(8, NFC - f0)
                src = WD[f][f0 * 128:(f0 + n) * 128, cb * 512:(cb + 1) * 512].rearrange("(fc p) c -> p fc c", p=128)
                cast(st_d(f, cb), SD[f][cb][:, f0:f0 + n, :], src, fb == 5 and (f == 0 or cb == 3))

    cast_gu(0)
    cast_down(0)
    for i_, b in enumerate(WIN_CAST_ORDER):
        src = WIN[:, b * 256:(b + 1) * 256].rearrange("(kc p) c -> p kc c", p=128)
        cast(st_in(b), SIN[b], src, i_ == 7 or i_ == 23)
    for b in range(8):
        src = WOUT[:, b * 256:(b + 1) * 256].rearrange("(kc p) c -> p kc c", p=128)
        cast("wout", SOUT[b], src, b == 7)
    cast_gu(1)
    cast_down(1)

    cl = []
    def cload(out, in_, name):
        dma("sp", out, in_, "const", writes=[name])
        cl.append(name)
    cload(IDB[:], C_IDB[:], "IDB")
    cload(IDF[:], C_IDF[:], "IDF")
    cload(MASK[:], C_MASK[:], "MASK")
    cload(XIT[:].rearrange("p a b -> p (a b)"), C_XI.partition_broadcast(128), "XIT")
    cload(GINVT[:].rearrange("p a b -> p (a b)"), C_GINV.partition_broadcast(128), "GINVT")
    cload(BST[:].rearrange("p a b -> p (a b)"), SGU_B.partition_broadcast(128), "BST")
    cload(SGAIN[:], G_SGU.partition_broadcast(128), "SGAIN")
    cload(INVF[:], C_INVF[:], "INVF")
    cload(SFL[:], SFLAG[:], "SFL")
    cload(RETG[:], G_RET[:], "RETG")
    for k_ in ("f1pre", "mpre", "f2pre"):
        cload(GPRE[k_][:], GTP[k_][:], "GPRE")
    last_const = sc.ops[-1].id
    for n in cl:
        sc.lw[n] = last_const

    A("dve", lambda e: e.memset(EPST[:], EPS), writes=["EPST"])
    A("dve", lambda e: e.memset(EPS4[:], 4.0 * EPS), writes=["EPS4"])
    A("dve", lambda e: e.memset(S[:].rearrange("p a b -> p (a b)"), 0.0), writes=["S"])
    A("dve", lambda e: e.memset(SBF[:].rearrange("p a b -> p (a b)"), 0.0), writes=["SBF"])

    for g in range(NH):
        wtmp = TMPA if g % 2 == 0 else TMPB
        nm = ("TMPW", g % 2)
        dma("sp", wtmp[:, 0:128], SGU_W[g], "misc%d" % (g % 2), writes=[nm])
        b = ps_next()
        A("pe", lambda e, b=b, w=wtmp: e.transpose(out=PS[b][:, 0:128], in_=w[:, 0:128], identity=IDF[:]),
          reads=[nm, "IDF"], writes=[("ps", b)])
        A("dve", lambda e, b=b, g=g: e.tensor_tensor(out=WTS[:, g, :], in0=PS[b][:, 0:128], in1=MASK[:], op=ALU.mult),
          reads=[("ps", b), "MASK"], writes=[("WTS", g)])

    wq = []

    def seq_ffn(f):
        for b in range(22):
            wq.append(("g", f, b))
            wq.append(("u", f, b))
        for cb in range(4):
            for fb in range(6):
                wq.append(("d", f, cb, fb))

    for g in range(NP):
        seq_ffn(0)
        for b in (4, 5, 6, 7, 8, 9, 10, 11):
            wq.append(("in", b))
    MIX_ORDER = [20, 21, 22, 23, 16, 17, 18, 19, 12, 13, 14, 15, 8, 9, 10, 11, 4, 5, 6, 7, 0, 1, 2, 3]
    for g in range(NM):
        seq_ffn(0)
        for b in MIX_ORDER:
            wq.append(("in", b))
        for b in range(8):
            wq.append(("out", b))
        seq_ffn(1)
    wstate = {"issued": 0, "taken": 0}

    def w_issue(i):
        key = wq[i]
        slot = i % NSLOT
        if key[0] in ("g", "u"):
            f, b = key[1], key[2]
            src = (SG if key[0] == "g" else SU)[f][b].rearrange("p a b -> p (a b)")
            dst = WS[slot][:, 0:4096]
            stage = st_gu(f, b)
        elif key[0] == "d":
            f, cb, fb = key[1], key[2], key[3]
            f0 = fb * 8
            n = min(8, NFC - f0)
            src = SD[f][cb][:, f0:f0 + n, :].rearrange("p a b -> p (a b)")
            dst = WS[slot][:, 0:n * 512]
            stage = st_d(f, cb)
        elif key[0] == "in":
            src = SIN[key[1]].rearrange("p a b -> p (a b)")
            dst = WS[slot][:, 0:4096]
            stage = st_in(key[1])
        else:
            src = SOUT[key[1]].rearrange("p a b -> p (a b)")
            dst = WS[slot][:, 0:4096]
            stage = "wout"
        dma("sp", dst, src, "ws%d" % slot, reads=[("wstage", stage)], writes=[("ws", slot)])

    def w_take(key, hold=0):
        while wstate["issued"] < min(len(wq), wstate["taken"] - hold + NSLOT):
            w_issue(wstate["issued"])
            wstate["issued"] += 1
        i = wstate["taken"]
        assert wq[i] == key, (wq[i], key)
        wstate["taken"] += 1
        return i % NSLOT

    def rstd_small(dst, src, n, scale, nm):
        A("act", lambda e: e.activation(out=dst, in_=src, func=AF.Sqrt, bias=EPST[:, 0:1], scale=scale),
          reads=[nm, "EPST"], writes=[nm])
        A("dve", lambda e: e.reciprocal(out=dst, in_=dst), reads=[nm], writes=[nm])

    def load_gain(key):
        dma("act", GS[:], GV[key].partition_broadcast(128), "gs", writes=["GS"])

    def S2(tt):
        xb = XNTOK[tt % 2]
        xn = ("XNTOK", tt % 2)
        A("act", lambda e: e.activation(out=xb[:], in_=Xs[:, tt, :], func=AF.Square, accum_out=ST[:, tt:tt + 1]),
          reads=[("X", tt)], writes=[xn, ("s2", tt)])
        A("act", lambda e: e.activation(out=ST[:, 8 + tt:9 + tt], in_=ST[:, tt:tt + 1], func=AF.Sqrt, bias=EPST[:, 0:1], scale=1.0 / D),
          reads=[("s2", tt), "EPST"], writes=[("r2", tt)])
        A("dve", lambda e: e.reciprocal(out=ST[:, 8 + tt:9 + tt], in_=ST[:, 8 + tt:9 + tt]), reads=[("r2", tt)], writes=[("r2", tt)])
        A("act", lambda e: e.activation(out=xb[:], in_=Xs[:, tt, :], func=AF.Copy, scale=ST[:, 8 + tt:9 + tt]),
          reads=[("X", tt), ("r2", tt)], writes=[xn])

    def S3(tt, gkey):
        xb = XNTOK[tt % 2]
        xn = ("XNTOK", tt % 2)
        gp = GPRE[gkey]
        for half in range(2):
            b = ps_next()
            pb = PS[b][:].bitcast(BF16)
            for k8 in range(8):
                kc = half * 8 + k8
                A("pe", lambda e, pb=pb, k8=k8, kc=kc: e.transpose(out=pb[:, k8 * 128:(k8 + 1) * 128],
                                                                  in_=xb[:, kc * 128:(kc + 1) * 128], identity=IDB[:]),
                  reads=[xn, "IDB"], writes=[("ps", b)])
            A("dve", lambda e, pb=pb, half=half: e.tensor_tensor(
                out=XNT[:, half * 8:(half + 1) * 8, tt * 128:(tt + 1) * 128],
                in0=pb.rearrange("p (a b) -> p a b", a=8),
                in1=gp[:, half * 8:(half + 1) * 8].unsqueeze(2).to_broadcast([128, 8, 128]), op=ALU.mult),
              reads=[("ps", b), "GPRE"], writes=[("XNT", tt)])

    XNT_ALL = [("XNT", tt) for tt in range(4)]

    def S1(tt, nparts, mul, to_h):
        A("dve", lambda e: e.reduce_sum(out=ST[:, 48 + tt:49 + tt], in_=ST[:, 16 + tt * 8:16 + tt * 8 + nparts], axis=AX.X),
          reads=[("ssp", tt)], writes=[("r1", tt)])
        bias = EPST if mul == 1.0 else EPS4
        assert mul in (1.0, 0.5)
        A("act", lambda e: e.activation(out=ST[:, 52 + tt:53 + tt], in_=ST[:, 48 + tt:49 + tt], func=AF.Sqrt, bias=bias[:, 0:1],
                                        scale=1.0 / (D * mul * mul)), reads=[("r1", tt), "EPST", "EPS4"], writes=[("r1b", tt)])
        A("dve", lambda e: e.reciprocal(out=ST[:, 52 + tt:53 + tt], in_=ST[:, 52 + tt:53 + tt]), reads=[("r1b", tt)], writes=[("r1b", tt)])
        if to_h:
            A("dve", lambda e: e.scalar_tensor_tensor(out=H[:, tt, :], in0=H[:, tt, :], scalar=ST[:, 52 + tt:53 + tt],
                                                      in1=Xs[:, tt, :], op0=ALU.mult, op1=ALU.add),
              reads=[("H", tt), ("r1b", tt), ("X", tt)], writes=[("H", tt)])
        else:
            A("dve", lambda e: e.scalar_tensor_tensor(out=Xs[:, tt, :], in0=H[:, tt, :], scalar=ST[:, 52 + tt:53 + tt],
                                                      in1=Xs[:, tt, :], op0=ALU.mult, op1=ALU.add),
              reads=[("H", tt), ("r1b", tt), ("X", tt)], writes=[("X", tt)])

    def pipelined(stages):
        n = len(stages)
        if not PIPE_NORMS:
            for tt in range(4):
                for st_ in stages:
                    st_(tt)
            return
        for step in range(4 + n - 1):
            for si in range(n):
                tt = step - si
                if 0 <= tt < 4:
                    stages[si](tt)

    def prenorm_all(gkey):
        pipelined([S2, lambda tt: S3(tt, gkey)])

    def boundary_all(nparts, mul, next_gkey, after_s1=None, to_h=False):
        def s1(tt):
            S1(tt, nparts, mul, to_h)
            if after_s1 is not None:
                after_s1(tt)
        st = [s1]
        if next_gkey is not None:
            st += [S2, lambda tt: S3(tt, next_gkey)]
        pipelined(st)

    FFN_NAMES = [("ACTT", j) for j in range(NFC)] + [("H", tt) for tt in range(4)] + [("SILU", i) for i in range(2)]
    MIX_NAMES = ([("MIXT", c) for c in range(4)] + ["UT", "VSG", "VLN", "SGT0", "SGT1", "GT", "QT", "KZ", "V",
                 "RA0", "RA1", "RB0", "RB1", "RETSB", "RETSQ", "RETN", "PT", "TMPG", "POSI", "ANG", "TMPA", "TMPB",
                 ("TMPW", 0), ("TMPW", 1)] + [("MIXS", g) for g in range(8)] + [("H", tt) for tt in range(4)] + [("KT", h_) for h_ in range(NH)]
                 + [("PT", 0), ("PT", 1)] + [("VSG", t_) for t_ in range(4)] + [("RS4", c_) for c_ in range(4)])

    def ffn(f, post, next_gkey, after_s1=None, to_h=False):
        sc.alias(FFN_NAMES, MIX_NAMES)
        load_gain(post)
        for b in range(22):
            sg = w_take(("g", f, b))
            su = w_take(("u", f, b), hold=1)
            gv = WS[sg][:, 0:4096].rearrange("p (a b) -> p a b", a=16)
            uv = WS[su][:, 0:4096].rearrange("p (a b) -> p a b", a=16)
            for jj in range(2):
                j = 2 * b + jj
                pg = ps_next()
                pu = ps_next()
                for kc in range(NKC):
                    A("pe", lambda e, pg=pg, gv=gv, kc=kc, jj=jj: e.matmul(PS[pg][:], lhsT=gv[:, kc, jj * 128:(jj + 1) * 128],
                                                                         rhs=XNT[:, kc, :], start=(kc == 0), stop=(kc == NKC - 1)),
                      reads=[("ws", sg)] + XNT_ALL, writes=[("ps", pg)])
                for kc in range(NKC):
                    A("pe", lambda e, pu=pu, uv=uv, kc=kc, jj=jj: e.matmul(PS[pu][:], lhsT=uv[:, kc, jj * 128:(jj + 1) * 128],
                                                                         rhs=XNT[:, kc, :], start=(kc == 0), stop=(kc == NKC - 1)),
                      reads=[("ws", su)] + XNT_ALL, writes=[("ps", pu)])
                si = j % 2
                A("act", lambda e, pg=pg, si=si: e.activation(out=SILU[si][:], in_=PS[pg][:], func=AF.Silu),
                  reads=[("ps", pg)], writes=[("SILU", si)])
                A("dve", lambda e, pu=pu, si=si, j=j: e.tensor_tensor(out=ACTT[:, j, :], in0=PS[pu][:], in1=SILU[si][:], op=ALU.mult),
                  reads=[("ps", pu), ("SILU", si)], writes=[("ACTT", j)])
        for cb in range(4):
            banks = [ps_next() for _ in range(4)]
            for fb in range(6):
                sl = w_take(("d", f, cb, fb))
                f0 = fb * 8
                n = min(8, NFC - f0)
                dv = WS[sl][:, 0:n * 512].rearrange("p (a b) -> p a b", a=n)
                for tt in range(4):
                    for fl in range(n):
                        fc = f0 + fl
                        A("pe", lambda e, bk=banks[tt], dv=dv, fl=fl, fc=fc, tt=tt: e.matmul(
                            PS[bk][:], lhsT=ACTT[:, fc, tt * 128:(tt + 1) * 128], rhs=dv[:, fl, :],
                            start=(fc == 0), stop=(fc == NFC - 1)),
                          reads=[("ws", sl), ("ACTT", fc)], writes=[("ps", banks[tt])])
            for tt in range(4):
                bk = banks[tt]
                A("dve", lambda e, bk=bk, tt=tt, cb=cb: e.tensor_tensor(out=H[:, tt, cb * 512:(cb + 1) * 512], in0=PS[bk][:],
                                                                        in1=GS[:, cb * 512:(cb + 1) * 512], op=ALU.mult),
                  reads=[("ps", bk), "GS"], writes=[("H", tt)])
                A("act", lambda e, bk=bk, tt=tt, cb=cb: e.activation(out=SILU[tt % 2][:], in_=PS[bk][:], func=AF.Square,
                                                                     accum_out=ST[:, 16 + tt * 8 + cb:17 + tt * 8 + cb]),
                  reads=[("ps", bk)], writes=[("SILU", tt % 2), ("ssp", tt)])
        boundary_all(4, 0.5, next_gkey, after_s1, to_h)

    def pos_tables(POS, g):
        dma("act", POSI[:], POS[:, g * T:(g + 1) * T].partition_broadcast(128), "pos", writes=["POSI"])
        A("dve", lambda e: e.tensor_copy(out=ANG[:], in_=POSI[:]), reads=["POSI"], writes=["ANG"])
        A("dve", lambda e: e.tensor_scalar(out=ANG[:], in0=ANG[:], scalar1=INVF[:, 0:1], scalar2=None, op0=ALU.mult),
          reads=["ANG", "INVF"], writes=["ANG"])
        for which in range(2):
            dst = SSG if which == 0 else COS
            shift = 0.0 if which == 0 else float(np.pi / 2)
            A("dve", lambda e, shift=shift: e.tensor_scalar(out=TMPA[:], in0=ANG[:], scalar1=shift, scalar2=None, op0=ALU.add),
              reads=["ANG"], writes=["TMPA"])
            A("dve", lambda e: e.tensor_scalar(out=TMPB[:], in0=TMPA[:], scalar1=float(1.0 / TWO_PI), scalar2=0.5,
                                               op0=ALU.mult, op1=ALU.add), reads=["TMPA"], writes=["TMPB"])
            A("dve", lambda e: e.tensor_copy(out=POSI[:], in_=TMPB[:]), reads=["TMPB"], writes=["POSI"])
            A("dve", lambda e: e.tensor_copy(out=TMPB[:], in_=POSI[:]), reads=["POSI"], writes=["TMPB"])
            A("dve", lambda e: e.scalar_tensor_tensor(out=TMPA[:], in0=TMPB[:], scalar=-TWO_PI, in1=TMPA[:],
                                                      op0=ALU.mult, op1=ALU.add), reads=["TMPB", "TMPA"], writes=["TMPA"])
            A("dve", lambda e: e.tensor_scalar(out=TMPB[:], in0=TMPA[:], scalar1=-float(np.pi), scalar2=TWO_PI,
                                               op0=ALU.is_lt, op1=ALU.mult), reads=["TMPA"], writes=["TMPB"])
            A("dve", lambda e: e.tensor_tensor(out=TMPA[:], in0=TMPA[:], in1=TMPB[:], op=ALU.add),
              reads=["TMPA", "TMPB"], writes=["TMPA"])
            A("act", lambda e, dst=dst: e.activation(out=dst[:], in_=TMPA[:], func=AF.Sin), reads=["TMPA"],
              writes=["SSG" if which == 0 else "COS"])
        A("dve", lambda e: e.tensor_scalar(out=SSG[0:64, :], in0=SSG[0:64, :], scalar1=-1.0, scalar2=None, op0=ALU.mult),
          reads=["SSG"], writes=["SSG"])

    def proj_fm(blk, handler):
        sl = w_take(("in", blk))
        wv = WS[sl][:, 0:4096].rearrange("p (a b) -> p a b", a=16)
        for cc in range(2):
            b = ps_next()
            for kc in range(NKC):
                A("pe", lambda e, b=b, wv=wv, kc=kc, cc=cc: e.matmul(PS[b][:], lhsT=wv[:, kc, cc * 128:(cc + 1) * 128],
                                                                  rhs=XNT[:, kc, :], start=(kc == 0), stop=(kc == NKC - 1)),
                  reads=[("ws", sl)] + XNT_ALL, writes=[("ps", b)])
            handler(b, cc)

    def proj_tm(blk, handler):
        sl = w_take(("in", blk))
        wv = WS[sl][:, 0:4096].rearrange("p (a b) -> p a b", a=16)
        for tt in range(4):
            b = ps_next()
            for kc in range(NKC):
                A("pe", lambda e, b=b, wv=wv, kc=kc, tt=tt: e.matmul(PS[b][:, 0:256], lhsT=XNT[:, kc, tt * 128:(tt + 1) * 128],
                                                                  rhs=wv[:, kc, :], start=(kc == 0), stop=(kc == NKC - 1)),
                  reads=[("ws", sl), ("XNT", tt)], writes=[("ps", b)])
            handler(b, tt)

    def rotary(b, h, dstT, dname, decT):
        i = h % 2
        A("dve", lambda e: e.tensor_tensor(out=RA[i][:], in0=PS[b][:], in1=COS[:], op=ALU.mult),
          reads=[("ps", b), "COS"], writes=["RA%d" % i])
        A("dve", lambda e: e.tensor_tensor(out=RB[i][0:64, :], in0=PS[b][64:128, :], in1=SSG[0:64, :], op=ALU.mult),
          reads=[("ps", b), "SSG"], writes=["RB%d" % i])
        A("dve", lambda e: e.tensor_tensor(out=RB[i][64:128, :], in0=PS[b][0:64, :], in1=SSG[64:128, :], op=ALU.mult),
          reads=[("ps", b), "SSG"], writes=["RB%d" % i])
        A("dve", lambda e: e.tensor_tensor(out=RA[i][:], in0=RA[i][:], in1=RB[i][:], op=ALU.add),
          reads=["RA%d" % i, "RB%d" % i], writes=["RA%d" % i])
        A("dve", lambda e: e.tensor_tensor(out=dstT[:, h, :].rearrange("p (a b) -> p a b", a=4),
                                           in0=RA[i][:].rearrange("p (a b) -> p a b", a=4),
                                           in1=decT[:, h, :].unsqueeze(1).to_broadcast([128, 4, 128]), op=ALU.mult),
          reads=["RA%d" % i, "XIT", "GINVT"], writes=[dname])

    kpend = []

    def k_flush():
        while kpend:
            h = kpend.pop(0)
            b2 = ps_next()
            pb = PS[b2][:].bitcast(BF16)
            for c in range(4):
                A("pe", lambda e, pb=pb, c=c, h=h: e.transpose(out=pb[:, c * 128:(c + 1) * 128], in_=KT[:, h, c * 128:(c + 1) * 128],
                                                            identity=IDB[:]), reads=[("KT", h), "IDB"], writes=[("ps", b2)])
            A("act", lambda e, pb=pb, h=h: e.activation(out=KZ[:, :, h * 128:(h + 1) * 128],
                                                       in_=pb[:, 0:512].rearrange("p (a b) -> p a b", a=4),
                                                       func=AF.Copy, scale=cdv[h]),
              reads=[("ps", b2)], writes=["KZ"])

    def k_block(blk):
        sl = w_take(("in", blk))
        wv = WS[sl][:, 0:4096].rearrange("p (a b) -> p a b", a=16)
        for cc in range(2):
            h = (blk - 4) * 2 + cc
            b = ps_next()
            for kc in range(NKC):
                A("pe", lambda e, b=b, wv=wv, kc=kc, cc=cc: e.matmul(PS[b][:], lhsT=wv[:, kc, cc * 128:(cc + 1) * 128],
                                                                  rhs=XNT[:, kc, :], start=(kc == 0), stop=(kc == NKC - 1)),
                  reads=[("ws", sl)] + XNT_ALL, writes=[("ps", b)])
            k_flush()
            rotary(b, h, KT, ("KT", h), GINVT)
            kpend.append(h)

    def v_block(blk):
        def hv(b, tt):
            c0 = (blk - 8) * 256
            A("act", lambda e, b=b, tt=tt, c0=c0: e.activation(out=V[:, tt, c0:c0 + 256], in_=PS[b][:, 0:256], func=AF.Copy),
              reads=[("ps", b)], writes=["V"])
        proj_tm(blk, hv)

    def kv_update(c):
        for hb in range(2):
            b = ps_next()
            for hh in range(4):
                h = hb * 4 + hh
                A("pe", lambda e, b=b, hh=hh, h=h, c=c: e.matmul(PS[b][:, hh * 128:(hh + 1) * 128], lhsT=KZ[:, c, h * 128:(h + 1) * 128],
                                                              rhs=V[:, c, h * 128:(h + 1) * 128], start=True, stop=True),
                  reads=["KZ", "V"], writes=[("ps", b)])
            for hh in range(4):
                h = hb * 4 + hh
                A("dve", lambda e, b=b, hh=hh, h=h: e.scalar_tensor_tensor(out=S[:, h, :], in0=S[:, h, :], scalar=cdv[h],
                                                                        in1=PS[b][:, hh * 128:(hh + 1) * 128], op0=ALU.mult, op1=ALU.add),
                  reads=["S", ("ps", b)], writes=["S"])
        A("act", lambda e: e.activation(out=SBF[:].rearrange("p a b -> p (a b)"), in_=S[:].rearrange("p a b -> p (a b)"), func=AF.Copy),
          reads=["S"], writes=["SBF"])

    def load_x_tt(XD, g, tt, eng="sp"):
        dma("sp", Xs[:, tt, :], XD[g * T + tt * 128: g * T + (tt + 1) * 128, :], "xi%d" % tt, writes=[("X", tt)])

    def start_group(XD, g, eng="pool"):
        for tt in range(4):
            load_x_tt(XD, g, tt, eng)
        prenorm_all("f1pre")

    def prefix_group(g, nxt):
        ffn(0, "f1post", "mpre")
        sc.alias(MIX_NAMES, FFN_NAMES)
        for tt in range(4):
            load_x_tt(nxt[0], nxt[1], tt)
        pos_tables(POSP, g)
        for blk in (4, 5, 6, 7):
            k_block(blk)
        k_flush()
        for blk in (8, 9, 10, 11):
            v_block(blk)
        prenorm_all("f1pre")
        for c in range(4):
            kv_update(c)

    def store_y_tt(g, tt):
        dma("pool", Y[g * T + tt * 128: g * T + (tt + 1) * 128, :], Xs[:, tt, :], "st%d" % tt, reads=[("X", tt)])

    def store_y(g):
        for tt in range(4):
            store_y_tt(g, tt)

    def main_group(g, nxt):
        ffn(0, "f1post", None if DBG == "ffn1" else "mpre")
        if DBG == "ffn1":
            return store_y(g)
        sc.alias(MIX_NAMES, FFN_NAMES)
        load_gain("mpost")
        pos_tables(POSM, g)
        for blk in (20, 21, 22, 23):
            def hvs(b, tt, blk=blk):
                c0 = (blk - 20) * 256
                A("act", lambda e, b=b, tt=tt, c0=c0: e.activation(out=VSG[:, tt, c0:c0 + 256], in_=PS[b][:, 0:256],
                                                                   func=AF.Gelu_apprx_tanh), reads=[("ps", b)], writes=[("VSG", tt)])
            proj_tm(blk, hvs)
        for tt in range(4):
            A("dve", lambda e, tt=tt: e.reduce_sum(out=ST[:, 56:57], in_=VSG[:, tt, :], axis=AX.X),
              reads=[("VSG", tt)], writes=["lnv"])
            A("act", lambda e, tt=tt: e.activation(out=VLN[:, tt, :], in_=VSG[:, tt, :], func=AF.Square, accum_out=ST[:, 57:58]),
              reads=[("VSG", tt)], writes=["VLN", "lnv"])
            A("dve", lambda e: e.tensor_scalar(out=ST[:, 58:59], in0=ST[:, 56:57], scalar1=1.0 / 1024, scalar2=None, op0=ALU.mult),
              reads=["lnv"], writes=["lnv"])
            A("dve", lambda e: e.tensor_tensor(out=ST[:, 59:60], in0=ST[:, 58:59], in1=ST[:, 58:59], op=ALU.mult),
              reads=["lnv"], writes=["lnv"])
            A("dve", lambda e: e.scalar_tensor_tensor(out=ST[:, 59:60], in0=ST[:, 57:58], scalar=1.0 / 1024, in1=ST[:, 59:60],
                                                      op0=ALU.mult, op1=ALU.subtract), reads=["lnv"], writes=["lnv"])
            rstd_small(ST[:, 60:61], ST[:, 59:60], 1, 1.0, "lnv")
            A("dve", lambda e: e.scalar_tensor_tensor(out=ST[:, 61:62], in0=ST[:, 58:59], scalar=-1.0, in1=ST[:, 60:61],
                                                      op0=ALU.mult, op1=ALU.mult), reads=["lnv"], writes=["lnv"])
            A("dve", lambda e, tt=tt: e.tensor_scalar(out=VSG[:, tt, :], in0=VSG[:, tt, :], scalar1=ST[:, 60:61], scalar2=ST[:, 61:62],
                                                      op0=ALU.mult, op1=ALU.add), reads=[("VSG", tt), "lnv"], writes=[("VSG", tt)])
            A("dve", lambda e, tt=tt: e.tensor_tensor(out=VLN[:, tt, :], in0=VSG[:, tt, :], in1=SGAIN[:], op=ALU.mult),
              reads=[("VSG", tt), "SGAIN"], writes=["VLN"])
        for blk in (16, 17, 18, 19):
            def hu(b, cc, blk=blk):
                gi = (blk - 16) * 2 + cc
                A("act", lambda e, b=b, gi=gi: e.activation(out=UT[:, gi, :], in_=PS[b][:], func=AF.Gelu_apprx_tanh),
                  reads=[("ps", b)], writes=["UT"])
            proj_fm(blk, hu)
        for gi in range(NH):
            b = ps_next()
            for c in range(4):
                A("pe", lambda e, b=b, c=c, gi=gi: e.matmul(PS[b][:, c * 128:(c + 1) * 128], lhsT=VLN[:, c, gi * 128:(gi + 1) * 128],
                                                          rhs=WTS[:, gi, :], start=True, stop=True),
                  reads=["VLN", ("WTS", gi)], writes=[("ps", b)])
            i = gi % 2
            A("dve", lambda e, b=b, gi=gi, i=i: e.tensor_tensor(out=SGT[i][:].rearrange("p (a b) -> p a b", a=4),
                                                              in0=PS[b][:].rearrange("p (a b) -> p a b", a=4),
                                                              in1=BST[:, gi, :].unsqueeze(1).to_broadcast([128, 4, 128]), op=ALU.add),
              reads=[("ps", b), "BST"], writes=["SGT%d" % i])
            A("dve", lambda e, gi=gi, i=i: e.tensor_tensor(out=MIXT[:, 8 + gi, :], in0=SGT[i][:], in1=UT[:, gi, :], op=ALU.mult),
              reads=["SGT%d" % i, "UT"], writes=[("MIXS", gi)])
        sc.alias(["GT", "QT", "KZ", "V"] + [("KT", h_) for h_ in range(NH)], ["UT", "VSG", "VLN", "SGT0", "SGT1"] + [("VSG", t) for t in range(4)])
        for blk in (12, 13, 14, 15):
            def hg(b, cc, blk=blk):
                h = (blk - 12) * 2 + cc
                A("act", lambda e, b=b, h=h: e.activation(out=GT[:, h, :], in_=PS[b][:], func=AF.Silu),
                  reads=[("ps", b)], writes=["GT"])
            proj_fm(blk, hg)
        for blk in (8, 9, 10, 11):
            v_block(blk)
        for blk in (4, 5, 6, 7):
            k_block(blk)
        k_flush()
        for blk in (0, 1, 2, 3):
            def hq(b, cc, blk=blk):
                h = blk * 2 + cc
                rotary(b, h, QT, "QT", XIT)
            proj_fm(blk, hq)
        sc.alias([("RS4", c_) for c_ in range(4)] + ["RETSQ"], ["RA0", "RA1", "RB0", "RB1", "POSI", "ANG", "TMPA", "TMPB", "RETSB", "RETN"])
        for c in range(4):
            cs = slice(c * 128, (c + 1) * 128)
            rb = []
            for hb in range(2):
                b = ps_next()
                for hh in range(4):
                    h = hb * 4 + hh
                    A("pe", lambda e, b=b, hh=hh, h=h, cs=cs: e.matmul(PS[b][:, hh * 128:(hh + 1) * 128], lhsT=KT[:, h, cs], rhs=QT[:, h, cs],
                                                                    start=True, stop=True),
                      reads=[("KT", h), "QT"], writes=[("ps", b)])
                A("dve", lambda e, b=b, hb=hb: e.tensor_tensor(out=PT[:, hb, :].rearrange("p (a b) -> p a b", a=4),
                                                             in0=PS[b][:].rearrange("p (a b) -> p a b", a=4),
                                                             in1=MASK[:].unsqueeze(1).to_broadcast([128, 4, 128]), op=ALU.mult),
                  reads=[("ps", b), "MASK"], writes=[("PT", hb)])
            for hb in range(2):
                b = ps_next()
                rb.append(b)
                for hh in range(4):
                    h = hb * 4 + hh
                    A("pe", lambda e, b=b, hh=hh, h=h, hb=hb, c=c: e.matmul(PS[b][:, hh * 128:(hh + 1) * 128], lhsT=PT[:, hb, hh * 128:(hh + 1) * 128],
                                                                         rhs=V[:, c, h * 128:(h + 1) * 128], start=True, stop=False),
                      reads=[("PT", hb), "V"], writes=[("ps", b)])
                    A("pe", lambda e, b=b, hh=hh, h=h, cs=cs: e.matmul(PS[b][:, hh * 128:(hh + 1) * 128], lhsT=QT[:, h, cs],
                                                                    rhs=SBF[:, h, :], start=False, stop=True),
                      reads=["QT", "SBF"], writes=[("ps", b)])
            kv_update(c)
            for hb in range(2):
                b = rb[hb]
                A("act", lambda e, b=b, hb=hb, c=c: e.activation(out=RS4[:, c * 8 + hb * 4:c * 8 + (hb + 1) * 4, :].rearrange("p a b -> p (a b)"),
                                                               in_=PS[b][:], func=AF.Copy), reads=[("ps", b)], writes=[("RS4", c)])
                A("act", lambda e, b=b, hb=hb: e.activation(out=RETSQ[:, hb * 4:(hb + 1) * 4, :].rearrange("p a b -> p (a b)"),
                                                          in_=PS[b][:], func=AF.Square), reads=[("ps", b)], writes=["RETSQ"])
            A("dve", lambda e, c=c: e.reduce_sum(out=ST[:, 96 + c * 8:104 + c * 8], in_=RETSQ[:], axis=AX.X),
              reads=["RETSQ"], writes=["lnr"])
        RS4_ALL = [("RS4", c_) for c_ in range(4)]
        A("dve", lambda e: e.reduce_sum(out=ST[:, 64:96], in_=RS4[:], axis=AX.X), reads=RS4_ALL, writes=["lnr"])
        A("dve", lambda e: e.tensor_scalar(out=ST[:, 64:96], in0=ST[:, 64:96], scalar1=1.0 / 128, scalar2=None, op0=ALU.mult),
          reads=["lnr"], writes=["lnr"])
        A("dve", lambda e: e.tensor_tensor(out=ST[:, 128:160], in0=ST[:, 64:96], in1=ST[:, 64:96], op=ALU.mult),
          reads=["lnr"], writes=["lnr"])
        A("dve", lambda e: e.scalar_tensor_tensor(out=ST[:, 96:128], in0=ST[:, 96:128], scalar=1.0 / 128, in1=ST[:, 128:160],
                                                  op0=ALU.mult, op1=ALU.subtract), reads=["lnr"], writes=["lnr"])
        rstd_small(ST[:, 96:128], ST[:, 96:128], 32, 1.0, "lnr")
        for c in range(4):
            cs = slice(c * 128, (c + 1) * 128)
            A("dve", lambda e, c=c: e.tensor_tensor(out=RS4[:, c * 8:(c + 1) * 8, :], in0=RS4[:, c * 8:(c + 1) * 8, :],
                                                    in1=ST[:, 64 + c * 8:72 + c * 8].unsqueeze(2).to_broadcast([128, 8, 128]),
                                                    op=ALU.subtract), reads=[("RS4", c), "lnr"], writes=[("RS4", c)])
            A("dve", lambda e, c=c: e.tensor_tensor(out=RS4[:, c * 8:(c + 1) * 8, :], in0=RS4[:, c * 8:(c + 1) * 8, :],
                                                    in1=ST[:, 96 + c * 8:104 + c * 8].unsqueeze(2).to_broadcast([128, 8, 128]),
                                                    op=ALU.mult), reads=[("RS4", c), "lnr"], writes=[("RS4", c)])
            for hb in range(2):
                b = ps_next()
                for hh in range(4):
                    h = hb * 4 + hh
                    A("pe", lambda e, b=b, hh=hh, h=h, c=c: e.transpose(out=PS[b][:, hh * 128:(hh + 1) * 128], in_=RS4[:, c * 8 + h, :],
                                                                     identity=IDF[:]),
                      reads=[("RS4", c), "IDF"], writes=[("ps", b)])
                A("dve", lambda e, b=b, hb=hb: e.tensor_tensor(out=TMPG[:, hb * 4:(hb + 1) * 4, :],
                                                             in0=PS[b][:].rearrange("p (a b) -> p a b", a=4),
                                                             in1=RETG[:, hb * 4:(hb + 1) * 4].unsqueeze(2).to_broadcast([128, 4, 128]),
                                                             op=ALU.mult), reads=[("ps", b), "RETG"], writes=["RETSQ"])
                A("dve", lambda e, hb=hb, cs=cs: e.tensor_tensor(out=MIXT[:, hb * 4:(hb + 1) * 4, cs], in0=TMPG[:, hb * 4:(hb + 1) * 4, :],
                                                               in1=GT[:, hb * 4:(hb + 1) * 4, cs], op=ALU.mult),
                  reads=["RETSQ", "GT"], writes=[("MIXT", c)])
        sc.alias([("H", tt) for tt in range(4)], ["KZ", "V", "RA0", "RA1", "RB0", "RB1", "RETSB", "RETSQ", "RETN", "QT"] + [("KT", h_) for h_ in range(NH)] + [("RS4", c_) for c_ in range(4)] + [
                                                  "POSI", "ANG", "TMPA", "TMPB"])
        MIX_ALL = [("MIXT", c) for c in range(4)] + [("MIXS", g_) for g_ in range(8)]
        for blk in range(8):
            sl = w_take(("out", blk))
            wv = WS[sl][:, 0:4096].rearrange("p (a b) -> p a b", a=16)
            for tt in range(4):
                b = ps_next()
                for fc in range(NKC):
                    A("pe", lambda e, b=b, wv=wv, fc=fc, tt=tt: e.matmul(PS[b][:, 0:256], lhsT=MIXT[:, fc, tt * 128:(tt + 1) * 128],
                                                                      rhs=wv[:, fc, :], start=(fc == 0), stop=(fc == NKC - 1)),
                      reads=[("ws", sl)] + MIX_ALL, writes=[("ps", b)])
                A("dve", lambda e, b=b, tt=tt, blk=blk: e.tensor_tensor(out=H[:, tt, blk * 256:(blk + 1) * 256], in0=PS[b][:, 0:256],
                                                                        in1=GS[:, blk * 256:(blk + 1) * 256], op=ALU.mult),
                  reads=[("ps", b), "GS"], writes=[("H", tt)])
                A("act", lambda e, b=b, tt=tt, blk=blk: e.activation(out=PT[:].rearrange("p a b -> p (a b)")[:, 0:256], in_=PS[b][:, 0:256],
                                                                     func=AF.Square, accum_out=ST[:, 16 + tt * 8 + blk:17 + tt * 8 + blk]),
                  reads=[("ps", b)], writes=[("PT", 0), ("ssp", tt)])
        boundary_all(8, 1.0, None if DBG == "mix" else "f2pre")
        if DBG == "mix":
            return store_y(g)

        def after(tt):
            dma("pool", Y[g * T + tt * 128: g * T + (tt + 1) * 128, :], H[:, tt, :], "st%d" % tt, reads=[("H", tt)])
            if nxt is not None:
                load_x_tt(nxt[0], nxt[1], tt)
        ffn(1, "f2post", "f1pre" if nxt is not None else None, after_s1=after, to_h=True)

    seq = [("p", g) for g in range(NP)] + [("m", g) for g in range(NM)]
    first = seq[0]
    start_group(XP if first[0] == "p" else XM, first[1], eng="sp")
    for i, (kind, g) in enumerate(seq):
        nx = seq[i + 1] if i + 1 < len(seq) else None
        nxt = None if nx is None else ((XP if nx[0] == "p" else XM), nx[1])
        if kind == "p":
            prefix_group(g, nxt)
            if nx is not None and nx[0] == "m":
                A("dve", lambda e: e.tensor_scalar(out=S[:].rearrange("p a b -> p (a b)"), in0=S[:].rearrange("p a b -> p (a b)"),
                                                   scalar1=SFL[:, 0:1], scalar2=None, op0=ALU.mult), reads=["S", "SFL"], writes=["S"])
                A("act", lambda e: e.activation(out=SBF[:].rearrange("p a b -> p (a b)"), in_=S[:].rearrange("p a b -> p (a b)"),
                                                func=AF.Copy), reads=["S"], writes=["SBF"])
        else:
            main_group(g, nxt)
    assert DBG or wstate["taken"] == len(wq)

    print('sbuf bytes remaining', nc.sbuf_bytes_remaining)
    sc.finalize()
    sem_names = sorted({s for op in sc.ops if op.sig for s in [op.sig[0]]} | {"e_" + e for e in ENGS})
    sems = {n: ES.enter_context(nc.semaphore(n)) for n in sem_names}

    def emit(engname, e):
        for op in sc.ops:
            if op.eng != engname:
                continue
            for s, v in op.waits:
                e.wait_ge(sems[s], v)
            ins = op.fn(e)
            if op.sig is not None:
                ins.then_inc(sems[op.sig[0]], 16 if op.dma_sem is not None else 1)
        if engname == "pool":
            for s, v in sc.dma_totals.items():
                if s.startswith("st"):
                    e.wait_ge(sems[s], v)

    with nc.Block() as block:
        @block.tensor
        def _(e):
            emit("pe", e)

        @block.scalar
        def _(e):
            emit("act", e)

        @block.vector
        def _(e):
            emit("dve", e)

        @block.gpsimd
        def _(e):
            emit("pool", e)

        @block.sync
        def _(e):
            emit("sp", e)
    ES.close()
    return nc, CST


def make_in_maps(inputs, NP, NM, B, CST):
    x = np.asarray(inputs["x"], dtype=np.float32)
    pos = np.asarray(inputs["positions"], dtype=np.int32)
    half = NM * T
    f32 = lambda a: np.ascontiguousarray(np.asarray(a, dtype=np.float32))
    shared = {
        "wg1": f32(inputs["ffn1_w_gate"][0]), "wu1": f32(inputs["ffn1_w_up"][0]), "wd1": f32(inputs["ffn1_w_down"][0]),
        "wg2": f32(inputs["ffn2_w_gate"][0]), "wu2": f32(inputs["ffn2_w_up"][0]), "wd2": f32(inputs["ffn2_w_down"][0]),
        "win": f32(inputs["w_in"][0]), "wout": f32(inputs["w_out"][0]),
        "g_f1pre": f32(inputs["ffn1_pre_g"]).reshape(1, D), "g_f1post": f32(inputs["ffn1_post_g"]).reshape(1, D),
        "g_mpre": f32(inputs["mix_pre_g"]).reshape(1, D), "g_mpost": f32(inputs["mix_post_g"]).reshape(1, D),
        "g_f2pre": f32(inputs["ffn2_pre_g"]).reshape(1, D), "g_f2post": f32(inputs["ffn2_post_g"]).reshape(1, D),
        "gt_f1pre": f32(np.asarray(inputs["ffn1_pre_g"], dtype=np.float32).reshape(NKC, 128).T),
        "gt_mpre": f32(np.asarray(inputs["mix_pre_g"], dtype=np.float32).reshape(NKC, 128).T),
        "gt_f2pre": f32(np.asarray(inputs["ffn2_pre_g"], dtype=np.float32).reshape(NKC, 128).T),
        "g_ret": f32(np.asarray(inputs["ret_norm_g"], dtype=np.float32).reshape(NH, 128).T),
        "g_sgu": f32(inputs["sgu_norm_g"]).reshape(1, 1024),
        "sgu_w": f32(inputs["sgu_w_s"][0]), "sgu_b": f32(inputs["sgu_b_s"][0]).reshape(1, 1024),
        "c_idb": CST["c_idb"], "c_idf": CST["c_idf"], "c_mask": CST["c_mask"], "c_xi": CST["c_xi"],
        "c_ginv": CST["c_ginv"], "c_invf": CST["c_invf"],
    }
    maps = []
    zeros_half = np.zeros((half, D), np.float32)
    for c in range(2 * B):
        b, hf = c // 2, c % 2
        m = dict(shared)
        m["xm"] = np.ascontiguousarray(x[b, hf * half:(hf + 1) * half])
        m["xp"] = np.ascontiguousarray(x[b, 0:half]) if hf == 1 else zeros_half
        m["posm"] = np.ascontiguousarray(pos[b, hf * half:(hf + 1) * half]).reshape(1, half)
        m["posp"] = np.ascontiguousarray(pos[b, 0:half]).reshape(1, half)
        m["sflag"] = np.full((128, 1), float(hf), np.float32)
        maps.append(m)
    return maps


def run(inputs, NG):
    B = np.asarray(inputs["x"]).shape[0]
    assert 2 * B == 8
    nc, CST = build(NG, NG)
    maps = make_in_maps(inputs, NG, NG, B, CST)
    res = run_bass_kernel_spmd(nc, maps, core_ids=list(range(8)))
    half = NG * T
    out = np.empty((B, 2 * half, D), np.float32)
    for c in range(8):
        out[c // 2, (c % 2) * half:(c % 2 + 1) * half] = res.results[c]["y"]
    return out


def kernel(**inputs):
    return run(inputs, 8)
```

```python
import numpy as np
import ml_dtypes
from contextlib import ExitStack
import concourse.bass as bass
import concourse.mybir as mybir
from concourse.bass_utils import run_bass_kernel_spmd

F32 = mybir.dt.float32
BF16 = mybir.dt.bfloat16
I32 = mybir.dt.int32
AF = mybir.ActivationFunctionType
ALU = mybir.AluOpType
AX = mybir.AxisListType

D = 2048
DFF = 5632
T = 512
NH = 8
NKC = 16
NFC = 44
EPS = 1e-6
NSLOT = 4
TWO_PI = float(2 * np.pi)
SAME_ENGINE_SYNC = True
DBG = None
PIPE_NORMS = True
SCRATCH_KIND = "ExternalOutput"

ENGS = ("pe", "act", "dve", "pool", "sp")


class Op:
    __slots__ = ("id", "eng", "fn", "deps", "dma_sem", "signal", "sig", "waits", "real")


class Sched:
    def __init__(self):
        self.ops = []
        self.lw = {}
        self.rd = {}

    def add(self, eng, fn, reads=(), writes=(), dma_sem=None):
        op = Op()
        op.id = len(self.ops)
        op.eng = eng
        op.fn = fn
        op.dma_sem = dma_sem
        op.signal = False
        op.sig = None
        deps = set()
        for r in reads:
            w = self.lw.get(r)
            if w is not None:
                deps.add(w)
        if eng != "pe":
            for r in reads:
                if isinstance(r, tuple) and r[0] == "ps":
                    key = ("psr", r[1])
                    l = self.lw.get(key)
                    if l is not None and self.ops[l].eng != eng:
                        deps.add(l)
                    self.lw[key] = op.id
        for w in writes:
            l = self.lw.get(w)
            if l is not None:
                deps.add(l)
            for x in self.rd.get(w, ()):
                deps.add(x)
        for w in writes:
            self.lw[w] = op.id
            self.rd[w] = []
        for r in reads:
            self.rd.setdefault(r, []).append(op.id)
        deps.discard(op.id)
        op.deps = deps
        self.ops.append(op)
        return op

    def alias(self, new_names, old_names):
        acc = set()
        for o in old_names:
            l = self.lw.get(o)
            if l is not None:
                acc.add(l)
            acc.update(self.rd.get(o, ()))
        for n in new_names:
            l = self.lw.get(n)
            if l is not None:
                acc.add(l)
            acc.update(self.rd.get(n, ()))
        acc = sorted(acc)
        for n in new_names:
            self.lw[n] = None
            self.rd[n] = list(acc)

    def finalize(self):
        ops = self.ops
        for op in ops:
            real = []
            best = {}
            for d in op.deps:
                p = ops[d]
                if p.dma_sem is None and p.eng == op.eng:
                    if op.eng in ("pe", "sp") or not SAME_ENGINE_SYNC:
                        continue
                if p.dma_sem is None:
                    if p.eng not in best or best[p.eng].id < p.id:
                        best[p.eng] = p
                else:
                    real.append(p)
            for p in best.values():
                real.append(p)
            for p in real:
                p.signal = True
            op.real = real
        cnt = {e: 0 for e in ENGS}
        dcnt = {}
        for op in ops:
            if op.dma_sem is not None:
                dcnt[op.dma_sem] = dcnt.get(op.dma_sem, 0) + 16
                op.sig = (op.dma_sem, dcnt[op.dma_sem])
            elif op.signal:
                cnt[op.eng] += 1
                op.sig = ("e_" + op.eng, cnt[op.eng])
        seen = {e: {} for e in ENGS}
        for op in ops:
            need = {}
            for p in op.real:
                s, v = p.sig
                if v > need.get(s, 0):
                    need[s] = v
            op.waits = []
            for s, v in need.items():
                if seen[op.eng].get(s, 0) >= v:
                    continue
                seen[op.eng][s] = v
                op.waits.append((s, v))
        self.dma_totals = dcnt


def _consts():
    f = np.float32
    h = np.arange(NH, dtype=f)
    log_gamma = np.log1p(-np.exp2(f(-5.0) - h)).astype(f)
    idx = np.arange(128, dtype=f)
    xi = np.exp(log_gamma[None, :] * (idx + f(1.0))[:, None]).astype(f)
    ginv = np.exp(-log_gamma[None, :] * (idx + f(1.0))[:, None]).astype(f)
    cd = np.exp(log_gamma * f(128.0)).astype(f)
    scale = f(128.0 ** -0.5)
    c_xi = np.ascontiguousarray((xi * scale).T).reshape(1, NH * 128).astype(f)
    c_ginv = np.ascontiguousarray(ginv.T).reshape(1, NH * 128).astype(f)
    half = 64
    inv_freq = (f(10000.0) ** (-np.arange(half, dtype=f) / f(half))).astype(f)
    c_invf = np.concatenate([inv_freq, inv_freq]).reshape(128, 1).astype(f)
    m = np.arange(128)
    c_mask = (m[None, :] >= m[:, None]).astype(f)
    return dict(c_xi=c_xi, c_ginv=c_ginv, cd=[float(x) for x in cd], c_invf=c_invf, c_mask=c_mask,
                c_idb=np.eye(128).astype(ml_dtypes.bfloat16), c_idf=np.eye(128, dtype=f))


def build(NP, NM):
    nc = bass.Bass("TRN2", target_bir_lowering=False)
    CST = _consts()
    cdv = CST["cd"]
    NTP = max(NP, 1) * T
    NTM = NM * T

    def din(name, shape, dt=F32):
        return nc.dram_tensor(name, list(shape), dt, kind="ExternalInput").ap()

    XP = din("xp", [NTP, D])
    XM = din("xm", [NTM, D])
    POSP = din("posp", [1, NTP], I32)
    POSM = din("posm", [1, NTM], I32)
    WG = [din("wg1", [D, DFF]), din("wg2", [D, DFF])]
    WU = [din("wu1", [D, DFF]), din("wu2", [D, DFF])]
    WD = [din("wd1", [DFF, D]), din("wd2", [DFF, D])]
    WIN = din("win", [D, 6144])
    WOUT = din("wout", [D, D])
    GV = {k: din("g_" + k, [1, D]) for k in ("f1pre", "f1post", "mpre", "mpost", "f2pre", "f2post")}
    GTP = {k: din("gt_" + k, [128, NKC]) for k in ("f1pre", "mpre", "f2pre")}
    G_RET = din("g_ret", [128, NH])
    G_SGU = din("g_sgu", [1, 1024])
    SGU_W = din("sgu_w", [NH, 128, 128])
    SGU_B = din("sgu_b", [1, 1024])
    C_IDB = din("c_idb", [128, 128], BF16)
    C_IDF = din("c_idf", [128, 128])
    C_MASK = din("c_mask", [128, 128])
    C_XI = din("c_xi", [1, 1024])
    C_GINV = din("c_ginv", [1, 1024])
    C_INVF = din("c_invf", [128, 1])
    SFLAG = din("sflag", [128, 1])
    Y = nc.dram_tensor("y", [NTM, D], F32, kind="ExternalOutput").ap()

    def dint(name, shape):
        return nc.dram_tensor(name, list(shape), BF16, kind=SCRATCH_KIND).ap()

    SG = [dint("sg1", [22, 128, 16, 256]), dint("sg2", [22, 128, 16, 256])]
    SU = [dint("su1", [22, 128, 16, 256]), dint("su2", [22, 128, 16, 256])]
    SD = [dint("sd1", [4, 128, NFC, 512]), dint("sd2", [4, 128, NFC, 512])]
    SIN = dint("sin_", [24, 128, 16, 256])
    SOUT = dint("sout", [8, 128, 16, 256])

    ES = ExitStack()

    def sb(name, shape, dt=F32):
        return ES.enter_context(nc.sbuf_tensor(name, list(shape), dt))

    Xs = sb("Xs", [128, 4, D])
    XNT = sb("XNT", [128, NKC, T], BF16)
    WS = [sb("WS%d" % i, [128, 4096], BF16) for i in range(NSLOT)]
    GS = sb("GS", [128, D])
    XNTOK = [sb("XNTOK0", [128, D], BF16), sb("XNTOK1", [128, D], BF16)]
    EPS4 = sb("EPS4", [128, 1])
    GPRE = {k: sb("GP_" + k, [128, NKC]) for k in ("f1pre", "mpre", "f2pre")}
    IDB = sb("IDB", [128, 128], BF16)
    IDF = sb("IDF", [128, 128])
    MASK = sb("MASK", [128, 128])
    XIT = sb("XIT", [128, NH, 128])
    GINVT = sb("GINVT", [128, NH, 128])
    S = sb("S", [128, NH, 128])
    SBF = sb("SBF", [128, NH, 128], BF16)
    WTS = sb("WTS", [128, NH, 128], BF16)
    BST = sb("BST", [128, NH, 128])
    SGAIN = sb("SGAIN", [128, 1024])
    COS = sb("COS", [128, T])
    SSG = sb("SSG", [128, T])
    INVF = sb("INVF", [128, 1])
    SFL = sb("SFL", [128, 1])
    EPST = sb("EPST", [128, 1])
    RETG = sb("RETG", [128, NH])
    ST = sb("ST", [128, 160])
    REG = sb("REG", [128, 81920 // 4])

    def carve(off_bytes, shape, dt):
        n = int(np.prod(shape[1:]))
        if dt == BF16:
            v = REG[:, off_bytes // 4: off_bytes // 4 + n // 2].bitcast(BF16)
        elif dt == I32:
            v = REG[:, off_bytes // 4: off_bytes // 4 + n].bitcast(I32)
        else:
            v = REG[:, off_bytes // 4: off_bytes // 4 + n]
        if len(shape) == 3:
            v = v.rearrange("p (a b) -> p a b", a=shape[1])
        return v

    ACTT = carve(0, [128, NFC, T], BF16)
    H = carve(45056, [128, 4, D], F32)
    SILU = [carve(77824, [128, T], F32), carve(79872, [128, T], F32)]
    MIXT = carve(0, [128, NKC, T], BF16)
    UT = carve(16384, [128, NH, T], BF16)
    VSG = carve(24576, [128, 4, 1024], F32)
    VLN = carve(40960, [128, 4, 1024], BF16)
    SGT = [carve(49152, [128, T], F32), carve(51200, [128, T], F32)]
    GT = carve(16384, [128, NH, T], BF16)
    QT = carve(24576, [128, NH, T], BF16)
    KT = carve(32768, [128, NH, T], BF16)
    KZ = carve(40960, [128, 4, 1024], BF16)
    V = carve(49152, [128, 4, 1024], BF16)
    RA = [carve(57344, [128, T], F32), carve(61440, [128, T], F32)]
    RB = [carve(59392, [128, T], F32), carve(63488, [128, T], F32)]
    RETSB = carve(65536, [128, NH, 128], F32)
    RETSQ = carve(73728, [128, NH, 128], F32)
    RS4 = carve(57344, [128, 4 * NH, 128], F32)
    SBF2 = carve(79872, [128, NH, 128], BF16)
    SBFS = [SBF, SBF2]
    RETN = carve(73728, [128, NH, 128], F32)
    PT = carve(77824, [128, 2, T], BF16)
    TMPG = RETSQ
    POSI = carve(57344, [128, T], I32)
    ANG = carve(59392, [128, T], F32)
    TMPA = carve(61440, [128, T], F32)
    TMPB = carve(63488, [128, T], F32)

    PS = [ES.enter_context(nc.psum_tensor("PS%d" % i, [128, 512], F32)) for i in range(8)]

    sc = Sched()
    A = sc.add

    psrr = [0]

    def ps_next():
        b = psrr[0] % 8
        psrr[0] += 1
        return b

    def dma(eng, out, in_, sem, reads=(), writes=()):
        return A(eng, lambda e, o=out, i=in_: e.dma_start(out=o, in_=i), reads=reads, writes=writes, dma_sem=sem)

    def cast(stage, dst, src, last):
        A("pool", lambda e, o=dst, i=src: e.dma_start(out=o, in_=i), reads=(),
          writes=([("wstage", stage)] if last else ()), dma_sem="cast_" + stage)

    def st_gu(f, b):
        return "g%d_%d" % (f + 1, b // 4) if f == 0 else "g2"

    def st_d(f, cb):
        return "d1_%d" % cb if f == 0 else "d2"

    WIN_CAST_ORDER = [4, 5, 6, 7, 8, 9, 10, 11] + [b for b in range(24) if not (4 <= b <= 11)]

    def st_in(b):
        return "win_0" if 4 <= b <= 11 else "win_1"

    def cast_gu(f):
        for b in range(22):
            for W, Sx, lastw in ((WG[f], SG[f], False), (WU[f], SU[f], True)):
                src = W[:, b * 256:(b + 1) * 256].rearrange("(kc p) c -> p kc c", p=128)
                last = lastw and (b == 21 or (f == 0 and b % 4 == 3))
                cast(st_gu(f, b), Sx[b], src, last)

    def cast_down(f):
        for cb in range(4):
            for fb in range(6):
                f0 = fb * 8
                n = min(8, NFC - f0)
                src = WD[f][f0 * 128:(f0 + n) * 128, cb * 512:(cb + 1) * 512].rearrange("(fc p) c -> p fc c", p=128)
                cast(st_d(f, cb), SD[f][cb][:, f0:f0 + n, :], src, fb == 5 and (f == 0 or cb == 3))

    cast_gu(0)
    cast_down(0)
    for i_, b in enumerate(WIN_CAST_ORDER):
        src = WIN[:, b * 256:(b + 1) * 256].rearrange("(kc p) c -> p kc c", p=128)
        cast(st_in(b), SIN[b], src, i_ == 7 or i_ == 23)
    for b in range(8):
        src = WOUT[:, b * 256:(b + 1) * 256].rearrange("(kc p) c -> p kc c", p=128)
        cast("wout", SOUT[b], src, b == 7)
    cast_gu(1)
    cast_down(1)

    cl = []
    def cload(out, in_, name):
        dma("sp", out, in_, "const", writes=[name])
        cl.append(name)
    cload(IDB[:], C_IDB[:], "IDB")
    cload(IDF[:], C_IDF[:], "IDF")
    cload(MASK[:], C_MASK[:], "MASK")
    cload(XIT[:].rearrange("p a b -> p (a b)"), C_XI.partition_broadcast(128), "XIT")
    cload(GINVT[:].rearrange("p a b -> p (a b)"), C_GINV.partition_broadcast(128), "GINVT")
    cload(BST[:].rearrange("p a b -> p (a b)"), SGU_B.partition_broadcast(128), "BST")
    cload(SGAIN[:], G_SGU.partition_broadcast(128), "SGAIN")
    cload(INVF[:], C_INVF[:], "INVF")
    cload(SFL[:], SFLAG[:], "SFL")
    cload(RETG[:], G_RET[:], "RETG")
    for k_ in ("f1pre", "mpre", "f2pre"):
        cload(GPRE[k_][:], GTP[k_][:], "GPRE")
    last_const = sc.ops[-1].id
    for n in cl:
        sc.lw[n] = last_const

    A("dve", lambda e: e.memset(EPST[:], EPS), writes=["EPST"])
    A("dve", lambda e: e.memset(EPS4[:], 4.0 * EPS), writes=["EPS4"])
    A("dve", lambda e: e.memset(S[:].rearrange("p a b -> p (a b)"), 0.0), writes=["S"])
    A("dve", lambda e: e.memset(SBF[:].rearrange("p a b -> p (a b)"), 0.0), writes=[("SBF", 0)])

    for g in range(NH):
        wtmp = TMPA if g % 2 == 0 else TMPB
        nm = ("TMPW", g % 2)
        dma("sp", wtmp[:, 0:128], SGU_W[g], "misc%d" % (g % 2), writes=[nm])
        b = ps_next()
        A("pe", lambda e, b=b, w=wtmp: e.transpose(out=PS[b][:, 0:128], in_=w[:, 0:128], identity=IDF[:]),
          reads=[nm, "IDF"], writes=[("ps", b)])
        A("dve", lambda e, b=b, g=g: e.tensor_tensor(out=WTS[:, g, :], in0=PS[b][:, 0:128], in1=MASK[:], op=ALU.mult),
          reads=[("ps", b), "MASK"], writes=[("WTS", g)])

    wq = []

    def seq_ffn(f):
        for b in range(22):
            wq.append(("g", f, b))
            wq.append(("u", f, b))
        for cb in range(4):
            for fb in range(6):
                wq.append(("d", f, cb, fb))

    for g in range(NP):
        seq_ffn(0)
        for b in (4, 5, 6, 7, 8, 9, 10, 11):
            wq.append(("in", b))
    MIX_ORDER = [20, 21, 22, 23, 16, 17, 18, 19, 12, 13, 14, 15, 8, 9, 10, 11, 4, 5, 6, 7, 0, 1, 2, 3]
    for g in range(NM):
        seq_ffn(0)
        for b in MIX_ORDER:
            wq.append(("in", b))
        for b in range(8):
            wq.append(("out", b))
        seq_ffn(1)
    wstate = {"issued": 0, "taken": 0}

    def w_issue(i):
        key = wq[i]
        slot = i % NSLOT
        if key[0] in ("g", "u"):
            f, b = key[1], key[2]
            src = (SG if key[0] == "g" else SU)[f][b].rearrange("p a b -> p (a b)")
            dst = WS[slot][:, 0:4096]
            stage = st_gu(f, b)
        elif key[0] == "d":
            f, cb, fb = key[1], key[2], key[3]
            f0 = fb * 8
            n = min(8, NFC - f0)
            src = SD[f][cb][:, f0:f0 + n, :].rearrange("p a b -> p (a b)")
            dst = WS[slot][:, 0:n * 512]
            stage = st_d(f, cb)
        elif key[0] == "in":
            src = SIN[key[1]].rearrange("p a b -> p (a b)")
            dst = WS[slot][:, 0:4096]
            stage = st_in(key[1])
        else:
            src = SOUT[key[1]].rearrange("p a b -> p (a b)")
            dst = WS[slot][:, 0:4096]
            stage = "wout"
        dma("sp", dst, src, "ws%d" % slot, reads=[("wstage", stage)], writes=[("ws", slot)])

    def w_take(key, hold=0):
        while wstate["issued"] < min(len(wq), wstate["taken"] - hold + NSLOT):
            w_issue(wstate["issued"])
            wstate["issued"] += 1
        i = wstate["taken"]
        assert wq[i] == key, (wq[i], key)
        wstate["taken"] += 1
        return i % NSLOT

    def rstd_small(dst, src, n, scale, nm):
        A("act", lambda e: e.activation(out=dst, in_=src, func=AF.Sqrt, bias=EPST[:, 0:1], scale=scale),
          reads=[nm, "EPST"], writes=[nm])
        A("dve", lambda e: e.reciprocal(out=dst, in_=dst), reads=[nm], writes=[nm])

    def load_gain(key):
        dma("act", GS[:], GV[key].partition_broadcast(128), "gs", writes=["GS"])

    def S2(tt):
        xb = XNTOK[tt % 2]
        xn = ("XNTOK", tt % 2)
        A("act", lambda e: e.activation(out=xb[:], in_=Xs[:, tt, :], func=AF.Square, accum_out=ST[:, tt:tt + 1]),
          reads=[("X", tt)], writes=[xn, ("s2", tt)])
        A("act", lambda e: e.activation(out=ST[:, 8 + tt:9 + tt], in_=ST[:, tt:tt + 1], func=AF.Sqrt, bias=EPST[:, 0:1], scale=1.0 / D),
          reads=[("s2", tt), "EPST"], writes=[("r2", tt)])
        A("dve", lambda e: e.reciprocal(out=ST[:, 8 + tt:9 + tt], in_=ST[:, 8 + tt:9 + tt]), reads=[("r2", tt)], writes=[("r2", tt)])
        A("act", lambda e: e.activation(out=xb[:], in_=Xs[:, tt, :], func=AF.Copy, scale=ST[:, 8 + tt:9 + tt]),
          reads=[("X", tt), ("r2", tt)], writes=[xn])

    def S3(tt, gkey):
        xb = XNTOK[tt % 2]
        xn = ("XNTOK", tt % 2)
        gp = GPRE[gkey]
        for half in range(2):
            b = ps_next()
            pb = PS[b][:].bitcast(BF16)
            for k8 in range(8):
                kc = half * 8 + k8
                A("pe", lambda e, pb=pb, k8=k8, kc=kc: e.transpose(out=pb[:, k8 * 128:(k8 + 1) * 128],
                                                                  in_=xb[:, kc * 128:(kc + 1) * 128], identity=IDB[:]),
                  reads=[xn, "IDB"], writes=[("ps", b)])
            A("dve", lambda e, pb=pb, half=half: e.tensor_tensor(
                out=XNT[:, half * 8:(half + 1) * 8, tt * 128:(tt + 1) * 128],
                in0=pb.rearrange("p (a b) -> p a b", a=8),
                in1=gp[:, half * 8:(half + 1) * 8].unsqueeze(2).to_broadcast([128, 8, 128]), op=ALU.mult),
              reads=[("ps", b), "GPRE"], writes=[("XNT", tt)])

    XNT_ALL = [("XNT", tt) for tt in range(4)]

    def S1(tt, nparts, mul, to_h):
        A("dve", lambda e: e.reduce_sum(out=ST[:, 48 + tt:49 + tt], in_=ST[:, 16 + tt * 8:16 + tt * 8 + nparts], axis=AX.X),
          reads=[("ssp", tt)], writes=[("r1", tt)])
        bias = EPST if mul == 1.0 else EPS4
        assert mul in (1.0, 0.5)
        A("act", lambda e: e.activation(out=ST[:, 52 + tt:53 + tt], in_=ST[:, 48 + tt:49 + tt], func=AF.Sqrt, bias=bias[:, 0:1],
                                        scale=1.0 / (D * mul * mul)), reads=[("r1", tt), "EPST", "EPS4"], writes=[("r1b", tt)])
        A("dve", lambda e: e.reciprocal(out=ST[:, 52 + tt:53 + tt], in_=ST[:, 52 + tt:53 + tt]), reads=[("r1b", tt)], writes=[("r1b", tt)])
        if to_h:
            A("dve", lambda e: e.scalar_tensor_tensor(out=H[:, tt, :], in0=H[:, tt, :], scalar=ST[:, 52 + tt:53 + tt],
                                                      in1=Xs[:, tt, :], op0=ALU.mult, op1=ALU.add),
              reads=[("H", tt), ("r1b", tt), ("X", tt)], writes=[("H", tt)])
        else:
            A("dve", lambda e: e.scalar_tensor_tensor(out=Xs[:, tt, :], in0=H[:, tt, :], scalar=ST[:, 52 + tt:53 + tt],
                                                      in1=Xs[:, tt, :], op0=ALU.mult, op1=ALU.add),
              reads=[("H", tt), ("r1b", tt), ("X", tt)], writes=[("X", tt)])

    def pipelined(stages):
        n = len(stages)
        if not PIPE_NORMS:
            for tt in range(4):
                for st_ in stages:
                    st_(tt)
            return
        for step in range(4 + n - 1):
            for si in range(n):
                tt = step - si
                if 0 <= tt < 4:
                    stages[si](tt)

    def prenorm_all(gkey):
        pipelined([S2, lambda tt: S3(tt, gkey)])

    def boundary_all(nparts, mul, next_gkey, after_s1=None, to_h=False):
        def s1(tt):
            S1(tt, nparts, mul, to_h)
            if after_s1 is not None:
                after_s1(tt)
        st = [s1]
        if next_gkey is not None:
            st += [S2, lambda tt: S3(tt, next_gkey)]
        pipelined(st)

    FFN_NAMES = [("ACTT", j) for j in range(NFC)] + [("H", tt) for tt in range(4)] + [("SILU", i) for i in range(2)]
    MIX_NAMES = ([("MIXT", c) for c in range(4)] + ["UT", "VSG", "VLN", "SGT0", "SGT1", "GT", "QT", "KZ", "V",
                 "RA0", "RA1", "RB0", "RB1", "RETSB", "RETSQ", "RETN", "PT", "TMPG", "POSI", "ANG", "TMPA", "TMPB",
                 ("TMPW", 0), ("TMPW", 1)] + [("MIXS", g) for g in range(8)] + [("H", tt) for tt in range(4)] + [("KT", h_) for h_ in range(NH)]
                 + [("PT", 0), ("PT", 1)] + [("VSG", t_) for t_ in range(4)] + [("RS4", c_) for c_ in range(4)] + [("SBF", 1)])

    def ffn(f, post, next_gkey, after_s1=None, to_h=False):
        sc.alias(FFN_NAMES, MIX_NAMES)
        load_gain(post)
        for b in range(22):
            sg = w_take(("g", f, b))
            su = w_take(("u", f, b), hold=1)
            gv = WS[sg][:, 0:4096].rearrange("p (a b) -> p a b", a=16)
            uv = WS[su][:, 0:4096].rearrange("p (a b) -> p a b", a=16)
            for jj in range(2):
                j = 2 * b + jj
                pg = ps_next()
                pu = ps_next()
                for kc in range(NKC):
                    A("pe", lambda e, pg=pg, gv=gv, kc=kc, jj=jj: e.matmul(PS[pg][:], lhsT=gv[:, kc, jj * 128:(jj + 1) * 128],
                                                                         rhs=XNT[:, kc, :], start=(kc == 0), stop=(kc == NKC - 1)),
                      reads=[("ws", sg)] + XNT_ALL, writes=[("ps", pg)])
                for kc in range(NKC):
                    A("pe", lambda e, pu=pu, uv=uv, kc=kc, jj=jj: e.matmul(PS[pu][:], lhsT=uv[:, kc, jj * 128:(jj + 1) * 128],
                                                                         rhs=XNT[:, kc, :], start=(kc == 0), stop=(kc == NKC - 1)),
                      reads=[("ws", su)] + XNT_ALL, writes=[("ps", pu)])
                si = j % 2
                A("act", lambda e, pg=pg, si=si: e.activation(out=SILU[si][:], in_=PS[pg][:], func=AF.Silu),
                  reads=[("ps", pg)], writes=[("SILU", si)])
                A("dve", lambda e, pu=pu, si=si, j=j: e.tensor_tensor(out=ACTT[:, j, :], in0=PS[pu][:], in1=SILU[si][:], op=ALU.mult),
                  reads=[("ps", pu), ("SILU", si)], writes=[("ACTT", j)])
        for cb in range(4):
            banks = [ps_next() for _ in range(4)]
            for fb in range(6):
                sl = w_take(("d", f, cb, fb))
                f0 = fb * 8
                n = min(8, NFC - f0)
                dv = WS[sl][:, 0:n * 512].rearrange("p (a b) -> p a b", a=n)
                for tt in range(4):
                    for fl in range(n):
                        fc = f0 + fl
                        A("pe", lambda e, bk=banks[tt], dv=dv, fl=fl, fc=fc, tt=tt: e.matmul(
                            PS[bk][:], lhsT=ACTT[:, fc, tt * 128:(tt + 1) * 128], rhs=dv[:, fl, :],
                            start=(fc == 0), stop=(fc == NFC - 1)),
                          reads=[("ws", sl), ("ACTT", fc)], writes=[("ps", banks[tt])])
            for tt in range(4):
                bk = banks[tt]
                A("dve", lambda e, bk=bk, tt=tt, cb=cb: e.tensor_tensor(out=H[:, tt, cb * 512:(cb + 1) * 512], in0=PS[bk][:],
                                                                        in1=GS[:, cb * 512:(cb + 1) * 512], op=ALU.mult),
                  reads=[("ps", bk), "GS"], writes=[("H", tt)])
                A("act", lambda e, bk=bk, tt=tt, cb=cb: e.activation(out=SILU[tt % 2][:], in_=PS[bk][:], func=AF.Square,
                                                                     accum_out=ST[:, 16 + tt * 8 + cb:17 + tt * 8 + cb]),
                  reads=[("ps", bk)], writes=[("SILU", tt % 2), ("ssp", tt)])
        boundary_all(4, 0.5, next_gkey, after_s1, to_h)

    def pos_tables(POS, g):
        dma("act", POSI[:], POS[:, g * T:(g + 1) * T].partition_broadcast(128), "pos", writes=["POSI"])
        A("dve", lambda e: e.tensor_copy(out=ANG[:], in_=POSI[:]), reads=["POSI"], writes=["ANG"])
        A("dve", lambda e: e.tensor_scalar(out=ANG[:], in0=ANG[:], scalar1=INVF[:, 0:1], scalar2=None, op0=ALU.mult),
          reads=["ANG", "INVF"], writes=["ANG"])
        for which in range(2):
            dst = SSG if which == 0 else COS
            shift = 0.0 if which == 0 else float(np.pi / 2)
            A("dve", lambda e, shift=shift: e.tensor_scalar(out=TMPA[:], in0=ANG[:], scalar1=shift, scalar2=None, op0=ALU.add),
              reads=["ANG"], writes=["TMPA"])
            A("dve", lambda e: e.tensor_scalar(out=TMPB[:], in0=TMPA[:], scalar1=float(1.0 / TWO_PI), scalar2=0.5,
                                               op0=ALU.mult, op1=ALU.add), reads=["TMPA"], writes=["TMPB"])
            A("dve", lambda e: e.tensor_copy(out=POSI[:], in_=TMPB[:]), reads=["TMPB"], writes=["POSI"])
            A("dve", lambda e: e.tensor_copy(out=TMPB[:], in_=POSI[:]), reads=["POSI"], writes=["TMPB"])
            A("dve", lambda e: e.scalar_tensor_tensor(out=TMPA[:], in0=TMPB[:], scalar=-TWO_PI, in1=TMPA[:],
                                                      op0=ALU.mult, op1=ALU.add), reads=["TMPB", "TMPA"], writes=["TMPA"])
            A("dve", lambda e: e.tensor_scalar(out=TMPB[:], in0=TMPA[:], scalar1=-float(np.pi), scalar2=TWO_PI,
                                               op0=ALU.is_lt, op1=ALU.mult), reads=["TMPA"], writes=["TMPB"])
            A("dve", lambda e: e.tensor_tensor(out=TMPA[:], in0=TMPA[:], in1=TMPB[:], op=ALU.add),
              reads=["TMPA", "TMPB"], writes=["TMPA"])
            A("act", lambda e, dst=dst: e.activation(out=dst[:], in_=TMPA[:], func=AF.Sin), reads=["TMPA"],
              writes=["SSG" if which == 0 else "COS"])
        A("dve", lambda e: e.tensor_scalar(out=SSG[0:64, :], in0=SSG[0:64, :], scalar1=-1.0, scalar2=None, op0=ALU.mult),
          reads=["SSG"], writes=["SSG"])

    def proj_fm(blk, handler):
        sl = w_take(("in", blk))
        wv = WS[sl][:, 0:4096].rearrange("p (a b) -> p a b", a=16)
        for cc in range(2):
            b = ps_next()
            for kc in range(NKC):
                A("pe", lambda e, b=b, wv=wv, kc=kc, cc=cc: e.matmul(PS[b][:], lhsT=wv[:, kc, cc * 128:(cc + 1) * 128],
                                                                  rhs=XNT[:, kc, :], start=(kc == 0), stop=(kc == NKC - 1)),
                  reads=[("ws", sl)] + XNT_ALL, writes=[("ps", b)])
            handler(b, cc)

    def proj_tm(blk, handler):
        sl = w_take(("in", blk))
        wv = WS[sl][:, 0:4096].rearrange("p (a b) -> p a b", a=16)
        for tt in range(4):
            b = ps_next()
            for kc in range(NKC):
                A("pe", lambda e, b=b, wv=wv, kc=kc, tt=tt: e.matmul(PS[b][:, 0:256], lhsT=XNT[:, kc, tt * 128:(tt + 1) * 128],
                                                                  rhs=wv[:, kc, :], start=(kc == 0), stop=(kc == NKC - 1)),
                  reads=[("ws", sl), ("XNT", tt)], writes=[("ps", b)])
            handler(b, tt)

    def rotary(b, h, dstT, dname, decT):
        i = h % 2
        A("dve", lambda e: e.tensor_tensor(out=RA[i][:], in0=PS[b][:], in1=COS[:], op=ALU.mult),
          reads=[("ps", b), "COS"], writes=["RA%d" % i])
        A("dve", lambda e: e.tensor_tensor(out=RB[i][0:64, :], in0=PS[b][64:128, :], in1=SSG[0:64, :], op=ALU.mult),
          reads=[("ps", b), "SSG"], writes=["RB%d" % i])
        A("dve", lambda e: e.tensor_tensor(out=RB[i][64:128, :], in0=PS[b][0:64, :], in1=SSG[64:128, :], op=ALU.mult),
          reads=[("ps", b), "SSG"], writes=["RB%d" % i])
        A("dve", lambda e: e.tensor_tensor(out=RA[i][:], in0=RA[i][:], in1=RB[i][:], op=ALU.add),
          reads=["RA%d" % i, "RB%d" % i], writes=["RA%d" % i])
        A("dve", lambda e: e.tensor_tensor(out=dstT[:, h, :].rearrange("p (a b) -> p a b", a=4),
                                           in0=RA[i][:].rearrange("p (a b) -> p a b", a=4),
                                           in1=decT[:, h, :].unsqueeze(1).to_broadcast([128, 4, 128]), op=ALU.mult),
          reads=["RA%d" % i, "XIT", "GINVT"], writes=[dname])

    kpend = []

    def k_flush():
        while kpend:
            h = kpend.pop(0)
            b2 = ps_next()
            pb = PS[b2][:].bitcast(BF16)
            for c in range(4):
                A("pe", lambda e, pb=pb, c=c, h=h: e.transpose(out=pb[:, c * 128:(c + 1) * 128], in_=KT[:, h, c * 128:(c + 1) * 128],
                                                            identity=IDB[:]), reads=[("KT", h), "IDB"], writes=[("ps", b2)])
            A("act", lambda e, pb=pb, h=h: e.activation(out=KZ[:, :, h * 128:(h + 1) * 128],
                                                       in_=pb[:, 0:512].rearrange("p (a b) -> p a b", a=4),
                                                       func=AF.Copy, scale=cdv[h]),
              reads=[("ps", b2)], writes=["KZ"])

    def k_block(blk):
        sl = w_take(("in", blk))
        wv = WS[sl][:, 0:4096].rearrange("p (a b) -> p a b", a=16)
        for cc in range(2):
            h = (blk - 4) * 2 + cc
            b = ps_next()
            for kc in range(NKC):
                A("pe", lambda e, b=b, wv=wv, kc=kc, cc=cc: e.matmul(PS[b][:], lhsT=wv[:, kc, cc * 128:(cc + 1) * 128],
                                                                  rhs=XNT[:, kc, :], start=(kc == 0), stop=(kc == NKC - 1)),
                  reads=[("ws", sl)] + XNT_ALL, writes=[("ps", b)])
            k_flush()
            rotary(b, h, KT, ("KT", h), GINVT)
            kpend.append(h)

    def v_block(blk):
        def hv(b, tt):
            c0 = (blk - 8) * 256
            A("act", lambda e, b=b, tt=tt, c0=c0: e.activation(out=V[:, tt, c0:c0 + 256], in_=PS[b][:, 0:256], func=AF.Copy),
              reads=[("ps", b)], writes=["V"])
        proj_tm(blk, hv)

    def kv_update(c, dst=0):
        for hb in range(2):
            b = ps_next()
            for hh in range(4):
                h = hb * 4 + hh
                A("pe", lambda e, b=b, hh=hh, h=h, c=c: e.matmul(PS[b][:, hh * 128:(hh + 1) * 128], lhsT=KZ[:, c, h * 128:(h + 1) * 128],
                                                              rhs=V[:, c, h * 128:(h + 1) * 128], start=True, stop=True),
                  reads=["KZ", "V"], writes=[("ps", b)])
            for hh in range(4):
                h = hb * 4 + hh
                A("dve", lambda e, b=b, hh=hh, h=h: e.scalar_tensor_tensor(out=S[:, h, :], in0=S[:, h, :], scalar=cdv[h],
                                                                        in1=PS[b][:, hh * 128:(hh + 1) * 128], op0=ALU.mult, op1=ALU.add),
                  reads=["S", ("ps", b)], writes=["S"])
        A("act", lambda e: e.activation(out=SBFS[dst][:].rearrange("p a b -> p (a b)"), in_=S[:].rearrange("p a b -> p (a b)"), func=AF.Copy),
          reads=["S"], writes=[("SBF", dst)])

    def load_x_tt(XD, g, tt, eng="sp"):
        dma("sp", Xs[:, tt, :], XD[g * T + tt * 128: g * T + (tt + 1) * 128, :], "xi%d" % tt, writes=[("X", tt)])

    def start_group(XD, g, eng="pool"):
        for tt in range(4):
            load_x_tt(XD, g, tt, eng)
        prenorm_all("f1pre")

    def prefix_group(g, nxt):
        ffn(0, "f1post", "mpre")
        sc.alias(MIX_NAMES, FFN_NAMES)
        for tt in range(4):
            load_x_tt(nxt[0], nxt[1], tt)
        pos_tables(POSP, g)
        for blk in (4, 5, 6, 7):
            k_block(blk)
        k_flush()
        for blk in (8, 9, 10, 11):
            v_block(blk)
        prenorm_all("f1pre")
        for c in range(4):
            kv_update(c)

    def store_y_tt(g, tt):
        dma("pool", Y[g * T + tt * 128: g * T + (tt + 1) * 128, :], Xs[:, tt, :], "st%d" % tt, reads=[("X", tt)])

    def store_y(g):
        for tt in range(4):
            store_y_tt(g, tt)

    def main_group(g, nxt):
        ffn(0, "f1post", None if DBG == "ffn1" else "mpre")
        if DBG == "ffn1":
            return store_y(g)
        sc.alias(MIX_NAMES, FFN_NAMES)
        load_gain("mpost")
        pos_tables(POSM, g)
        for blk in (20, 21, 22, 23):
            def hvs(b, tt, blk=blk):
                c0 = (blk - 20) * 256
                A("act", lambda e, b=b, tt=tt, c0=c0: e.activation(out=VSG[:, tt, c0:c0 + 256], in_=PS[b][:, 0:256],
                                                                   func=AF.Gelu_apprx_tanh), reads=[("ps", b)], writes=[("VSG", tt)])
            proj_tm(blk, hvs)
        for tt in range(4):
            A("dve", lambda e, tt=tt: e.reduce_sum(out=ST[:, 56:57], in_=VSG[:, tt, :], axis=AX.X),
              reads=[("VSG", tt)], writes=["lnv"])
            A("act", lambda e, tt=tt: e.activation(out=VLN[:, tt, :], in_=VSG[:, tt, :], func=AF.Square, accum_out=ST[:, 57:58]),
              reads=[("VSG", tt)], writes=["VLN", "lnv"])
            A("dve", lambda e: e.tensor_scalar(out=ST[:, 58:59], in0=ST[:, 56:57], scalar1=1.0 / 1024, scalar2=None, op0=ALU.mult),
              reads=["lnv"], writes=["lnv"])
            A("dve", lambda e: e.tensor_tensor(out=ST[:, 59:60], in0=ST[:, 58:59], in1=ST[:, 58:59], op=ALU.mult),
              reads=["lnv"], writes=["lnv"])
            A("dve", lambda e: e.scalar_tensor_tensor(out=ST[:, 59:60], in0=ST[:, 57:58], scalar=1.0 / 1024, in1=ST[:, 59:60],
                                                      op0=ALU.mult, op1=ALU.subtract), reads=["lnv"], writes=["lnv"])
            rstd_small(ST[:, 60:61], ST[:, 59:60], 1, 1.0, "lnv")
            A("dve", lambda e: e.scalar_tensor_tensor(out=ST[:, 61:62], in0=ST[:, 58:59], scalar=-1.0, in1=ST[:, 60:61],
                                                      op0=ALU.mult, op1=ALU.mult), reads=["lnv"], writes=["lnv"])
            A("dve", lambda e, tt=tt: e.tensor_scalar(out=VSG[:, tt, :], in0=VSG[:, tt, :], scalar1=ST[:, 60:61], scalar2=ST[:, 61:62],
                                                      op0=ALU.mult, op1=ALU.add), reads=[("VSG", tt), "lnv"], writes=[("VSG", tt)])
            A("dve", lambda e, tt=tt: e.tensor_tensor(out=VLN[:, tt, :], in0=VSG[:, tt, :], in1=SGAIN[:], op=ALU.mult),
              reads=[("VSG", tt), "SGAIN"], writes=["VLN"])
        for blk in (16, 17, 18, 19):
            def hu(b, cc, blk=blk):
                gi = (blk - 16) * 2 + cc
                A("act", lambda e, b=b, gi=gi: e.activation(out=UT[:, gi, :], in_=PS[b][:], func=AF.Gelu_apprx_tanh),
                  reads=[("ps", b)], writes=["UT"])
            proj_fm(blk, hu)
        for gi in range(NH):
            b = ps_next()
            for c in range(4):
                A("pe", lambda e, b=b, c=c, gi=gi: e.matmul(PS[b][:, c * 128:(c + 1) * 128], lhsT=VLN[:, c, gi * 128:(gi + 1) * 128],
                                                          rhs=WTS[:, gi, :], start=True, stop=True),
                  reads=["VLN", ("WTS", gi)], writes=[("ps", b)])
            i = gi % 2
            A("dve", lambda e, b=b, gi=gi, i=i: e.tensor_tensor(out=SGT[i][:].rearrange("p (a b) -> p a b", a=4),
                                                              in0=PS[b][:].rearrange("p (a b) -> p a b", a=4),
                                                              in1=BST[:, gi, :].unsqueeze(1).to_broadcast([128, 4, 128]), op=ALU.add),
              reads=[("ps", b), "BST"], writes=["SGT%d" % i])
            A("dve", lambda e, gi=gi, i=i: e.tensor_tensor(out=MIXT[:, 8 + gi, :], in0=SGT[i][:], in1=UT[:, gi, :], op=ALU.mult),
              reads=["SGT%d" % i, "UT"], writes=[("MIXS", gi)])
        sc.alias(["GT", "QT", "KZ", "V"] + [("KT", h_) for h_ in range(NH)], ["UT", "VSG", "VLN", "SGT0", "SGT1"] + [("VSG", t) for t in range(4)])
        for blk in (12, 13, 14, 15):
            def hg(b, cc, blk=blk):
                h = (blk - 12) * 2 + cc
                A("act", lambda e, b=b, h=h: e.activation(out=GT[:, h, :], in_=PS[b][:], func=AF.Silu),
                  reads=[("ps", b)], writes=["GT"])
            proj_fm(blk, hg)
        for blk in (8, 9, 10, 11):
            v_block(blk)
        for blk in (4, 5, 6, 7):
            k_block(blk)
        k_flush()
        for blk in (0, 1, 2, 3):
            def hq(b, cc, blk=blk):
                h = blk * 2 + cc
                rotary(b, h, QT, "QT", XIT)
            proj_fm(blk, hq)
        sc.alias([("RS4", c_) for c_ in range(4)] + ["RETSQ"], ["RA0", "RA1", "RB0", "RB1", "POSI", "ANG", "TMPA", "TMPB", "RETSB", "RETN"])
        for c in range(4):
            cs = slice(c * 128, (c + 1) * 128)
            rb = []
            for hb in range(2):
                b = ps_next()
                for hh in range(4):
                    h = hb * 4 + hh
                    A("pe", lambda e, b=b, hh=hh, h=h, cs=cs: e.matmul(PS[b][:, hh * 128:(hh + 1) * 128], lhsT=KT[:, h, cs], rhs=QT[:, h, cs],
                                                                    start=True, stop=True),
                      reads=[("KT", h), "QT"], writes=[("ps", b)])
                A("dve", lambda e, b=b, hb=hb: e.tensor_tensor(out=PT[:, hb, :].rearrange("p (a b) -> p a b", a=4),
                                                             in0=PS[b][:].rearrange("p (a b) -> p a b", a=4),
                                                             in1=MASK[:].unsqueeze(1).to_broadcast([128, 4, 128]), op=ALU.mult),
                  reads=[("ps", b), "MASK"], writes=[("PT", hb)])
            kv_update(c, dst=(c + 1) % 2)
            for hb in range(2):
                b = ps_next()
                rb.append(b)
                for hh in range(4):
                    h = hb * 4 + hh
                    A("pe", lambda e, b=b, hh=hh, h=h, hb=hb, c=c: e.matmul(PS[b][:, hh * 128:(hh + 1) * 128], lhsT=PT[:, hb, hh * 128:(hh + 1) * 128],
                                                                         rhs=V[:, c, h * 128:(h + 1) * 128], start=True, stop=False),
                      reads=[("PT", hb), "V"], writes=[("ps", b)])
                    A("pe", lambda e, b=b, hh=hh, h=h, cs=cs, c=c: e.matmul(PS[b][:, hh * 128:(hh + 1) * 128], lhsT=QT[:, h, cs],
                                                                    rhs=SBFS[c % 2][:, h, :], start=False, stop=True),
                      reads=["QT", ("SBF", c % 2)], writes=[("ps", b)])
            for hb in range(2):
                b = rb[hb]
                A("act", lambda e, b=b, hb=hb, c=c: e.activation(out=RS4[:, c * 8 + hb * 4:c * 8 + (hb + 1) * 4, :].rearrange("p a b -> p (a b)"),
                                                               in_=PS[b][:], func=AF.Copy), reads=[("ps", b)], writes=[("RS4", c)])
                A("act", lambda e, b=b, hb=hb: e.activation(out=RETSQ[:, hb * 4:(hb + 1) * 4, :].rearrange("p a b -> p (a b)"),
                                                          in_=PS[b][:], func=AF.Square), reads=[("ps", b)], writes=["RETSQ"])
            A("dve", lambda e, c=c: e.reduce_sum(out=ST[:, 96 + c * 8:104 + c * 8], in_=RETSQ[:], axis=AX.X),
              reads=["RETSQ"], writes=["lnr"])
        RS4_ALL = [("RS4", c_) for c_ in range(4)]
        A("dve", lambda e: e.reduce_sum(out=ST[:, 64:96], in_=RS4[:], axis=AX.X), reads=RS4_ALL, writes=["lnr"])
        A("dve", lambda e: e.tensor_scalar(out=ST[:, 64:96], in0=ST[:, 64:96], scalar1=1.0 / 128, scalar2=None, op0=ALU.mult),
          reads=["lnr"], writes=["lnr"])
        A("dve", lambda e: e.tensor_tensor(out=ST[:, 128:160], in0=ST[:, 64:96], in1=ST[:, 64:96], op=ALU.mult),
          reads=["lnr"], writes=["lnr"])
        A("dve", lambda e: e.scalar_tensor_tensor(out=ST[:, 96:128], in0=ST[:, 96:128], scalar=1.0 / 128, in1=ST[:, 128:160],
                                                  op0=ALU.mult, op1=ALU.subtract), reads=["lnr"], writes=["lnr"])
        rstd_small(ST[:, 96:128], ST[:, 96:128], 32, 1.0, "lnr")
        for c in range(4):
            cs = slice(c * 128, (c + 1) * 128)
            A("dve", lambda e, c=c: e.tensor_tensor(out=RS4[:, c * 8:(c + 1) * 8, :], in0=RS4[:, c * 8:(c + 1) * 8, :],
                                                    in1=ST[:, 64 + c * 8:72 + c * 8].unsqueeze(2).to_broadcast([128, 8, 128]),
                                                    op=ALU.subtract), reads=[("RS4", c), "lnr"], writes=[("RS4", c)])
            A("dve", lambda e, c=c: e.tensor_tensor(out=RS4[:, c * 8:(c + 1) * 8, :], in0=RS4[:, c * 8:(c + 1) * 8, :],
                                                    in1=ST[:, 96 + c * 8:104 + c * 8].unsqueeze(2).to_broadcast([128, 8, 128]),
                                                    op=ALU.mult), reads=[("RS4", c), "lnr"], writes=[("RS4", c)])
            for hb in range(2):
                b = ps_next()
                for hh in range(4):
                    h = hb * 4 + hh
                    A("pe", lambda e, b=b, hh=hh, h=h, c=c: e.transpose(out=PS[b][:, hh * 128:(hh + 1) * 128], in_=RS4[:, c * 8 + h, :],
                                                                     identity=IDF[:]),
                      reads=[("RS4", c), "IDF"], writes=[("ps", b)])
                A("dve", lambda e, b=b, hb=hb: e.tensor_tensor(out=TMPG[:, hb * 4:(hb + 1) * 4, :],
                                                             in0=PS[b][:].rearrange("p (a b) -> p a b", a=4),
                                                             in1=RETG[:, hb * 4:(hb + 1) * 4].unsqueeze(2).to_broadcast([128, 4, 128]),
                                                             op=ALU.mult), reads=[("ps", b), "RETG"], writes=["RETSQ"])
                A("dve", lambda e, hb=hb, cs=cs: e.tensor_tensor(out=MIXT[:, hb * 4:(hb + 1) * 4, cs], in0=TMPG[:, hb * 4:(hb + 1) * 4, :],
                                                               in1=GT[:, hb * 4:(hb + 1) * 4, cs], op=ALU.mult),
                  reads=["RETSQ", "GT"], writes=[("MIXT", c)])
        sc.alias([("H", tt) for tt in range(4)], ["KZ", "V", "RA0", "RA1", "RB0", "RB1", "RETSB", "RETSQ", "RETN", "QT"] + [("KT", h_) for h_ in range(NH)] + [("RS4", c_) for c_ in range(4)] + [
                                                  "POSI", "ANG", "TMPA", "TMPB"])
        MIX_ALL = [("MIXT", c) for c in range(4)] + [("MIXS", g_) for g_ in range(8)]
        for blk in range(8):
            sl = w_take(("out", blk))
            wv = WS[sl][:, 0:4096].rearrange("p (a b) -> p a b", a=16)
            for tt in range(4):
                b = ps_next()
                for fc in range(NKC):
                    A("pe", lambda e, b=b, wv=wv, fc=fc, tt=tt: e.matmul(PS[b][:, 0:256], lhsT=MIXT[:, fc, tt * 128:(tt + 1) * 128],
                                                                      rhs=wv[:, fc, :], start=(fc == 0), stop=(fc == NKC - 1)),
                      reads=[("ws", sl)] + MIX_ALL, writes=[("ps", b)])
                A("dve", lambda e, b=b, tt=tt, blk=blk: e.tensor_tensor(out=H[:, tt, blk * 256:(blk + 1) * 256], in0=PS[b][:, 0:256],
                                                                        in1=GS[:, blk * 256:(blk + 1) * 256], op=ALU.mult),
                  reads=[("ps", b), "GS"], writes=[("H", tt)])
                A("act", lambda e, b=b, tt=tt, blk=blk: e.activation(out=PT[:].rearrange("p a b -> p (a b)")[:, 0:256], in_=PS[b][:, 0:256],
                                                                     func=AF.Square, accum_out=ST[:, 16 + tt * 8 + blk:17 + tt * 8 + blk]),
                  reads=[("ps", b)], writes=[("PT", 0), ("ssp", tt)])
        boundary_all(8, 1.0, None if DBG == "mix" else "f2pre")
        if DBG == "mix":
            return store_y(g)

        def after(tt):
            dma("pool", Y[g * T + tt * 128: g * T + (tt + 1) * 128, :], H[:, tt, :], "st%d" % tt, reads=[("H", tt)])
            if nxt is not None:
                load_x_tt(nxt[0], nxt[1], tt)
        ffn(1, "f2post", "f1pre" if nxt is not None else None, after_s1=after, to_h=True)

    seq = [("p", g) for g in range(NP)] + [("m", g) for g in range(NM)]
    first = seq[0]
    start_group(XP if first[0] == "p" else XM, first[1], eng="sp")
    for i, (kind, g) in enumerate(seq):
        nx = seq[i + 1] if i + 1 < len(seq) else None
        nxt = None if nx is None else ((XP if nx[0] == "p" else XM), nx[1])
        if kind == "p":
            prefix_group(g, nxt)
            if nx is not None and nx[0] == "m":
                A("dve", lambda e: e.tensor_scalar(out=S[:].rearrange("p a b -> p (a b)"), in0=S[:].rearrange("p a b -> p (a b)"),
                                                   scalar1=SFL[:, 0:1], scalar2=None, op0=ALU.mult), reads=["S", "SFL"], writes=["S"])
                A("act", lambda e: e.activation(out=SBF[:].rearrange("p a b -> p (a b)"), in_=S[:].rearrange("p a b -> p (a b)"),
                                                func=AF.Copy), reads=["S"], writes=[("SBF", 0)])
        else:
            main_group(g, nxt)
    assert DBG or wstate["taken"] == len(wq)

    print('sbuf bytes remaining', nc.sbuf_bytes_remaining)
    sc.finalize()
    sem_names = sorted({s for op in sc.ops if op.sig for s in [op.sig[0]]} | {"e_" + e for e in ENGS})
    sems = {n: ES.enter_context(nc.semaphore(n)) for n in sem_names}

    def emit(engname, e):
        for op in sc.ops:
            if op.eng != engname:
                continue
            for s, v in op.waits:
                e.wait_ge(sems[s], v)
            ins = op.fn(e)
            if op.sig is not None:
                ins.then_inc(sems[op.sig[0]], 16 if op.dma_sem is not None else 1)
        if engname == "pool":
            for s, v in sc.dma_totals.items():
                if s.startswith("st"):
                    e.wait_ge(sems[s], v)

    with nc.Block() as block:
        @block.tensor
        def _(e):
            emit("pe", e)

        @block.scalar
        def _(e):
            emit("act", e)

        @block.vector
        def _(e):
            emit("dve", e)

        @block.gpsimd
        def _(e):
            emit("pool", e)

        @block.sync
        def _(e):
            emit("sp", e)
    ES.close()
    return nc, CST


def make_in_maps(inputs, NP, NM, B, CST):
    x = np.asarray(inputs["x"], dtype=np.float32)
    pos = np.asarray(inputs["positions"], dtype=np.int32)
    half = NM * T
    f32 = lambda a: np.ascontiguousarray(np.asarray(a, dtype=np.float32))
    shared = {
        "wg1": f32(inputs["ffn1_w_gate"][0]), "wu1": f32(inputs["ffn1_w_up"][0]), "wd1": f32(inputs["ffn1_w_down"][0]),
        "wg2": f32(inputs["ffn2_w_gate"][0]), "wu2": f32(inputs["ffn2_w_up"][0]), "wd2": f32(inputs["ffn2_w_down"][0]),
        "win": f32(inputs["w_in"][0]), "wout": f32(inputs["w_out"][0]),
        "g_f1pre": f32(inputs["ffn1_pre_g"]).reshape(1, D), "g_f1post": f32(inputs["ffn1_post_g"]).reshape(1, D),
        "g_mpre": f32(inputs["mix_pre_g"]).reshape(1, D), "g_mpost": f32(inputs["mix_post_g"]).reshape(1, D),
        "g_f2pre": f32(inputs["ffn2_pre_g"]).reshape(1, D), "g_f2post": f32(inputs["ffn2_post_g"]).reshape(1, D),
        "gt_f1pre": f32(np.asarray(inputs["ffn1_pre_g"], dtype=np.float32).reshape(NKC, 128).T),
        "gt_mpre": f32(np.asarray(inputs["mix_pre_g"], dtype=np.float32).reshape(NKC, 128).T),
        "gt_f2pre": f32(np.asarray(inputs["ffn2_pre_g"], dtype=np.float32).reshape(NKC, 128).T),
        "g_ret": f32(np.asarray(inputs["ret_norm_g"], dtype=np.float32).reshape(NH, 128).T),
        "g_sgu": f32(inputs["sgu_norm_g"]).reshape(1, 1024),
        "sgu_w": f32(inputs["sgu_w_s"][0]), "sgu_b": f32(inputs["sgu_b_s"][0]).reshape(1, 1024),
        "c_idb": CST["c_idb"], "c_idf": CST["c_idf"], "c_mask": CST["c_mask"], "c_xi": CST["c_xi"],
        "c_ginv": CST["c_ginv"], "c_invf": CST["c_invf"],
    }
    maps = []
    zeros_half = np.zeros((half, D), np.float32)
    for c in range(2 * B):
        b, hf = c // 2, c % 2
        m = dict(shared)
        m["xm"] = np.ascontiguousarray(x[b, hf * half:(hf + 1) * half])
        m["xp"] = np.ascontiguousarray(x[b, 0:half]) if hf == 1 else zeros_half
        m["posm"] = np.ascontiguousarray(pos[b, hf * half:(hf + 1) * half]).reshape(1, half)
        m["posp"] = np.ascontiguousarray(pos[b, 0:half]).reshape(1, half)
        m["sflag"] = np.full((128, 1), float(hf), np.float32)
        maps.append(m)
    return maps


def run(inputs, NG):
    B = np.asarray(inputs["x"]).shape[0]
    assert 2 * B == 8
    nc, CST = build(NG, NG)
    maps = make_in_maps(inputs, NG, NG, B, CST)
    res = run_bass_kernel_spmd(nc, maps, core_ids=list(range(8)))
    half = NG * T
    out = np.empty((B, 2 * half, D), np.float32)
    for c in range(8):
        out[c // 2, (c % 2) * half:(c % 2 + 1) * half] = res.results[c]["y"]
    return out


def kernel(**inputs):
    return run(inputs, 8)
```
